# Optimizing a Trainium2 kernel written in Bass

```python
import jax, jax.numpy as jnp
from jax import lax
import numpy as np

D_MODEL = 2048
BATCH = 2
SEQ = 8192
DEPTH = 2

N_A = DEPTH // 2
N_B = DEPTH - N_A
N_HEADS = 16
HEAD_DIM = D_MODEL // N_HEADS
D_FF = ((8 * D_MODEL // 3 + 255) // 256) * 256
CONV_WIDTH = 3
Q_BLOCK = 128
N_SUB = 3
N_MOD = 3
EPS = 1e-6

kernel_name = "yoco_shortconv_fox_macaron_adaln"


def rmsnorm(x, g):
    x32 = x.astype(jnp.float32)
    y = x32 * lax.rsqrt(jnp.mean(x32 * x32, axis=-1, keepdims=True) + EPS)
    return y.astype(x.dtype) * g


def modulate(h, shift, scale):
    return h * (1.0 + scale[:, None, :]) + shift[:, None, :]


def swiglu(h, w_in, w_out):
    a, b = jnp.split(h @ w_in, 2, axis=-1)
    return (jax.nn.silu(a) * b) @ w_out


def short_gated_conv(h, w_in, conv_w, conv_b, w_out):
    bg, cg, xv = jnp.split(h @ w_in, 3, axis=-1)
    u = cg * xv
    u = lax.conv_general_dilated(
        u, conv_w[:, None, :], window_strides=(1,),
        padding=[(CONV_WIDTH - 1, 0)],
        dimension_numbers=("NWC", "WIO", "NWC"),
        feature_group_count=D_MODEL) + conv_b
    return (bg * u) @ w_out


def forgetting_attention(q, k, v, fcum):
    b, s_len, h, dh = q.shape
    nb = s_len // Q_BLOCK
    scale = 1.0 / float(np.sqrt(dh))
    qb = q.reshape(b, nb, Q_BLOCK, h, dh).transpose(1, 0, 2, 3, 4)
    fb = fcum.reshape(b, h, nb, Q_BLOCK).transpose(2, 0, 1, 3)
    kpos = jnp.arange(s_len)

    def one_block(args):
        qi, fi, i = args
        logits = jnp.einsum("bqhd,bkhd->bhqk", qi, k,
                            preferred_element_type=jnp.float32) * scale
        logits = logits + fi[..., :, None] - fcum[:, :, None, :]
        qpos = i * Q_BLOCK + jnp.arange(Q_BLOCK)
        mask = qpos[:, None] >= kpos[None, :]
        logits = jnp.where(mask[None, None], logits, -jnp.inf)
        p = jax.nn.softmax(logits, axis=-1)
        return jnp.einsum("bhqk,bkhd->bqhd", p.astype(v.dtype), v)

    out = lax.map(one_block, (qb, fb, jnp.arange(nb)))
    return out.transpose(1, 0, 2, 3, 4).reshape(b, s_len, h * dh)


def setup_inputs(seed: int = 0) -> dict:
    key = jax.random.key(seed)
    ks = jax.random.split(key, 24)
    D, F, H = D_MODEL, D_FF, N_HEADS
    nrm = lambda k, shape, fan_in: jax.random.normal(k, shape, jnp.float32) * (fan_in ** -0.5)
    return {
        "x": jax.random.normal(ks[0], (BATCH, SEQ, D), jnp.float32),
        "c": jax.random.normal(ks[1], (BATCH, D), jnp.float32),
        "norm_g": 1.0 + 0.05 * jax.random.normal(ks[2], (DEPTH, N_SUB, D), jnp.float32),
        "w_ada": nrm(ks[3], (DEPTH, D, N_SUB * N_MOD * D), D),
        "b_ada": 0.02 * jax.random.normal(ks[4], (DEPTH, N_SUB * N_MOD * D), jnp.float32),
        "w_ffn_in": nrm(ks[5], (DEPTH, 2, D, 2 * F), D),
        "w_ffn_out": nrm(ks[6], (DEPTH, 2, F, D), F),
        "w_conv_in": nrm(ks[7], (N_A, D, 3 * D), D),
        "conv_w": nrm(ks[8], (N_A, CONV_WIDTH, D), CONV_WIDTH),
        "conv_b": 0.02 * jax.random.normal(ks[9], (N_A, D), jnp.float32),
        "w_conv_out": nrm(ks[10], (N_A, D, D), D),
        "kv_norm_g": 1.0 + 0.05 * jax.random.normal(ks[11], (D,), jnp.float32),
        "w_ada_kv": nrm(ks[12], (D, 2 * D), D),
        "b_ada_kv": 0.02 * jax.random.normal(ks[13], (2 * D,), jnp.float32),
        "w_kvf": jnp.concatenate([nrm(ks[14], (D, 2 * D), D),
                                  0.1 * nrm(ks[15], (D, H), D)], axis=-1),
        "b_fgate": jax.random.uniform(ks[16], (H,), jnp.float32, 1.0, 6.0),
        "w_q": nrm(ks[17], (N_B, D, D), D),
        "w_o": nrm(ks[18], (N_B, D, D), D),
        "final_g": 1.0 + 0.05 * jax.random.normal(ks[19], (D,), jnp.float32),
    }


def reference(x, c, norm_g, w_ada, b_ada, w_ffn_in, w_ffn_out, w_conv_in, conv_w, conv_b,
              w_conv_out, kv_norm_g, w_ada_kv, b_ada_kv, w_kvf, b_fgate, w_q, w_o, final_g):
    b, s_len, d = x.shape
    cond = jax.nn.silu(c)
    k = v = fcum = None
    for l in range(DEPTH):
        ada = (cond @ w_ada[l] + b_ada[l]).reshape(b, N_SUB, N_MOD, d)

        h = modulate(rmsnorm(x, norm_g[l, 0]), ada[:, 0, 0], ada[:, 0, 1])
        x = x + 0.5 * ada[:, 0, 2][:, None, :] * swiglu(h, w_ffn_in[l, 0], w_ffn_out[l, 0])

        h = modulate(rmsnorm(x, norm_g[l, 1]), ada[:, 1, 0], ada[:, 1, 1])
        if l < N_A:
            y = short_gated_conv(h, w_conv_in[l], conv_w[l], conv_b[l], w_conv_out[l])
        else:
            if l == N_A:
                ada_kv = (cond @ w_ada_kv + b_ada_kv).reshape(b, 2, d)
                hkv = modulate(rmsnorm(x, kv_norm_g), ada_kv[:, 0], ada_kv[:, 1])
                kvf = hkv @ w_kvf
                k = kvf[..., :d].reshape(b, s_len, N_HEADS, HEAD_DIM)
                v = kvf[..., d:2 * d].reshape(b, s_len, N_HEADS, HEAD_DIM)
                zf = (kvf[..., 2 * d:] + b_fgate).astype(jnp.float32)
                fcum = jnp.cumsum(jax.nn.log_sigmoid(zf), axis=1).transpose(0, 2, 1)
            j = l - N_A
            q = (h @ w_q[j]).reshape(b, s_len, N_HEADS, HEAD_DIM)
            y = forgetting_attention(q, k, v, fcum) @ w_o[j]
        x = x + ada[:, 1, 2][:, None, :] * y

        h = modulate(rmsnorm(x, norm_g[l, 2]), ada[:, 2, 0], ada[:, 2, 1])
        x = x + 0.5 * ada[:, 2, 2][:, None, :] * swiglu(h, w_ffn_in[l, 1], w_ffn_out[l, 1])
    return rmsnorm(x, final_g)
```

```python
import numpy as np
from contextlib import ExitStack
import ml_dtypes
import concourse.bass as bass
import concourse.mybir as mybir
from concourse.bass_utils import run_bass_kernel_spmd

F32 = mybir.dt.float32
BF16 = mybir.dt.bfloat16
AF = mybir.ActivationFunctionType
ALU = mybir.AluOpType

D = 2048
KD = 16
FF = 5632
KF = 44
NH = 16
SEQ = 8192
TOK = 1024
HALO = 2
TW = TOK + HALO
EPS = 1e-6
GRP = 4
NSLOT = 5
SLOT_ELEMS = 4096

TILES_M = [(2, 514), (514, 1026)]
TILES_H = [(0, 2), (2, 514), (514, 1026)]

V_NORMG = 0
V_KVG = 6
V_FING = 7
V_CONVW = 8
V_CONVB = 11
V_BADA = 12
V_BKV = 30
NVEC = 32
NADA = 20


class Res:
    __slots__ = ("name", "last_w", "readers", "dsem", "dcnt")

    def __init__(self, name):
        self.name = name
        self.last_w = None
        self.readers = []
        self.dsem = None
        self.dcnt = 0


class Sched:
    def __init__(self, nc, stack):
        self.nc = nc
        self.stack = stack
        self.eng = {"pe": nc.tensor, "act": nc.scalar, "dve": nc.vector, "pool": nc.gpsimd, "sp": nc.sync}
        self.esem = {}
        self.ecnt = {}
        for e in self.eng:
            self.esem[e] = stack.enter_context(nc.semaphore("prog_" + e))
            self.ecnt[e] = 0
        self.seen = {}
        self.sem_owner = {id(self.esem[e]): e for e in self.eng}
        self.nsem = 0
        self.out_tokens = []
        self.pending = []
        self.pump_ctr = 0

    def new_sem(self, name):
        self.nsem += 1
        return self.stack.enter_context(self.nc.semaphore(f"d{self.nsem}_{name}"))

    def _wait(self, eng, tok):
        sem, val = tok
        if self.sem_owner.get(id(sem)) == eng:
            return
        key = (eng, id(sem))
        if self.seen.get(key, 0) >= val:
            return
        self.seen[key] = val
        self.eng[eng].wait_ge(sem, val)

    def _deps(self, eng, reads, writes):
        for r in reads:
            if r.last_w is not None:
                self._wait(eng, r.last_w)
        for w in writes:
            if w.last_w is not None:
                self._wait(eng, w.last_w)
            for t in w.readers:
                self._wait(eng, t)

    def _commit(self, tok, reads, writes):
        for r in reads:
            sid = id(tok[0])
            r.readers = [t for t in r.readers if id(t[0]) != sid]
            r.readers.append(tok)
        for w in writes:
            w.last_w = tok
            w.readers = []

    def op(self, eng, fn, reads=(), writes=()):
        self._deps(eng, reads, writes)
        ins = fn()
        self.ecnt[eng] += 1
        ins.then_inc(self.esem[eng], 1)
        tok = (self.esem[eng], self.ecnt[eng])
        self._commit(tok, reads, writes)
        return tok

    def dma(self, eng, out, in_, reads=(), writes=(), sem_res=None, is_output=False):
        self._deps(eng, reads, writes)
        r = sem_res if sem_res is not None else (writes[0] if writes else reads[0])
        if r.dsem is None:
            r.dsem = self.new_sem(r.name)
        ins = self.eng[eng].dma_start(out=out, in_=in_)
        r.dcnt += 16
        ins.then_inc(r.dsem, 16)
        tok = (r.dsem, r.dcnt)
        self._commit(tok, reads, writes)
        if is_output:
            self.out_tokens.append(tok)
        return tok

    def collective(self, kind, groups, src, dst, sem_res, ntok=None):
        for t in (self.out_tokens if ntok is None else self.out_tokens[:ntok]):
            self._wait("pool", t)
        if sem_res.dsem is None:
            sem_res.dsem = self.new_sem(sem_res.name)
        ins = self.nc.gpsimd.collective_compute(kind, ALU.bypass, replica_groups=groups, ins=[src.opt()], outs=[dst.opt()])
        sem_res.dcnt += 1
        ins.then_inc(sem_res.dsem)
        tok = (sem_res.dsem, sem_res.dcnt)
        sem_res.last_w = tok
        sem_res.readers = []
        return tok

    def defer_collective(self, *args, front=False):
        item = args + (len(self.out_tokens),)
        if front:
            self.pending.insert(0, item)
        else:
            self.pending.append(item)

    def pump(self, every=4):
        self.pump_ctr += 1
        if self.pending and self.pump_ctr % every == 0:
            self.collective(*self.pending.pop(0))

    def flush_collectives(self):
        while self.pending:
            self.collective(*self.pending.pop(0))

    def gather(self, out, in_, idx_ap, reads=(), writes=()):
        self._deps("pool", reads, writes)
        r = writes[0]
        if r.dsem is None:
            r.dsem = self.new_sem(r.name)
        ins = self.nc.gpsimd.indirect_dma_start(out=out, out_offset=None, in_=in_,
                                                in_offset=bass.IndirectOffsetOnAxis(ap=idx_ap, axis=0))
        r.dcnt += 16
        ins.then_inc(r.dsem, 16)
        tok = (r.dsem, r.dcnt)
        self._commit(tok, reads, writes)
        return tok

    def drain(self):
        for e in self.eng:
            for t in self.out_tokens:
                self._wait(e, t)
        self.out_tokens = []
        self.barrier()

    def fence(self, eng):
        if self.ecnt[eng] > 0:
            self.eng[eng].wait_ge(self.esem[eng], self.ecnt[eng])

    def barrier(self):
        toks = [(self.esem[e], self.ecnt[e]) for e in self.eng if self.ecnt[e] > 0]
        for e in self.eng:
            for t in toks:
                self._wait(e, t)

    def finish(self):
        for t in self.out_tokens:
            self._wait("sp", t)
        self.out_tokens = []


class WStream:
    def __init__(self, S, slots, blocks, depth):
        self.S = S
        self.slots = slots
        self.blocks = blocks
        self.depth = depth
        self.issued = 0
        self.taken = 0

    def _view(self, tile, shape):
        n = int(np.prod(shape[1:]))
        view = tile[:, 0:n]
        if len(shape) == 3:
            view = view.rearrange("p (a b) -> p a b", a=shape[1])
        elif len(shape) == 4:
            view = view.rearrange("p (a b c) -> p a b c", a=shape[1], b=shape[2])
        return view

    def _issue(self):
        i = self.issued
        ap, dshape, vshape = self.blocks[i]
        tile, res = self.slots[i % len(self.slots)]
        self.S.dma("pool", out=self._view(tile, dshape), in_=ap, writes=[res])
        self.issued += 1

    def next(self):
        while self.issued < min(len(self.blocks), self.taken + self.depth + 1):
            self._issue()
        self.S.pump()
        i = self.taken
        self.taken += 1
        tile, res = self.slots[i % len(self.slots)]
        ap, dshape, vshape = self.blocks[i]
        return self._view(tile, vshape), res


class Ctx:
    pass


_ALLOC_N = [0]


def alloc(nc, stack, name, shape, dt):
    _ALLOC_N[0] += 1
    return stack.enter_context(nc.sbuf_tensor(f"s{_ALLOC_N[0]}_" + name, list(shape), dt))


def grouped_order(n_in, n_groups, n_out):
    order = []
    per = n_in // n_groups
    for f in range(per):
        order.append(("in", 0, f))
    for g in range(n_groups):
        if g + 1 < n_groups:
            for f in range(per):
                order.append(("in", g + 1, f))
        for o in range(n_out):
            order.append(("out", g, o))
    return order


def ffn_blocks(w_in, w_out):
    win = w_in.rearrange("(k p) (f c) -> p k f c", p=128, c=256)
    wout = w_out.rearrange("(f p) d -> p f d", p=128)
    blocks = []
    for kind, g, i in grouped_order(KF, KF // GRP, 8):
        if kind == "in":
            f = g * GRP + i
            blocks.append((win[:, :, f, :], (128, KD, 256), (128, KD, 2, 128)))
        else:
            blocks.append((wout[:, g * GRP:(g + 1) * GRP, i * 256:(i + 1) * 256], (128, GRP, 256), (128, GRP, 256)))
    return blocks


def conv_blocks(w_ci, w_co):
    wci = w_ci.rearrange("(k p) (f c) -> p k f c", p=128, c=384)
    wco = w_co.rearrange("(i p) d -> p i d", p=128)
    blocks = []
    for kind, g, i in grouped_order(KD, KD // GRP, 8):
        if kind == "in":
            f = g * GRP + i
            blocks.append((wci[:, :, f, 0:256], (128, KD, 256), (128, KD, 2, 128)))
            blocks.append((wci[:, :, f, 256:384], (128, KD, 128), (128, KD, 1, 128)))
        else:
            blocks.append((wco[:, g * GRP:(g + 1) * GRP, i * 256:(i + 1) * 256], (128, GRP, 256), (128, GRP, 256)))
    return blocks


def proj_blocks(w, col0, ncols):
    wv = w.rearrange("(k p) d -> p k d", p=128)
    return [(wv[:, :, col0 + i * 256: col0 + (i + 1) * 256], (128, KD, 256), (128, KD, 256)) for i in range(ncols // 256)]


def wo_blocks(w_o):
    wv = w_o.rearrange("(i p) d -> p i d", p=128)
    blocks = []
    for g in range(KD // GRP):
        for o in range(8):
            blocks.append((wv[:, g * GRP:(g + 1) * GRP, o * 256:(o + 1) * 256], (128, GRP, 256), (128, GRP, 256)))
    return blocks


def setup_common(nc, S, stack, C):
    C.banks = []
    for i in range(8):
        t = stack.enter_context(nc.psum_tensor(f"bank{i}", [128, 512], F32))
        C.banks.append((t, Res(f"bank{i}")))
    C.ones_f = alloc(nc, stack, "ones_f", [128, 128], F32)
    C.ones_b = alloc(nc, stack, "ones_b", [128, 128], BF16)
    C.r_const = Res("const")
    S.op("pool", lambda: nc.gpsimd.memset(C.ones_f[:], 1.0), writes=[C.r_const])
    S.op("pool", lambda: nc.gpsimd.memset(C.ones_b[:], 1.0), writes=[C.r_const])


def setup_stream_bufs(nc, S, stack, C, nslot=NSLOT):
    C.slots = []
    for i in range(nslot):
        t = alloc(nc, stack, f"wslot{i}", [128, SLOT_ELEMS], BF16)
        C.slots.append((t, Res(f"wslot{i}")))


def setup_act_bufs(nc, S, stack, C):
    C.xT = alloc(nc, stack, "xT", [128, KD, TW], F32)
    C.hT = alloc(nc, stack, "hT", [128, KD, TW], BF16)
    C.gT = [alloc(nc, stack, f"gT{i}", [128, GRP, TW], BF16) for i in range(2)]
    C.r_x = [[Res(f"x{k}_{t}") for t in range(3)] for k in range(KD)]
    C.r_h = [[Res(f"h{k}_{t}") for t in range(3)] for k in range(KD)]
    C.r_g = [[Res(f"g{i}_{t}") for t in range(3)] for i in range(2)]
    C.sq = [alloc(nc, stack, f"sq{i}", [128, 512], F32) for i in range(2)]
    C.r_sq = [Res(f"sq{i}") for i in range(2)]
    C.acc = alloc(nc, stack, "acc", [128, 512], F32)
    C.r_acc = Res("acc")
    C.sd = C.sq[0]
    C.r_sd = C.r_sq[0]
    C.rstd = alloc(nc, stack, "rstd", [128, TW], F32)
    C.r_rstd = [Res(f"rstd{t}") for t in range(3)]
    C.tmp = [alloc(nc, stack, f"tmp{i}", [128, 512], F32) for i in range(3)]
    C.r_tmp = [Res(f"tmp{i}") for i in range(3)]
    C.tmp_i = 0
    if not hasattr(C, "ADA"):
        setup_small(nc, S, stack, C)


def setup_small(nc, S, stack, C):
    C.ADA = alloc(nc, stack, "ADA", [128, NADA, KD], F32)
    C.r_ada = Res("ADA")
    C.AMOD = alloc(nc, stack, "AMOD", [128, 8, KD], F32)
    C.GATE = alloc(nc, stack, "GATE", [128, 6, KD], F32)
    C.vecs = alloc(nc, stack, "vecs", [128, NVEC, KD], F32)
    C.r_vecs = Res("vecs")
    C.r_mod = Res("mod")


def tile_idx(t0):
    return {0: 0, 2: 1, 514: 2}[t0]


def next_tmp(C):
    i = C.tmp_i % len(C.tmp)
    C.tmp_i += 1
    return C.tmp[i], C.r_tmp[i]


def emit_derived(nc, S, C):
    rd = [C.r_ada, C.r_vecs]
    for l in range(2):
        for sub in range(3):
            n = l * 3 + sub
            sc = C.ADA[:, l * 9 + sub * 3 + 1, :]
            gt = C.ADA[:, l * 9 + sub * 3 + 2, :]
            S.op("dve", lambda n=n, sc=sc: nc.vector.scalar_tensor_tensor(
                out=C.AMOD[:, n, :], in0=sc, scalar=1.0, in1=C.vecs[:, V_NORMG + n, :],
                op0=ALU.add, op1=ALU.mult), reads=rd, writes=[C.r_mod])
            S.op("dve", lambda n=n, gt=gt, sub=sub: nc.vector.tensor_scalar(
                out=C.GATE[:, n, :], in0=gt, scalar1=(1.0 if sub == 1 else 0.5), scalar2=None,
                op0=ALU.mult), reads=rd, writes=[C.r_mod])
    S.op("dve", lambda: nc.vector.scalar_tensor_tensor(
        out=C.AMOD[:, 6, :], in0=C.ADA[:, 19, :], scalar=1.0, in1=C.vecs[:, V_KVG, :],
        op0=ALU.add, op1=ALU.mult), reads=rd, writes=[C.r_mod])


def emit_rstd(nc, S, C, tiles):
    bank, r_bank = C.banks[7]
    for (t0, t1) in tiles:
        n = t1 - t0
        ti = tile_idx(t0)
        for k in range(KD):
            if k == 0:
                S.op("act", lambda: nc.scalar.activation(out=C.acc[:, :n], in_=C.xT[:, 0, t0:t1], func=AF.Square),
                     reads=[C.r_x[0][ti]], writes=[C.r_acc])
            else:
                sq, r_sq = C.sq[k % 2], C.r_sq[k % 2]
                S.op("act", lambda k=k, sq=sq: nc.scalar.activation(out=sq[:, :n], in_=C.xT[:, k, t0:t1], func=AF.Square),
                     reads=[C.r_x[k][ti]], writes=[r_sq])
                S.op("dve", lambda sq=sq: nc.vector.tensor_tensor(out=C.acc[:, :n], in0=C.acc[:, :n], in1=sq[:, :n], op=ALU.add),
                     reads=[r_sq, C.r_acc], writes=[C.r_acc])
        S.op("pe", lambda: nc.tensor.matmul(bank[:, :n], lhsT=C.ones_f[:], rhs=C.acc[:, :n], start=True, stop=True),
             reads=[C.r_acc, C.r_const], writes=[r_bank])
        S.op("act", lambda: nc.scalar.activation(out=C.sd[:, :n], in_=bank[:, :n], func=AF.Sqrt, scale=1.0 / D, bias=EPS),
             reads=[r_bank], writes=[C.r_sd])
        S.op("dve", lambda: nc.vector.reciprocal(out=C.rstd[:, t0:t1], in_=C.sd[:, :n]),
             reads=[C.r_sd], writes=[C.r_rstd[ti]])


def emit_modulate(nc, S, C, tiles, a_ap, sh_ap):
    for (t0, t1) in tiles:
        n = t1 - t0
        ti = tile_idx(t0)
        for k in range(KD):
            tmp, r_tmp = next_tmp(C)
            S.op("dve", lambda k=k, tmp=tmp: nc.vector.scalar_tensor_tensor(
                out=tmp[:, :n], in0=C.xT[:, k, t0:t1], scalar=a_ap[:, k:k + 1], in1=C.rstd[:, t0:t1],
                op0=ALU.mult, op1=ALU.mult), reads=[C.r_x[k][ti], C.r_rstd[ti], C.r_mod], writes=[r_tmp])
            S.op("act", lambda k=k, tmp=tmp: nc.scalar.activation(
                out=C.hT[:, k, t0:t1], in_=tmp[:, :n], func=AF.Identity, bias=sh_ap[:, k:k + 1], scale=1.0),
                reads=[r_tmp, C.r_mod], writes=[C.r_h[k][ti]])


def emit_out_block(nc, S, C, slot, r_slot, o, src, r_src, tiles, gate_ap, bank_ctr, nbanks=2):
    for dd in range(2):
        d = 2 * o + dd
        for (t0, t1) in tiles:
            n = t1 - t0
            ti = tile_idx(t0)
            bank, r_bank = C.banks[4 + (bank_ctr[0] % nbanks)]
            bank_ctr[0] += 1

            def mm(bank=bank, dd=dd, t0=t0, t1=t1, n=n):
                for fi in range(GRP):
                    ins = nc.tensor.matmul(bank[:, :n], lhsT=slot[:, fi, dd * 128:(dd + 1) * 128],
                                           rhs=src[:, fi, t0:t1], start=(fi == 0), stop=(fi == GRP - 1))
                return ins
            S.op("pe", mm, reads=[r_slot, r_src[ti]], writes=[r_bank])
            S.op("dve", lambda bank=bank, d=d, t0=t0, t1=t1, n=n: nc.vector.scalar_tensor_tensor(
                out=C.xT[:, d, t0:t1], in0=bank[:, :n], scalar=gate_ap[:, d:d + 1], in1=C.xT[:, d, t0:t1],
                op0=ALU.mult, op1=ALU.add), reads=[r_bank, C.r_x[d][ti], C.r_mod], writes=[C.r_x[d][ti]])


def emit_ffn(nc, S, C, ws, tiles, gate_ap):
    pair = [0]
    octr = [0]
    silu_i = [0]
    for kind, g, i in grouped_order(KF, KF // GRP, 8):
        slot, r_slot = ws.next()
        if kind == "in":
            gbuf = C.gT[g % 2]
            r_gb = C.r_g[g % 2]
            for (t0, t1) in tiles:
                n = t1 - t0
                ti = tile_idx(t0)
                pa, r_pa = C.banks[(pair[0] % 2) * 2]
                pb, r_pb = C.banks[(pair[0] % 2) * 2 + 1]
                pair[0] += 1

                def mm(pa=pa, pb=pb, t0=t0, t1=t1, n=n):
                    for k in range(KD):
                        nc.tensor.matmul(pa[:, :n], lhsT=slot[:, k, 0, :], rhs=C.hT[:, k, t0:t1],
                                         start=(k == 0), stop=(k == KD - 1))
                    for k in range(KD):
                        ins = nc.tensor.matmul(pb[:, :n], lhsT=slot[:, k, 1, :], rhs=C.hT[:, k, t0:t1],
                                               start=(k == 0), stop=(k == KD - 1))
                    return ins
                S.op("pe", mm, reads=[r_slot] + [C.r_h[k][ti] for k in range(KD)], writes=[r_pa, r_pb])
                tmp, r_tmp = next_tmp(C)
                S.op("act", lambda pa=pa, tmp=tmp, n=n: nc.scalar.activation(out=tmp[:, :n], in_=pa[:, :n], func=AF.Silu),
                     reads=[r_pa], writes=[r_tmp])
                S.op("dve", lambda pb=pb, tmp=tmp, n=n, t0=t0, t1=t1, gbuf=gbuf, i=i: nc.vector.tensor_tensor(
                    out=gbuf[:, i, t0:t1], in0=pb[:, :n], in1=tmp[:, :n], op=ALU.mult),
                    reads=[r_pb, r_tmp], writes=[r_gb[ti]])
        else:
            emit_out_block(nc, S, C, slot, r_slot, i, C.gT[g % 2], C.r_g[g % 2], tiles, gate_ap, octr, nbanks=4)


def emit_conv(nc, S, C, ws, c, gate_ap, flag):
    octr = [0]
    bank7, r_b7 = C.banks[7]
    bank6, r_b6 = C.banks[6]
    cw = lambda k: C.vecs[:, V_CONVW + k, :]
    cb = C.vecs[:, V_CONVB, :]
    for kind, g, i in grouped_order(KD, KD // GRP, 8):
        slot, r_slot = ws.next()
        if kind == "out":
            emit_out_block(nc, S, C, slot, r_slot, i, C.gT[g % 2], C.r_g[g % 2], TILES_M, gate_ap, octr)
            continue
        f = g * GRP + i
        slot2, r_slot2 = ws.next()
        U, r_U = C.U[f % 2], C.r_U[f % 2]
        BG, r_BG = C.BG[f % 2], C.r_BG[f % 2]
        rh = lambda ti: [C.r_h[k][ti] for k in range(KD)]

        def mmh():
            for k in range(KD):
                nc.tensor.matmul(bank7[:, 0:2], lhsT=slot[:, k, 1, :], rhs=C.hT[:, k, 0:2], start=(k == 0), stop=(k == KD - 1))
            for k in range(KD):
                ins = nc.tensor.matmul(bank7[:, 2:4], lhsT=slot2[:, k, 0, :], rhs=C.hT[:, k, 0:2], start=(k == 0), stop=(k == KD - 1))
            return ins
        S.op("pe", mmh, reads=[r_slot, r_slot2] + rh(0), writes=[r_b7])
        S.op("act", lambda: nc.scalar.activation(out=C.cgh[:, 0:2], in_=bank7[:, 0:2], func=AF.Identity),
             reads=[r_b7], writes=[C.r_cgh])
        S.op("dve", lambda U=U: nc.vector.scalar_tensor_tensor(
            out=U[:, 0:2], in0=bank7[:, 2:4], scalar=flag[:, c:c + 1], in1=C.cgh[:, 0:2],
            op0=ALU.mult, op1=ALU.mult), reads=[r_b7, C.r_cgh, C.r_vecs], writes=[r_U])
        for si, (t0, t1) in enumerate(TILES_M):
            ti = tile_idx(t0)
            pcg, r_pcg = C.banks[si * 2]
            pxv, r_pxv = C.banks[si * 2 + 1]

            def mm(pcg=pcg, pxv=pxv, t0=t0, t1=t1):
                for k in range(KD):
                    nc.tensor.matmul(bank6[:, :], lhsT=slot[:, k, 0, :], rhs=C.hT[:, k, t0:t1], start=(k == 0), stop=(k == KD - 1))
                for k in range(KD):
                    nc.tensor.matmul(pcg[:, :], lhsT=slot[:, k, 1, :], rhs=C.hT[:, k, t0:t1], start=(k == 0), stop=(k == KD - 1))
                for k in range(KD):
                    ins = nc.tensor.matmul(pxv[:, :], lhsT=slot2[:, k, 0, :], rhs=C.hT[:, k, t0:t1], start=(k == 0), stop=(k == KD - 1))
                return ins
            S.op("pe", mm, reads=[r_slot, r_slot2] + rh(ti), writes=[r_b6, r_pcg, r_pxv])
            S.op("act", lambda BG=BG, t0=t0, t1=t1: nc.scalar.activation(out=BG[:, t0 - 2:t1 - 2], in_=bank6[:, :], func=AF.Identity),
                 reads=[r_b6], writes=[r_BG])
            tmp, r_tmp = next_tmp(C)
            S.op("act", lambda tmp=tmp, pcg=pcg: nc.scalar.activation(out=tmp[:, :], in_=pcg[:, :], func=AF.Identity),
                 reads=[r_pcg], writes=[r_tmp])
            S.op("dve", lambda U=U, tmp=tmp, pxv=pxv, t0=t0, t1=t1: nc.vector.tensor_tensor(
                out=U[:, t0:t1], in0=pxv[:, :], in1=tmp[:, :], op=ALU.mult), reads=[r_pxv, r_tmp], writes=[r_U])
        S.op("dve", lambda U=U, f=f: nc.vector.tensor_scalar(
            out=C.c1[:, :], in0=U[:, 2:TW], scalar1=cw(2)[:, f:f + 1], scalar2=cb[:, f:f + 1], op0=ALU.mult, op1=ALU.add),
            reads=[r_U, C.r_vecs], writes=[C.r_c1])
        S.op("dve", lambda U=U, f=f: nc.vector.scalar_tensor_tensor(
            out=C.c1[:, :], in0=U[:, 1:TW - 1], scalar=cw(1)[:, f:f + 1], in1=C.c1[:, :], op0=ALU.mult, op1=ALU.add),
            reads=[r_U, C.r_c1], writes=[C.r_c1])
        S.op("dve", lambda U=U, f=f: nc.vector.scalar_tensor_tensor(
            out=C.c1[:, :], in0=U[:, 0:TW - 2], scalar=cw(0)[:, f:f + 1], in1=C.c1[:, :], op0=ALU.mult, op1=ALU.add),
            reads=[r_U, C.r_c1], writes=[C.r_c1])
        if getattr(C, "dbgc", None) is not None and f == 0:
            S.dma("sp", out=C.dbgc[0, :, :], in_=U[:, :], reads=[r_U], sem_res=Res("dU"), is_output=True)
            S.dma("sp", out=C.dbgc[1, :, 0:TOK], in_=BG[:, :], reads=[r_BG], sem_res=Res("dBG"), is_output=True)
            S.dma("sp", out=C.dbgc[2, :, 0:TOK], in_=C.c1[:, :], reads=[C.r_c1], sem_res=Res("dc1"), is_output=True)
        gb = C.gT[g % 2]
        S.op("dve", lambda gb=gb, BG=BG, i=i: nc.vector.tensor_tensor(
            out=gb[:, i, 2:TW], in0=C.c1[:, :], in1=BG[:, :], op=ALU.mult),
            reads=[C.r_c1, r_BG], writes=[C.r_g[g % 2][1], C.r_g[g % 2][2]])


def emit_featproj(nc, S, C, ws, nblk, dst_fn, stage_name):
    ctr = C.proj_ctr
    for hp in range(nblk):
        slot, r_slot = ws.next()
        for hh in range(2):
            h = 2 * hp + hh
            for (t0, t1) in TILES_M:
                ti = tile_idx(t0)
                bank, r_bank = C.banks[ctr[0] % 4]
                st, r_st = C.qst[ctr[0] % 3], C.r_qst[ctr[0] % 3]
                ctr[0] += 1

                def mm(bank=bank, hh=hh, t0=t0, t1=t1):
                    for k in range(KD):
                        ins = nc.tensor.matmul(bank[:, :], lhsT=slot[:, k, hh * 128:(hh + 1) * 128], rhs=C.hT[:, k, t0:t1],
                                               start=(k == 0), stop=(k == KD - 1))
                    return ins
                S.op("pe", mm, reads=[r_slot] + [C.r_h[k][ti] for k in range(KD)], writes=[r_bank])
                S.op("act", lambda bank=bank, st=st: nc.scalar.activation(out=st[:, :], in_=bank[:, :], func=AF.Identity),
                     reads=[r_bank], writes=[r_st])
                S.dma("sp", out=dst_fn(h, t0 - 2, t1 - 2), in_=st[:, :], reads=[r_st], sem_res=r_st, is_output=True)


def emit_proj(nc, S, C, ws, c, O):
    emit_rstd(nc, S, C, TILES_M)
    emit_modulate(nc, S, C, TILES_M, C.AMOD[:, 4, :], C.ADA[:, 12, :])
    emit_featproj(nc, S, C, ws, 8, lambda h, a, b: O.qdst(c, h, a, b), "q")
    if hasattr(O, "after"):
        O.after(S, "q", c)
    emit_modulate(nc, S, C, TILES_M, C.AMOD[:, 6, :], C.ADA[:, 18, :])
    emit_featproj(nc, S, C, ws, 8, lambda h, a, b: O.kdst(c, h, a, b), "k")
    if hasattr(O, "after"):
        O.after(S, "k", c)
    ctr = C.proj_ctr
    for vb in range(8):
        slot, r_slot = ws.next()
        for ts in range(8):
            ti = 1 if ts < 4 else 2
            c0 = 2 + ts * 128
            bank, r_bank = C.banks[ctr[0] % 4]
            st, r_st = C.vst[ctr[0] % 3], C.r_vst[ctr[0] % 3]
            ctr[0] += 1

            def mm(bank=bank, c0=c0):
                for k in range(KD):
                    ins = nc.tensor.matmul(bank[:, 0:256], lhsT=C.hT[:, k, c0:c0 + 128], rhs=slot[:, k, :],
                                           start=(k == 0), stop=(k == KD - 1))
                return ins
            S.op("pe", mm, reads=[r_slot] + [C.r_h[k][ti] for k in range(KD)], writes=[r_bank])
            S.op("act", lambda bank=bank, st=st: nc.scalar.activation(out=st[:, :], in_=bank[:, 0:256], func=AF.Identity),
                 reads=[r_bank], writes=[r_st])
            O.vstore(S, c, ts, vb, st, r_st)
    if hasattr(O, "after"):
        O.after(S, "v", c)
    for (t0, t1) in TILES_M:
        ti = tile_idx(t0)
        bank, r_bank = C.banks[ctr[0] % 4]
        es, r_es = C.est[ctr[0] % 2], C.r_est[ctr[0] % 2]
        ls, r_ls = C.lst[ctr[0] % 2], C.r_lst[ctr[0] % 2]
        ctr[0] += 1

        def mm(bank=bank, t0=t0, t1=t1):
            for k in range(KD):
                ins = nc.tensor.matmul(bank[0:16, :], lhsT=C.wfb[:, k, :], rhs=C.hT[:, k, t0:t1],
                                       start=(k == 0), stop=(k == KD - 1))
            return ins
        S.op("pe", mm, reads=[C.r_wf] + [C.r_h[k][ti] for k in range(KD)], writes=[r_bank])
        S.op("act", lambda bank=bank, es=es: nc.scalar.activation(out=es[:, :], in_=bank[0:16, :], func=AF.Exp,
                                                                  scale=-1.0, bias=C.negb[:, 0:1]),
             reads=[r_bank, C.r_negb], writes=[r_es])
        S.op("act", lambda es=es, ls=ls: nc.scalar.activation(out=ls[:, :], in_=es[:, :], func=AF.Ln, bias=1.0),
             reads=[r_es], writes=[r_ls])
        S.dma("sp", out=O.ldst(c, t0 - 2, t1 - 2), in_=ls[:, :], reads=[r_ls], sem_res=r_ls, is_output=True)


def emit_ada_sharded(nc, S, C, I, AGi, AGo):
    scratch = C.hT[:].rearrange("p k t -> p (k t)").bitcast(F32)
    wb = [scratch[:, i * 2048:(i + 1) * 2048] for i in range(3)]
    r_wb = [Res(f"adaw{i}") for i in range(3)]
    accA = scratch[:, 6144:8192]
    r_accA = Res("accA")
    bank6, r_b6 = C.banks[6]
    S.dma("sp", out=C.vecs[:], in_=I.vecs[:, :, :], writes=[C.r_vecs])
    S.dma("sp", out=C.cond[:], in_=I.condT[:, :], writes=[C.r_cond])
    r_bpart = Res("bpart")
    S.dma("sp", out=C.bpart[:], in_=I.bada_part[:, :, :], writes=[r_bpart])
    S.op("act", lambda: nc.scalar.activation(out=C.cond[:], in_=C.cond[:], func=AF.Silu), reads=[C.r_cond], writes=[C.r_cond])
    r_part = Res("adapart")
    n = 0
    for s in range(5):
        for k in range(KD):
            w, r_w = wb[n % 3], r_wb[n % 3]
            n += 1
            S.dma("sp" if n % 2 else "act", out=w, in_=I.w_ada_part[s, k * 128:(k + 1) * 128, :], writes=[r_w])
            if k == 0:
                S.op("dve", lambda w=w: nc.vector.tensor_scalar(out=accA, in0=w, scalar1=C.cond[:, 0:1], scalar2=None, op0=ALU.mult),
                     reads=[r_w, C.r_cond], writes=[r_accA])
            else:
                S.op("dve", lambda w=w, k=k: nc.vector.scalar_tensor_tensor(
                    out=accA, in0=w, scalar=C.cond[:, k:k + 1], in1=accA, op0=ALU.mult, op1=ALU.add),
                    reads=[r_w, C.r_cond, r_accA], writes=[r_accA])

        def mm():
            for dc in range(KD):
                ins = nc.tensor.matmul(bank6[:, dc:dc + 1], lhsT=accA[:, dc * 128:(dc + 1) * 128], rhs=C.ones_f[:, 0:1],
                                       start=True, stop=True)
            return ins
        S.op("pe", mm, reads=[r_accA, C.r_const], writes=[r_b6])
        S.op("dve", lambda s=s: nc.vector.tensor_tensor(out=C.apart[:, s, :], in0=bank6[:, 0:KD], in1=C.bpart[:, s, :], op=ALU.add),
             reads=[r_b6, r_bpart], writes=[r_part])
    S.dma("sp", out=AGi.rearrange("p (s k) -> p s k", s=5), in_=C.apart[:], reads=[r_part], sem_res=r_part, is_output=True)
    r_ag = Res("ccada")
    S.collective("AllGather", GROUPS, AGi, AGo, r_ag)
    for r in range(4):
        S.dma("sp", out=C.ADA[:, r * 5:(r + 1) * 5, :], in_=AGo[r * 128:(r + 1) * 128, :].rearrange("p (s k) -> p s k", s=5),
              reads=[r_ag], writes=[C.r_ada])
    S.barrier()


def emit_ada(nc, S, C, I):
    scratch = C.hT[:].rearrange("p k t -> p (k t)").bitcast(F32)
    wb = [scratch[:, i * 2048:(i + 1) * 2048] for i in range(3)]
    r_wb = [Res(f"adaw{i}") for i in range(3)]
    accA = scratch[:, 6144:8192]
    r_accA = Res("accA")
    bank6, r_b6 = C.banks[6]
    S.dma("sp", out=C.vecs[:], in_=I.vecs[:, :, :], writes=[C.r_vecs])
    S.dma("sp", out=C.cond[:], in_=I.condT[:, :], writes=[C.r_cond])
    S.op("act", lambda: nc.scalar.activation(out=C.cond[:], in_=C.cond[:], func=AF.Silu), reads=[C.r_cond], writes=[C.r_cond])
    n = 0
    for s in range(NADA):
        if s < 18:
            l, col = s // 9, (s % 9) * 2048
            src = lambda k: I.w_ada[l, k * 128:(k + 1) * 128, col:col + 2048]
            bvec = C.vecs[:, V_BADA + s, :]
        else:
            col = (s - 18) * 2048
            src = lambda k: I.w_ada_kv[k * 128:(k + 1) * 128, col:col + 2048]
            bvec = C.vecs[:, V_BKV + (s - 18), :]
        for k in range(KD):
            w, r_w = wb[n % 3], r_wb[n % 3]
            n += 1
            S.dma("sp", out=w, in_=src(k), writes=[r_w])
            if k == 0:
                S.op("dve", lambda w=w: nc.vector.tensor_scalar(out=accA, in0=w, scalar1=C.cond[:, 0:1], scalar2=None, op0=ALU.mult),
                     reads=[r_w, C.r_cond], writes=[r_accA])
            else:
                S.op("dve", lambda w=w, k=k: nc.vector.scalar_tensor_tensor(
                    out=accA, in0=w, scalar=C.cond[:, k:k + 1], in1=accA, op0=ALU.mult, op1=ALU.add),
                    reads=[r_w, C.r_cond, r_accA], writes=[r_accA])

        def mm():
            for dc in range(KD):
                ins = nc.tensor.matmul(bank6[:, dc:dc + 1], lhsT=accA[:, dc * 128:(dc + 1) * 128], rhs=C.ones_f[:, 0:1],
                                       start=True, stop=True)
            return ins
        S.op("pe", mm, reads=[r_accA, C.r_const], writes=[r_b6])
        S.op("dve", lambda s=s, bvec=bvec: nc.vector.tensor_tensor(out=C.ADA[:, s, :], in0=bank6[:, 0:KD], in1=bvec, op=ALU.add),
             reads=[r_b6, C.r_vecs], writes=[C.r_ada])
    S.barrier()


class IO:
    pass


def build_A(nc, debug=False):
    I = IO()
    I.xin = nc.dram_tensor("xin", [2, 128, KD, TW], F32, kind="ExternalInput").ap()
    I.flag = nc.dram_tensor("flag", [128, 2], F32, kind="ExternalInput").ap()
    I.condT = nc.dram_tensor("condT", [128, KD], F32, kind="ExternalInput").ap()
    I.vecs = nc.dram_tensor("vecs", [128, NVEC, KD], F32, kind="ExternalInput").ap()
    I.w_ada = nc.dram_tensor("w_ada", [2, D, 9 * D], F32, kind="ExternalInput").ap()
    I.w_ada_kv = nc.dram_tensor("w_ada_kv", [D, 2 * D], F32, kind="ExternalInput").ap()
    I.w_ffn_in = nc.dram_tensor("w_ffn_in", [3, D, 2 * FF], F32, kind="ExternalInput").ap()
    I.w_ffn_out = nc.dram_tensor("w_ffn_out", [3, FF, D], F32, kind="ExternalInput").ap()
    I.w_ci = nc.dram_tensor("w_ci", [D, 3 * D], F32, kind="ExternalInput").ap()
    I.w_co = nc.dram_tensor("w_co", [D, D], F32, kind="ExternalInput").ap()
    I.w_q = nc.dram_tensor("w_q", [D, D], F32, kind="ExternalInput").ap()
    I.w_kv = nc.dram_tensor("w_kv", [D, 2 * D], F32, kind="ExternalInput").ap()
    I.w_f = nc.dram_tensor("w_f", [128, KD, NH], F32, kind="ExternalInput").ap()
    I.b_f = nc.dram_tensor("b_f", [NH, 1], F32, kind="ExternalInput").ap()
    O = IO()
    O.xmid = nc.dram_tensor("xmid", [2, 128, KD, TOK], F32, kind="ExternalOutput").ap()
    O.q = nc.dram_tensor("q_o", [2, NH, 128, TOK], BF16, kind="ExternalOutput").ap()
    O.k = nc.dram_tensor("k_o", [2, NH, 128, TOK], BF16, kind="ExternalOutput").ap()
    O.v = nc.dram_tensor("v_o", [2, TOK, D], BF16, kind="ExternalOutput").ap()
    O.l = nc.dram_tensor("l_o", [2, NH, TOK], F32, kind="ExternalOutput").ap()
    O.ada = nc.dram_tensor("ada_o", [128, NADA, KD], F32, kind="ExternalOutput").ap()
    O.qdst = lambda c, h, a, b: O.q[c, h, :, a:b]
    O.kdst = lambda c, h, a, b: O.k[c, h, :, a:b]
    O.ldst = lambda c, a, b: O.l[c, :, a:b]
    O.vstore = lambda S, c, ts, vb, st, r_st: S.dma(
        "sp", out=O.v[c, ts * 128:(ts + 1) * 128, vb * 256:(vb + 1) * 256], in_=st[:, :],
        reads=[r_st], sem_res=r_st, is_output=True)
    if debug:
        O.dbg = nc.dram_tensor("dbg", [3, 128, KD, TW], F32, kind="ExternalOutput").ap()
        O.dbgc = nc.dram_tensor("dbgc", [3, 128, TW], F32, kind="ExternalOutput").ap()
        O.dbgh = nc.dram_tensor("dbgh", [128, KD, TW], BF16, kind="ExternalOutput").ap()

    def dbg_store(S, C, i):
        if debug:
            S.dma("sp", out=O.dbg[i, :, :, :], in_=C.xT[:], reads=[C.r_x[k][t] for k in range(KD) for t in range(3)],
                  sem_res=Res(f"dbg{i}"), is_output=True)

    with ExitStack() as stack:
        S = Sched(nc, stack)
        C = Ctx()
        setup_common(nc, S, stack, C)
        setup_stream_bufs(nc, S, stack, C)
        setup_act_bufs(nc, S, stack, C)
        C.cond = alloc(nc, stack, "cond", [128, KD], F32)
        C.r_cond = Res("cond")
        C.flag = alloc(nc, stack, "flag", [128, 2], F32)
        C.U = [alloc(nc, stack, f"U{i}", [128, TW], F32) for i in range(2)]
        C.r_U = [Res(f"U{i}") for i in range(2)]
        C.BG = [alloc(nc, stack, f"BG{i}", [128, TOK], F32) for i in range(2)]
        C.r_BG = [Res(f"BG{i}") for i in range(2)]
        C.c1 = alloc(nc, stack, "c1", [128, TOK], F32)
        C.r_c1 = Res("c1")
        C.cgh = alloc(nc, stack, "cgh", [128, 2], F32)
        C.r_cgh = Res("cgh")
        C.qst = [alloc(nc, stack, f"qst{i}", [128, 512], BF16) for i in range(3)]
        C.r_qst = [Res(f"qst{i}") for i in range(3)]
        C.vst = [alloc(nc, stack, f"vst{i}", [128, 256], BF16) for i in range(3)]
        C.r_vst = [Res(f"vst{i}") for i in range(3)]
        C.est = [alloc(nc, stack, "est0", [NH, 512], F32)] * 2
        C.r_est = [Res("est0")] * 2
        C.lst = [alloc(nc, stack, f"lst{i}", [NH, 512], F32) for i in range(2)]
        C.r_lst = [Res(f"lst{i}") for i in range(2)]
        C.wff = alloc(nc, stack, "wff", [128, KD, NH], F32)
        C.wfb = alloc(nc, stack, "wfb", [128, KD, NH], BF16)
        C.negb = alloc(nc, stack, "negb", [NH, 1], F32)
        C.r_wf = Res("wf")
        C.r_negb = Res("negb")
        C.proj_ctr = [0]
        C.r_xst = [Res(f"xst{i}") for i in range(4)]

        S.dma("sp", out=C.flag[:], in_=I.flag[:, :], writes=[C.r_vecs])
        S.dma("sp", out=C.wff[:], in_=I.w_f[:, :, :], writes=[C.r_wf])
        S.dma("sp", out=C.negb[:], in_=I.b_f[:, :], writes=[C.r_negb])
        S.op("dve", lambda: nc.vector.tensor_copy(out=C.wfb[:], in_=C.wff[:]), reads=[C.r_wf], writes=[C.r_wf])
        S.op("dve", lambda: nc.vector.tensor_scalar(out=C.negb[:], in0=C.negb[:], scalar1=-1.0, scalar2=None, op0=ALU.mult),
             reads=[C.r_negb], writes=[C.r_negb])
        emit_ada(nc, S, C, I)
        emit_derived(nc, S, C)
        S.dma("sp", out=O.ada[:, :, :], in_=C.ADA[:], reads=[C.r_ada], sem_res=Res("adao"), is_output=True)

        blocks = []
        for c in range(2):
            blocks += ffn_blocks(I.w_ffn_in[0], I.w_ffn_out[0])
            blocks += conv_blocks(I.w_ci, I.w_co)
            blocks += ffn_blocks(I.w_ffn_in[1], I.w_ffn_out[1])
            blocks += ffn_blocks(I.w_ffn_in[2], I.w_ffn_out[2])
            blocks += proj_blocks(I.w_q, 0, D)
            blocks += proj_blocks(I.w_kv, 0, D)
            blocks += proj_blocks(I.w_kv, D, D)
        ws = WStream(S, C.slots, blocks, NSLOT - 2)

        for c in range(1 if debug else 2):
            for k0 in range(0, KD, 4):
                S.dma("sp", out=C.xT[:, k0:k0 + 4, :], in_=I.xin[c, :, k0:k0 + 4, :],
                      writes=[C.r_x[k][t] for k in range(k0, k0 + 4) for t in range(3)])
            emit_rstd(nc, S, C, TILES_H)
            emit_modulate(nc, S, C, TILES_H, C.AMOD[:, 0, :], C.ADA[:, 0, :])
            emit_ffn(nc, S, C, ws, TILES_H, C.GATE[:, 0, :])
            if c == 0:
                dbg_store(S, C, 0)
            emit_rstd(nc, S, C, TILES_H)
            emit_modulate(nc, S, C, TILES_H, C.AMOD[:, 1, :], C.ADA[:, 3, :])
            if debug and c == 0:
                C.dbgc = O.dbgc
                S.dma("sp", out=O.dbgh[:, :, :], in_=C.hT[:], reads=[C.r_h[k][t] for k in range(KD) for t in range(3)],
                      sem_res=Res("dbgh"), is_output=True)
            emit_conv(nc, S, C, ws, c, C.GATE[:, 1, :], C.flag)
            C.dbgc = None
            if c == 0:
                dbg_store(S, C, 1)
            emit_rstd(nc, S, C, TILES_M)
            emit_modulate(nc, S, C, TILES_M, C.AMOD[:, 2, :], C.ADA[:, 6, :])
            emit_ffn(nc, S, C, ws, TILES_M, C.GATE[:, 2, :])
            if c == 0:
                dbg_store(S, C, 2)
            emit_rstd(nc, S, C, TILES_M)
            emit_modulate(nc, S, C, TILES_M, C.AMOD[:, 3, :], C.ADA[:, 9, :])
            emit_ffn(nc, S, C, ws, TILES_M, C.GATE[:, 3, :])
            emit_proj(nc, S, C, ws, c, O)
            for k0 in range(0, KD, 4):
                S.dma("sp", out=O.xmid[c, :, k0:k0 + 4, :], in_=C.xT[:, k0:k0 + 4, 2:TW],
                      reads=[C.r_x[k][t] for k in range(k0, k0 + 4) for t in (1, 2)], sem_res=C.r_xst[k0 // 4], is_output=True)
        S.finish()
    return nc


def _pk(v):
    return np.ascontiguousarray(np.asarray(v, np.float32).reshape(KD, 128).T)


def chunk_of(core, c):
    j = core % 4
    return j if c == 0 else 7 - j


def host_common(inp):
    H = {}
    vec = np.zeros((128, NVEC, KD), np.float32)
    for l in range(2):
        for sub in range(3):
            vec[:, V_NORMG + l * 3 + sub, :] = _pk(inp["norm_g"][l, sub])
    vec[:, V_KVG, :] = _pk(inp["kv_norm_g"])
    vec[:, V_FING, :] = _pk(inp["final_g"])
    for k in range(3):
        vec[:, V_CONVW + k, :] = _pk(inp["conv_w"][0, k])
    vec[:, V_CONVB, :] = _pk(inp["conv_b"][0])
    for l in range(2):
        for s in range(9):
            vec[:, V_BADA + l * 9 + s, :] = _pk(inp["b_ada"][l, s * D:(s + 1) * D])
    for s in range(2):
        vec[:, V_BKV + s, :] = _pk(inp["b_ada_kv"][s * D:(s + 1) * D])
    H["vecs"] = vec

    def perm_in(w):
        return np.ascontiguousarray(w.reshape(D, 2, KF, 128).transpose(0, 2, 1, 3).reshape(D, 2 * FF))
    wfi = np.asarray(inp["w_ffn_in"], np.float32)
    H["w_ffn_in4"] = [perm_in(wfi[0, 0]), perm_in(wfi[0, 1]), perm_in(wfi[1, 0]), perm_in(wfi[1, 1])]
    wci = np.asarray(inp["w_conv_in"], np.float32)[0]
    H["w_ci"] = np.ascontiguousarray(wci.reshape(D, 3, KD, 128).transpose(0, 2, 1, 3).reshape(D, 3 * D))
    wkvf = np.asarray(inp["w_kvf"], np.float32)
    H["w_kv"] = np.ascontiguousarray(wkvf[:, :2 * D])
    H["w_f"] = np.ascontiguousarray(wkvf[:, 2 * D:].reshape(KD, 128, NH).transpose(1, 0, 2))
    H["b_f"] = np.ascontiguousarray(np.asarray(inp["b_fgate"], np.float32).reshape(NH, 1))
    return H


def host_in_A(inp, H):
    x = np.asarray(inp["x"], np.float32)
    cvec = np.asarray(inp["c"], np.float32)
    wfo = np.asarray(inp["w_ffn_out"], np.float32)
    shared = {
        "vecs": H["vecs"],
        "w_ada": np.asarray(inp["w_ada"], np.float32),
        "w_ada_kv": np.asarray(inp["w_ada_kv"], np.float32),
        "w_ffn_in": np.stack(H["w_ffn_in4"][:3]),
        "w_ffn_out": np.ascontiguousarray(np.stack([wfo[0, 0], wfo[0, 1], wfo[1, 0]])),
        "w_ci": H["w_ci"],
        "w_co": np.asarray(inp["w_conv_out"], np.float32)[0],
        "w_q": np.asarray(inp["w_q"], np.float32)[0],
        "w_kv": H["w_kv"],
        "w_f": H["w_f"],
        "b_f": H["b_f"],
    }
    maps = []
    for core in range(8):
        b = core // 4
        xin = np.zeros((2, 128, KD, TW), np.float32)
        flag = np.zeros((128, 2), np.float32)
        for c in range(2):
            ci = chunk_of(core, c)
            lo = ci * TOK - HALO
            seg = np.zeros((TW, D), np.float32)
            if lo < 0:
                seg[HALO:] = x[b, 0:TOK]
            else:
                seg[:] = x[b, lo:lo + TW]
                flag[:, c] = 1.0
            xin[c] = seg.T.reshape(KD, 128, TW).transpose(1, 0, 2)
        m = dict(shared)
        m["xin"] = xin
        m["flag"] = flag
        m["condT"] = _pk(cvec[b])
        maps.append(m)
    return maps


def run_prog(build_fn, in_maps):
    nc = bass.Bass("TRN2", target_bir_lowering=False)
    build_fn(nc)
    res = run_bass_kernel_spmd(nc, in_maps, core_ids=list(range(8)))
    return res.results


HPC = 4
NQT = SEQ // 512
NKC = SEQ // 128
ATT_SCALE = 1.0 / float(np.sqrt(128.0))


def build_B1(nc):
    qT = nc.dram_tensor("qT", [HPC, 128, SEQ], BF16, kind="ExternalInput").ap()
    kT = nc.dram_tensor("kT", [HPC, 128, SEQ], BF16, kind="ExternalInput").ap()
    vv = nc.dram_tensor("vv", [HPC, SEQ, 128], BF16, kind="ExternalInput").ap()
    ll = nc.dram_tensor("ll", [HPC, SEQ], F32, kind="ExternalInput").ap()
    mask_d = nc.dram_tensor("mask", [128, 4, 512], F32, kind="ExternalInput").ap()
    sel_d = nc.dram_tensor("sel", [HPC, HPC + 1, 128], F32, kind="ExternalInput").ap()
    oT = nc.dram_tensor("oT", [HPC, 128, SEQ], BF16, kind="ExternalOutput").ap()
    with ExitStack() as stack:
        S = Sched(nc, stack)
        C = Ctx()
        setup_common(nc, S, stack, C)
        KT = [alloc(nc, stack, f"KT{i}", [128, SEQ], BF16) for i in range(2)]
        QT = [alloc(nc, stack, f"QT{i}", [128, SEQ], BF16) for i in range(2)]
        VV = [alloc(nc, stack, f"VV{i}", [128, NKC, 128], BF16) for i in range(2)]
        r_K = [Res(f"KT{i}") for i in range(2)]
        r_Q = [Res(f"QT{i}") for i in range(2)]
        r_V = [Res(f"VV{i}") for i in range(2)]
        Lrow = alloc(nc, stack, "Lrow", [HPC, SEQ], F32)
        r_L = Res("Lrow")
        onesr = alloc(nc, stack, "onesr", [HPC, SEQ], F32)
        sel = alloc(nc, stack, "sel", [HPC, HPC + 1, 128], F32)
        r_sel = Res("sel")
        LcT = alloc(nc, stack, "LcT", [128, HPC, NKC], F32)
        r_LcT = Res("LcT")
        MASK = alloc(nc, stack, "MASK", [128, 4, 512], F32)
        r_MASK = Res("MASK")
        FQ = [alloc(nc, stack, f"FQ{i}", [128, 5, 512], F32) for i in range(2)]
        r_FQ = [Res(f"FQ{i}") for i in range(2)]
        TMP = [alloc(nc, stack, f"atmp{i}", [128, 512], F32) for i in range(3)]
        r_TMP = [Res(f"atmp{i}") for i in range(3)]
        PP = [alloc(nc, stack, f"P{i}", [128, 512], BF16) for i in range(3)]
        r_PP = [Res(f"P{i}") for i in range(3)]
        RINV = alloc(nc, stack, "rinv", [128, 512], F32)
        r_RINV = Res("rinv")
        OST = [alloc(nc, stack, f"ost{i}", [128, 512], BF16) for i in range(2)]
        r_OST = [Res(f"ost{i}") for i in range(2)]

        S.dma("sp", out=Lrow[:], in_=ll[:, :], writes=[r_L])
        S.dma("sp", out=sel[:], in_=sel_d[:, :, :], writes=[r_sel])
        S.dma("sp", out=MASK[:], in_=mask_d[:, :, :], writes=[r_MASK])
        S.op("dve", lambda: nc.vector.memset(onesr[:], 1.0), writes=[r_sel])
        S.op("dve", lambda: nc.vector.tensor_tensor_scan(
            out=Lrow[:, :], data0=onesr[:, :], data1=Lrow[:, :], initial=0.0, op0=ALU.mult, op1=ALU.add),
            reads=[r_L, r_sel], writes=[r_L])
        bank7, r_b7 = C.banks[7]
        for kc0 in range(0, NKC, 32):
            def mm(kc0=kc0):
                for kc in range(kc0, kc0 + 32):
                    ins = nc.tensor.matmul(bank7[:, (kc - kc0) * 4:(kc - kc0) * 4 + 4], lhsT=Lrow[:, kc * 128:(kc + 1) * 128],
                                           rhs=sel[:, HPC, 0:HPC], start=True, stop=True)
                return ins
            S.op("pe", mm, reads=[r_L, r_sel], writes=[r_b7])
            S.op("dve", lambda kc0=kc0: nc.vector.tensor_copy(
                out=LcT[:, :, kc0:kc0 + 32], in_=bank7[:, 0:128].rearrange("p (c h) -> p h c", h=HPC)),
                reads=[r_b7], writes=[r_LcT])

        ev = 0
        for h in range(HPC):
            hb = h % 2
            for a in range(0, SEQ, 2048):
                S.dma("sp", out=KT[hb][:, a:a + 2048], in_=kT[h, :, a:a + 2048], writes=[r_K[hb]])
                S.dma("sp", out=QT[hb][:, a:a + 2048], in_=qT[h, :, a:a + 2048], writes=[r_Q[hb]])
                S.dma("sp", out=VV[hb][:, a // 128:(a + 2048) // 128, :],
                      in_=vv[h, a:a + 2048, :].rearrange("(c p) d -> p c d", p=128), writes=[r_V[hb]])
            for qt in range(NQT):
                fq, r_fq = FQ[ev % 2], r_FQ[ev % 2]
                ob, r_ob = C.banks[3 + ev % 2]
                rb, r_rb = C.banks[5 + ev % 2]
                ost, r_ost = OST[ev % 2], r_OST[ev % 2]
                ev += 1
                q0 = qt * 512
                S.op("pe", lambda: nc.tensor.matmul(bank7[:, :], lhsT=sel[:, h, :], rhs=Lrow[:, q0:q0 + 512], start=True, stop=True),
                     reads=[r_L, r_sel], writes=[r_b7])
                S.op("dve", lambda fq=fq: nc.vector.tensor_copy(out=fq[:, 4, :], in_=bank7[:, :]), reads=[r_b7], writes=[r_fq])
                for i in range(4):
                    S.op("dve", lambda fq=fq, i=i: nc.vector.tensor_tensor(out=fq[:, i, :], in0=bank7[:, :], in1=MASK[:, i, :], op=ALU.add),
                         reads=[r_b7, r_MASK], writes=[r_fq])
                nkc = 4 * (qt + 1)

                def emit_S(kc):
                    sb, r_sb = C.banks[kc % 3]
                    S.op("pe", lambda: nc.tensor.matmul(sb[:, :], lhsT=KT[hb][:, kc * 128:(kc + 1) * 128], rhs=QT[hb][:, q0:q0 + 512],
                                                        start=True, stop=True), reads=[r_K[hb], r_Q[hb]], writes=[r_sb])
                emit_S(0)
                emit_S(1)
                for kc in range(nkc):
                    sb, r_sb = C.banks[kc % 3]
                    tmp, r_tmp = TMP[kc % 3], r_TMP[kc % 3]
                    pp, r_pp = PP[kc % 3], r_PP[kc % 3]
                    fi = (kc - 4 * qt) if kc >= 4 * qt else 4
                    S.op("dve", lambda: nc.vector.scalar_tensor_tensor(
                        out=tmp[:, :], in0=sb[:, :], scalar=ATT_SCALE, in1=fq[:, fi, :], op0=ALU.mult, op1=ALU.add),
                        reads=[r_sb, r_fq], writes=[r_tmp])
                    S.op("act", lambda: nc.scalar.activation(out=pp[:, :], in_=tmp[:, :], func=AF.Exp, bias=LcT[:, h, kc:kc + 1], scale=1.0),
                         reads=[r_tmp, r_LcT], writes=[r_pp])
                    if kc + 2 < nkc:
                        emit_S(kc + 2)

                    def mm():
                        nc.tensor.matmul(ob[:, :], lhsT=VV[hb][:, kc, :], rhs=pp[:, :], start=(kc == 0), stop=(kc == nkc - 1))
                        return nc.tensor.matmul(rb[:, :], lhsT=C.ones_b[:, :], rhs=pp[:, :], start=(kc == 0), stop=(kc == nkc - 1))
                    S.op("pe", mm, reads=[r_V[hb], r_pp, C.r_const], writes=[r_ob, r_rb])
                S.op("dve", lambda: nc.vector.reciprocal(out=RINV[:, :], in_=rb[:, :]), reads=[r_rb], writes=[r_RINV])
                S.op("dve", lambda: nc.vector.tensor_tensor(out=ost[:, :], in0=ob[:, :], in1=RINV[:, :], op=ALU.mult),
                     reads=[r_ob, r_RINV], writes=[r_ost])
                S.dma("sp", out=oT[h, :, q0:q0 + 512], in_=ost[:, :], reads=[r_ost], sem_res=r_ost, is_output=True)
        S.finish()
    return nc


def build_B2(nc):
    I = IO()
    I.xmid = nc.dram_tensor("xmid_in", [2, 128, KD, TOK], F32, kind="ExternalInput").ap()
    I.oT = nc.dram_tensor("oT_in", [2, NH, 128, TOK], BF16, kind="ExternalInput").ap()
    I.ada = nc.dram_tensor("ada_in", [128, NADA, KD], F32, kind="ExternalInput").ap()
    I.vecs = nc.dram_tensor("vecs", [128, NVEC, KD], F32, kind="ExternalInput").ap()
    I.w_o = nc.dram_tensor("w_o", [D, D], F32, kind="ExternalInput").ap()
    I.w_ffn_in = nc.dram_tensor("w_ffn_in", [D, 2 * FF], F32, kind="ExternalInput").ap()
    I.w_ffn_out = nc.dram_tensor("w_ffn_out", [FF, D], F32, kind="ExternalInput").ap()
    out = nc.dram_tensor("out", [2, 128, KD, TOK], F32, kind="ExternalOutput").ap()
    with ExitStack() as stack:
        S = Sched(nc, stack)
        C = Ctx()
        setup_common(nc, S, stack, C)
        setup_stream_bufs(nc, S, stack, C)
        setup_act_bufs(nc, S, stack, C)
        S.dma("sp", out=C.vecs[:], in_=I.vecs[:, :, :], writes=[C.r_vecs])
        S.dma("sp", out=C.ADA[:], in_=I.ada[:, :, :], writes=[C.r_ada])
        emit_derived(nc, S, C)
        blocks = []
        for c in range(2):
            blocks += wo_blocks(I.w_o)
            blocks += ffn_blocks(I.w_ffn_in, I.w_ffn_out)
        ws = WStream(S, C.slots, blocks, NSLOT - 2)
        fg = C.vecs[:, V_FING, :]
        for c in range(2):
            for k0 in range(0, KD, 4):
                S.dma("sp", out=C.xT[:, k0:k0 + 4, 2:TW], in_=I.xmid[c, :, k0:k0 + 4, :],
                      writes=[C.r_x[k][t] for k in range(k0, k0 + 4) for t in (1, 2)])
            octr = [0]
            for g in range(KD // GRP):
                gb, r_gb = C.gT[g % 2], C.r_g[g % 2]
                S.dma("sp", out=gb[:, :, 2:TW], in_=I.oT[c, g * GRP:(g + 1) * GRP, :, :].rearrange("h p t -> p h t"),
                      writes=[r_gb[1], r_gb[2]])
                for o in range(8):
                    slot, r_slot = ws.next()
                    emit_out_block(nc, S, C, slot, r_slot, o, gb, r_gb, TILES_M, C.GATE[:, 4, :], octr)
            emit_rstd(nc, S, C, TILES_M)
            emit_modulate(nc, S, C, TILES_M, C.AMOD[:, 5, :], C.ADA[:, 15, :])
            emit_ffn(nc, S, C, ws, TILES_M, C.GATE[:, 5, :])
            emit_rstd(nc, S, C, TILES_M)
            for (t0, t1) in TILES_M:
                ti = tile_idx(t0)
                for k in range(KD):
                    tmp, r_tmp = next_tmp(C)
                    S.op("dve", lambda k=k, tmp=tmp, t0=t0, t1=t1: nc.vector.scalar_tensor_tensor(
                        out=tmp[:, :], in0=C.xT[:, k, t0:t1], scalar=fg[:, k:k + 1], in1=C.rstd[:, t0:t1],
                        op0=ALU.mult, op1=ALU.mult), reads=[C.r_x[k][ti], C.r_rstd[ti], C.r_vecs], writes=[r_tmp])
                    S.dma("sp", out=out[c, :, k, t0 - 2:t1 - 2], in_=tmp[:, :], reads=[r_tmp], sem_res=r_tmp, is_output=True)
        S.finish()
    return nc


_CAP = None


def kernel(**inp):
    inp = {k: np.asarray(v) for k, v in inp.items()}
    H = host_common(inp)
    resA = run_prog(build_A, host_in_A(inp, H))
    bf = ml_dtypes.bfloat16
    Q = np.zeros((2, NH, 128, SEQ), bf)
    Kf = np.zeros((2, NH, 128, SEQ), bf)
    V = np.zeros((2, SEQ, D), bf)
    L = np.zeros((2, NH, SEQ), np.float32)
    for core in range(8):
        b = core // 4
        for c in range(2):
            t0 = chunk_of(core, c) * TOK
            Q[b, :, :, t0:t0 + TOK] = np.asarray(resA[core]["q_o"])[c]
            Kf[b, :, :, t0:t0 + TOK] = np.asarray(resA[core]["k_o"])[c]
            V[b, t0:t0 + TOK, :] = np.asarray(resA[core]["v_o"])[c]
            L[b, :, t0:t0 + TOK] = np.asarray(resA[core]["l_o"])[c]
    mask = np.zeros((128, 4, 512), np.float32)
    pidx = np.arange(128)[:, None]
    tidx = np.arange(512)[None, :]
    for i in range(4):
        mask[:, i, :] = np.where(128 * i + pidx <= tidx, 0.0, -30000.0)
    sel = np.zeros((HPC, HPC + 1, 128), np.float32)
    for h in range(HPC):
        sel[h, h, :] = -1.0
        sel[h, HPC, h] = 1.0
    mapsB1 = []
    for core in range(8):
        b, j = core // 4, core % 4
        hs = slice(HPC * j, HPC * (j + 1))
        mapsB1.append({
            "qT": np.ascontiguousarray(Q[b, hs]), "kT": np.ascontiguousarray(Kf[b, hs]),
            "vv": np.ascontiguousarray(V[b].reshape(SEQ, NH, 128)[:, hs, :].transpose(1, 0, 2)),
            "ll": np.ascontiguousarray(L[b, hs]), "mask": mask, "sel": sel})
    resB1 = run_prog(build_B1, mapsB1)
    if _CAP is not None:
        _CAP.update(resA=resA, Q=Q, K=Kf, V=V, L=L, resB1=resB1)
    OT = np.zeros((2, NH, 128, SEQ), bf)
    for core in range(8):
        b, j = core // 4, core % 4
        OT[b, HPC * j:HPC * (j + 1)] = np.asarray(resB1[core]["oT"])
    wfo = np.asarray(inp["w_ffn_out"], np.float32)
    mapsB2 = []
    for core in range(8):
        b = core // 4
        oin = np.stack([OT[b, :, :, chunk_of(core, c) * TOK:(chunk_of(core, c) + 1) * TOK] for c in range(2)])
        mapsB2.append({
            "xmid_in": np.asarray(resA[core]["xmid"]), "oT_in": np.ascontiguousarray(oin),
            "ada_in": np.asarray(resA[core]["ada_o"]), "vecs": H["vecs"],
            "w_o": np.asarray(inp["w_o"], np.float32)[0], "w_ffn_in": H["w_ffn_in4"][3],
            "w_ffn_out": np.ascontiguousarray(wfo[1, 1])})
    resB2 = run_prog(build_B2, mapsB2)
    outp = np.zeros((2, SEQ, D), np.float32)
    for core in range(8):
        b = core // 4
        for c in range(2):
            t0 = chunk_of(core, c) * TOK
            o = np.asarray(resB2[core]["out"])[c]
            outp[b, t0:t0 + TOK, :] = o.transpose(2, 1, 0).reshape(TOK, D)
    return outp


GROWS = 6144
GROUPS = [[0, 1, 2, 3], [4, 5, 6, 7]]
PROWS = 512
NPIECE = GROWS // PROWS
IX_Q, IX_K, IX_V, IX_O, IX_N = 0, 16, 32, 160, 192


DEBUG_FUSED = False


def build_fused(nc):
    I = IO()
    I.xin = nc.dram_tensor("xin", [2, 128, KD, TW], F32, kind="ExternalInput").ap()
    I.flag = nc.dram_tensor("flag", [128, 2], F32, kind="ExternalInput").ap()
    I.condT = nc.dram_tensor("condT", [128, KD], F32, kind="ExternalInput").ap()
    I.vecs = nc.dram_tensor("vecs", [128, NVEC, KD], F32, kind="ExternalInput").ap()
    I.w_ada_part = nc.dram_tensor("w_ada_part", [5, D, D], F32, kind="ExternalInput").ap()
    I.bada_part = nc.dram_tensor("bada_part", [128, 5, KD], F32, kind="ExternalInput").ap()
    I.w_ffn_in = nc.dram_tensor("w_ffn_in", [4, D, 2 * FF], F32, kind="ExternalInput").ap()
    I.w_ffn_out = nc.dram_tensor("w_ffn_out", [4, FF, D], F32, kind="ExternalInput").ap()
    I.w_ci = nc.dram_tensor("w_ci", [D, 3 * D], F32, kind="ExternalInput").ap()
    I.w_co = nc.dram_tensor("w_co", [D, D], F32, kind="ExternalInput").ap()
    I.w_q = nc.dram_tensor("w_q", [D, D], F32, kind="ExternalInput").ap()
    I.w_kv = nc.dram_tensor("w_kv", [D, 2 * D], F32, kind="ExternalInput").ap()
    I.w_o = nc.dram_tensor("w_o", [D, D], F32, kind="ExternalInput").ap()
    I.w_f = nc.dram_tensor("w_f", [128, KD, NH], F32, kind="ExternalInput").ap()
    I.b_f = nc.dram_tensor("b_f", [NH, 1], F32, kind="ExternalInput").ap()
    I.mask = nc.dram_tensor("mask", [128, 4, 512], F32, kind="ExternalInput").ap()
    I.sel = nc.dram_tensor("sel", [NH, HPC + 1, 128], F32, kind="ExternalInput").ap()
    I.idx = nc.dram_tensor("idx", [128, IX_N], mybir.dt.int32, kind="ExternalInput").ap()
    out = nc.dram_tensor("out", [2, 128, KD, TOK], F32, kind="ExternalOutput").ap()
    G = [nc.dram_tensor(f"G{c}", [GROWS, TOK], BF16).ap() for c in range(2)]
    GO = [nc.dram_tensor(f"GO{c}", [NPIECE * 4 * PROWS, TOK], BF16).ap() for c in range(2)]
    GL = [nc.dram_tensor(f"GL{c}", [NH, TOK], F32).ap() for c in range(2)]
    GOL = [nc.dram_tensor(f"GOL{c}", [4 * NH, TOK], F32).ap() for c in range(2)]
    G2 = nc.dram_tensor("G2", [8 * 512, TOK], BF16).ap()
    GO2 = nc.dram_tensor("GO2", [8 * 4 * PROWS, TOK], BF16).ap()
    xmid_d = nc.dram_tensor("xmid_d", [2, 128, KD, TOK], F32).ap()
    AGi = nc.dram_tensor("AGi", [128, 5 * KD], F32).ap()
    AGo = nc.dram_tensor("AGo", [4 * 128, 5 * KD], F32).ap()
    if DEBUG_FUSED:
        dbgL = nc.dram_tensor("dbgL", [NH, SEQ], F32, kind="ExternalOutput").ap()
        dbgO = nc.dram_tensor("dbgO", [HPC, 128, SEQ], BF16, kind="ExternalOutput").ap()
        dbgA = nc.dram_tensor("dbgA", [128, NADA, KD], F32, kind="ExternalOutput").ap()

    O = IO()
    O.qdst = lambda c, h, a, b: G[c][h * 128:(h + 1) * 128, a:b]
    O.kdst = lambda c, h, a, b: G[c][2048 + h * 128:2048 + (h + 1) * 128, a:b]
    O.ldst = lambda c, a, b: GL[c][:, a:b]
    def vstore(S, c, ts, vb, st, r_st):
        for hh in range(2):
            h = 2 * vb + hh
            S.dma("sp", out=G[c][4096 + h * 128:4096 + (h + 1) * 128, ts * 128:(ts + 1) * 128], in_=st[:, hh * 128:(hh + 1) * 128],
                  reads=[r_st], sem_res=r_st, is_output=True)
    O.vstore = vstore

    def gather_pieces(S, c, lo, hi, r):
        for i in range(lo, hi):
            S.defer_collective("AllGather", GROUPS, G[c][i * PROWS:(i + 1) * PROWS, :], GO[c][i * 4 * PROWS:(i + 1) * 4 * PROWS, :], r)

    with ExitStack() as top:
        S = Sched(nc, top)
        C = Ctx()
        setup_common(nc, S, top, C)
        setup_small(nc, S, top, C)
        IDX = alloc(nc, top, "IDX", [128, IX_N], mybir.dt.int32)
        r_IDX = Res("IDX")
        S.dma("sp", out=IDX[:], in_=I.idx[:, :], writes=[r_IDX])
        r_cc = [Res("cc0"), Res("cc1"), Res("ccl0"), Res("ccl1"), Res("cc2")]
        O.after = lambda S, stage, c: gather_pieces(S, c, {"q": 0, "k": 4, "v": 8}[stage], {"q": 4, "k": 8, "v": 12}[stage], r_cc[c])

        with ExitStack() as stack:
            setup_stream_bufs(nc, S, stack, C)
            setup_act_bufs(nc, S, stack, C)
            C.cond = alloc(nc, stack, "cond", [128, KD], F32)
            C.r_cond = Res("cond")
            C.flag = alloc(nc, stack, "flag", [128, 2], F32)
            C.U = [alloc(nc, stack, f"U{i}", [128, TW], F32) for i in range(2)]
            C.r_U = [Res(f"U{i}") for i in range(2)]
            C.BG = [alloc(nc, stack, f"BG{i}", [128, TOK], F32) for i in range(2)]
            C.r_BG = [Res(f"BG{i}") for i in range(2)]
            C.c1 = alloc(nc, stack, "c1", [128, TOK], F32)
            C.r_c1 = Res("c1")
            C.cgh = alloc(nc, stack, "cgh", [128, 2], F32)
            C.r_cgh = Res("cgh")
            C.qst = [alloc(nc, stack, f"qst{i}", [128, 512], BF16) for i in range(3)]
            C.r_qst = [Res(f"qst{i}") for i in range(3)]
            C.vst = [alloc(nc, stack, f"vst{i}", [128, 256], BF16) for i in range(3)]
            C.r_vst = [Res(f"vst{i}") for i in range(3)]
            C.est = [alloc(nc, stack, "est0", [NH, 512], F32)] * 2
            C.r_est = [Res("est0")] * 2
            C.lst = [alloc(nc, stack, f"lst{i}", [NH, 512], F32) for i in range(2)]
            C.r_lst = [Res(f"lst{i}") for i in range(2)]
            C.wff = alloc(nc, stack, "wff", [128, KD, NH], F32)
            C.wfb = alloc(nc, stack, "wfb", [128, KD, NH], BF16)
            C.negb = alloc(nc, stack, "negb", [NH, 1], F32)
            C.r_wf = Res("wf")
            C.r_negb = Res("negb")
            C.proj_ctr = [0]
            C.r_xst = [Res(f"xst{i}") for i in range(4)]
            S.dma("sp", out=C.flag[:], in_=I.flag[:, :], writes=[C.r_vecs])
            S.dma("sp", out=C.wff[:], in_=I.w_f[:, :, :], writes=[C.r_wf])
            S.dma("sp", out=C.negb[:], in_=I.b_f[:, :], writes=[C.r_negb])
            S.op("dve", lambda: nc.vector.tensor_copy(out=C.wfb[:], in_=C.wff[:]), reads=[C.r_wf], writes=[C.r_wf])
            S.op("dve", lambda: nc.vector.tensor_scalar(out=C.negb[:], in0=C.negb[:], scalar1=-1.0, scalar2=None, op0=ALU.mult),
                 reads=[C.r_negb], writes=[C.r_negb])
            C.apart = alloc(nc, stack, "apart", [128, 5, KD], F32)
            C.bpart = alloc(nc, stack, "bpart", [128, 5, KD], F32)
            emit_ada_sharded(nc, S, C, I, AGi, AGo)
            if DEBUG_FUSED:
                S.dma("sp", out=dbgA[:, :, :], in_=C.ADA[:], reads=[C.r_ada], sem_res=Res("dbgA"), is_output=True)
            emit_derived(nc, S, C)
            blocks = []
            for c in range(2):
                blocks += ffn_blocks(I.w_ffn_in[0], I.w_ffn_out[0])
                blocks += conv_blocks(I.w_ci, I.w_co)
                blocks += ffn_blocks(I.w_ffn_in[1], I.w_ffn_out[1])
                blocks += ffn_blocks(I.w_ffn_in[2], I.w_ffn_out[2])
                blocks += proj_blocks(I.w_q, 0, D)
                blocks += proj_blocks(I.w_kv, 0, D)
                blocks += proj_blocks(I.w_kv, D, D)
            ws = WStream(S, C.slots, blocks, NSLOT - 1)
            for c in range(2):
                for k0 in range(0, KD, 4):
                    S.dma("sp", out=C.xT[:, k0:k0 + 4, :], in_=I.xin[c, :, k0:k0 + 4, :],
                          writes=[C.r_x[k][t] for k in range(k0, k0 + 4) for t in range(3)])
                emit_rstd(nc, S, C, TILES_H)
                emit_modulate(nc, S, C, TILES_H, C.AMOD[:, 0, :], C.ADA[:, 0, :])
                emit_ffn(nc, S, C, ws, TILES_H, C.GATE[:, 0, :])
                emit_rstd(nc, S, C, TILES_H)
                emit_modulate(nc, S, C, TILES_H, C.AMOD[:, 1, :], C.ADA[:, 3, :])
                ws.depth = NSLOT - 2
                emit_conv(nc, S, C, ws, c, C.GATE[:, 1, :], C.flag)
                ws.depth = NSLOT - 1
                emit_rstd(nc, S, C, TILES_M)
                emit_modulate(nc, S, C, TILES_M, C.AMOD[:, 2, :], C.ADA[:, 6, :])
                emit_ffn(nc, S, C, ws, TILES_M, C.GATE[:, 2, :])
                emit_rstd(nc, S, C, TILES_M)
                emit_modulate(nc, S, C, TILES_M, C.AMOD[:, 3, :], C.ADA[:, 9, :])
                emit_ffn(nc, S, C, ws, TILES_M, C.GATE[:, 3, :])
                emit_proj(nc, S, C, ws, c, O)
                for k0 in range(0, KD, 4):
                    S.dma("sp", out=xmid_d[c, :, k0:k0 + 4, :], in_=C.xT[:, k0:k0 + 4, 2:TW],
                          reads=[C.r_x[k][t] for k in range(k0, k0 + 4) for t in (1, 2)], sem_res=C.r_xst[k0 // 4], is_output=True)
                S.defer_collective("AllGather", GROUPS, GL[c], GOL[c], r_cc[2 + c], front=True)
            S.flush_collectives()
            S.drain()

        with ExitStack() as stack:
            KT = [alloc(nc, stack, f"KT{i}", [128, SEQ], BF16) for i in range(2)]
            QT = [alloc(nc, stack, f"QT{i}", [128, SEQ], BF16) for i in range(2)]
            VV = [alloc(nc, stack, f"VV{i}", [128, NKC, 128], BF16) for i in range(2)]
            r_K = [[Res(f"KT{i}")] * 8 for i in range(2)]
            r_Q = [[Res(f"QT{i}")] * 8 for i in range(2)]
            r_V = [[Res(f"VV{i}")] * 8 for i in range(2)]
            Lrow = alloc(nc, stack, "Lrow", [NH, SEQ], F32)
            r_L = Res("Lrow")
            onesr = alloc(nc, stack, "onesr", [NH, 1024], F32)
            sel = alloc(nc, stack, "sel", [NH, HPC + 1, 128], F32)
            r_sel = Res("sel")
            LcT = alloc(nc, stack, "LcT", [128, HPC, NKC], F32)
            r_LcT = Res("LcT")
            MASK = alloc(nc, stack, "MASK", [128, 4, 512], F32)
            r_MASK = Res("MASK")
            FQ = [alloc(nc, stack, f"FQ{i}", [128, 5, 512], F32) for i in range(2)]
            r_FQ = [Res(f"FQ{i}") for i in range(2)]
            TMP = [alloc(nc, stack, f"atmp{i}", [128, 512], F32) for i in range(4)]
            r_TMP = [Res(f"atmp{i}") for i in range(4)]
            PP = [alloc(nc, stack, f"P{i}", [128, 512], BF16) for i in range(4)]
            r_PP = [Res(f"P{i}") for i in range(4)]
            RINV = alloc(nc, stack, "rinv", [128, 512], F32)
            r_RINV = Res("rinv")
            OST = [alloc(nc, stack, f"ost{i}", [128, 512], BF16) for i in range(2)]
            r_OST = [Res(f"ost{i}") for i in range(2)]
            GOv = [GO[c].rearrange("r (a d) -> (r a) d", d=128) for c in range(2)]
            chunk = lambda r, c: (r if c == 0 else 7 - r)

            S.dma("sp", out=sel[:], in_=I.sel[:, :, :], writes=[r_sel])
            S.dma("sp", out=MASK[:], in_=I.mask[:, :, :], writes=[r_MASK])
            S.op("dve", lambda: nc.vector.memset(onesr[:], 1.0), writes=[r_sel])
            for c in range(2):
                for r in range(4):
                    tb = chunk(r, c) * TOK
                    S.dma("sp", out=Lrow[:, tb:tb + TOK], in_=GOL[c][r * NH:(r + 1) * NH, :], reads=[r_cc[2 + c]], writes=[r_L])
            for sg in range(SEQ // 1024):
                a, b = sg * 1024, (sg + 1) * 1024
                init = 0.0 if sg == 0 else Lrow[:, a - 1:a]
                S.fence("dve")
                S.op("dve", lambda a=a, b=b, init=init: nc.vector.tensor_tensor_scan(
                    out=Lrow[:, a:b], data0=onesr[:, :], data1=Lrow[:, a:b], initial=init, op0=ALU.mult, op1=ALU.add),
                    reads=[r_L, r_sel], writes=[r_L])
            if DEBUG_FUSED:
                S.dma("sp", out=dbgL[:, :], in_=Lrow[:, :], reads=[r_L], sem_res=Res("dbgL"), is_output=True)
            bank7, r_b7 = C.banks[7]
            for kc0 in range(0, NKC, 32):
                def mm(kc0=kc0):
                    for kc in range(kc0, kc0 + 32):
                        ins = nc.tensor.matmul(bank7[:, (kc - kc0) * 4:(kc - kc0) * 4 + 4], lhsT=Lrow[:, kc * 128:(kc + 1) * 128],
                                               rhs=sel[:, HPC, 0:HPC], start=True, stop=True)
                    return ins
                S.op("pe", mm, reads=[r_L, r_sel], writes=[r_b7])
                S.op("dve", lambda kc0=kc0: nc.vector.tensor_copy(
                    out=LcT[:, :, kc0:kc0 + 32], in_=bank7[:, 0:128].rearrange("p (c h) -> p h c", h=HPC)),
                    reads=[r_b7], writes=[r_LcT])

            ev = 0
            for h in range(HPC):
                hb = h % 2
                for blk in range(8):
                    c, r = (0, blk) if blk < 4 else (1, 7 - blk)
                    tb = blk * TOK
                    col = h * 4 + r
                    S.gather(KT[hb][:, tb:tb + TOK], GO[c][:, :], IDX[:, IX_K + col:IX_K + col + 1],
                             reads=[r_cc[c], r_IDX], writes=[r_K[hb][blk]])
                    S.gather(QT[hb][:, tb:tb + TOK], GO[c][:, :], IDX[:, IX_Q + col:IX_Q + col + 1],
                             reads=[r_cc[c], r_IDX], writes=[r_Q[hb][blk]])
                    S.gather(VV[hb][:, blk * 8:(blk + 1) * 8, :].rearrange("p a d -> p (a d)"), GO[c][:, :],
                             IDX[:, IX_V + col:IX_V + col + 1], reads=[r_cc[c], r_IDX], writes=[r_V[hb][blk]])
                for qt in range(NQT):
                    fq, r_fq = FQ[ev % 2], r_FQ[ev % 2]
                    ob, r_ob = C.banks[4 + ev % 2]
                    rb, r_rb = C.banks[6 + ev % 2]
                    ost, r_ost = OST[ev % 2], r_OST[ev % 2]
                    ev += 1
                    q0 = qt * 512
                    fqb, r_fqb = C.banks[3]
                    S.op("pe", lambda: nc.tensor.matmul(fqb[:, :], lhsT=sel[:, h, :], rhs=Lrow[:, q0:q0 + 512], start=True, stop=True),
                         reads=[r_L, r_sel], writes=[r_fqb])
                    S.op("dve", lambda fq=fq: nc.vector.tensor_copy(out=fq[:, 4, :], in_=fqb[:, :]), reads=[r_fqb], writes=[r_fq])
                    for i in range(4):
                        S.op("dve", lambda fq=fq, i=i: nc.vector.tensor_tensor(out=fq[:, i, :], in0=fqb[:, :], in1=MASK[:, i, :], op=ALU.add),
                             reads=[r_fqb, r_MASK], writes=[r_fq])
                    nkc = 4 * (qt + 1)

                    def emit_S(kc):
                        sb, r_sb = C.banks[kc % 4]
                        S.op("pe", lambda: nc.tensor.matmul(sb[:, :], lhsT=KT[hb][:, kc * 128:(kc + 1) * 128], rhs=QT[hb][:, q0:q0 + 512],
                                                            start=True, stop=True), reads=[r_K[hb][kc // 8], r_Q[hb][qt // 2]], writes=[r_sb])
                    emit_S(0)
                    emit_S(1)
                    emit_S(2)
                    for kc in range(nkc):
                        sb, r_sb = C.banks[kc % 4]
                        tmp, r_tmp = TMP[kc % 4], r_TMP[kc % 4]
                        pp, r_pp = PP[kc % 4], r_PP[kc % 4]
                        fi = (kc - 4 * qt) if kc >= 4 * qt else 4
                        S.op("dve", lambda: nc.vector.scalar_tensor_tensor(
                            out=tmp[:, :], in0=sb[:, :], scalar=ATT_SCALE, in1=fq[:, fi, :], op0=ALU.mult, op1=ALU.add),
                            reads=[r_sb, r_fq], writes=[r_tmp])
                        S.op("act", lambda: nc.scalar.activation(out=pp[:, :], in_=tmp[:, :], func=AF.Exp, bias=LcT[:, h, kc:kc + 1], scale=1.0),
                             reads=[r_tmp, r_LcT], writes=[r_pp])
                        if kc + 3 < nkc:
                            emit_S(kc + 3)

                        def mm():
                            nc.tensor.matmul(ob[:, :], lhsT=VV[hb][:, kc, :], rhs=pp[:, :], start=(kc == 0), stop=(kc == nkc - 1))
                            return nc.tensor.matmul(rb[:, :], lhsT=C.ones_b[:, :], rhs=pp[:, :], start=(kc == 0), stop=(kc == nkc - 1))
                        S.op("pe", mm, reads=[r_V[hb][kc // 8], r_pp, C.r_const], writes=[r_ob, r_rb])
                    S.op("dve", lambda: nc.vector.reciprocal(out=RINV[:, :], in_=rb[:, :]), reads=[r_rb], writes=[r_RINV])
                    S.op("dve", lambda: nc.vector.tensor_tensor(out=ost[:, :], in0=ob[:, :], in1=RINV[:, :], op=ALU.mult),
                         reads=[r_ob, r_RINV], writes=[r_ost])
                    ci = qt // 2
                    g2r = (h * 2 + ci // 4) * PROWS + (ci % 4) * 128
                    S.dma("sp", out=G2[g2r:g2r + 128, (qt % 2) * 512:(qt % 2 + 1) * 512], in_=ost[:, :],
                          reads=[r_ost], sem_res=r_ost, is_output=True)
                    if qt % 8 == 7:
                        i = h * 2 + qt // 8
                        S.collective("AllGather", GROUPS, G2[i * PROWS:(i + 1) * PROWS, :], GO2[i * 4 * PROWS:(i + 1) * 4 * PROWS, :], r_cc[4])
                    if DEBUG_FUSED:
                        S.dma("sp", out=dbgO[h, :, q0:q0 + 512], in_=ost[:, :], reads=[r_ost], sem_res=r_ost, is_output=True)
            S.drain()

        with ExitStack() as stack:
            setup_stream_bufs(nc, S, stack, C, nslot=8)
            setup_act_bufs(nc, S, stack, C)
            blocks = []
            for c in range(2):
                blocks += wo_blocks(I.w_o)
                blocks += ffn_blocks(I.w_ffn_in[3], I.w_ffn_out[3])
            ws = WStream(S, C.slots, blocks, 7)
            fg = C.vecs[:, V_FING, :]
            for c in range(2):
                for k0 in range(0, KD, 4):
                    S.dma("sp", out=C.xT[:, k0:k0 + 4, 2:TW], in_=xmid_d[c, :, k0:k0 + 4, :],
                          writes=[C.r_x[k][t] for k in range(k0, k0 + 4) for t in (1, 2)])
                octr = [0]
                for g in range(KD // GRP):
                    gb, r_gb = C.gT[g % 2], C.r_g[g % 2]
                    for hi in range(GRP):
                        oc = IX_O + c * NH + g * GRP + hi
                        S.gather(gb[:, hi, 2:TW], GO2[:, :], IDX[:, oc:oc + 1], reads=[r_cc[4], r_IDX], writes=[r_gb[1], r_gb[2]])
                    for o in range(8):
                        slot, r_slot = ws.next()
                        emit_out_block(nc, S, C, slot, r_slot, o, gb, r_gb, TILES_M, C.GATE[:, 4, :], octr, nbanks=4)
                emit_rstd(nc, S, C, TILES_M)
                emit_modulate(nc, S, C, TILES_M, C.AMOD[:, 5, :], C.ADA[:, 15, :])
                emit_ffn(nc, S, C, ws, TILES_M, C.GATE[:, 5, :])
                emit_rstd(nc, S, C, TILES_M)
                for (t0, t1) in TILES_M:
                    ti = tile_idx(t0)
                    for k in range(KD):
                        tmp, r_tmp = next_tmp(C)
                        S.op("dve", lambda k=k, tmp=tmp, t0=t0, t1=t1: nc.vector.scalar_tensor_tensor(
                            out=tmp[:, :], in0=C.xT[:, k, t0:t1], scalar=fg[:, k:k + 1], in1=C.rstd[:, t0:t1],
                            op0=ALU.mult, op1=ALU.mult), reads=[C.r_x[k][ti], C.r_rstd[ti], C.r_vecs], writes=[r_tmp])
                        S.dma("sp", out=out[c, :, k, t0 - 2:t1 - 2], in_=tmp[:, :], reads=[r_tmp], sem_res=r_tmp, is_output=True)
            S.finish()
    return nc


def make_idx(core):
    j = core % 4
    p = np.arange(128, dtype=np.int64)
    idx = np.zeros((128, IX_N), np.int64)
    for hl in range(HPC):
        hg = HPC * j + hl
        for r in range(4):
            col = hl * 4 + r
            base = r * PROWS + (hg % 4) * 128
            idx[:, IX_Q + col] = (hg // 4) * 4 * PROWS + base + p
            idx[:, IX_K + col] = (4 + hg // 4) * 4 * PROWS + base + p
            idx[:, IX_V + col] = (8 + hg // 4) * 4 * PROWS + base + p
    for c in range(2):
        ci = chunk_of(core, c)
        for hg in range(NH):
            idx[:, IX_O + c * NH + hg] = ((hg % HPC) * 2 + ci // 4) * 4 * PROWS + (hg // HPC) * PROWS + (ci % 4) * 128 + p
    return idx.astype(np.int32)


def host_in_fused(inp, H):
    maps = host_in_A(inp, H)
    wfo = np.asarray(inp["w_ffn_out"], np.float32)
    w_ffn_in = np.stack(H["w_ffn_in4"])
    w_ffn_out = np.ascontiguousarray(np.stack([wfo[0, 0], wfo[0, 1], wfo[1, 0], wfo[1, 1]]))
    w_o = np.asarray(inp["w_o"], np.float32)[0]
    mask = np.zeros((128, 4, 512), np.float32)
    pidx = np.arange(128)[:, None]
    tidx = np.arange(512)[None, :]
    for i in range(4):
        mask[:, i, :] = np.where(128 * i + pidx <= tidx, 0.0, -30000.0)
    p = np.arange(128, dtype=np.int64)
    w_ada = np.asarray(inp["w_ada"], np.float32)
    w_ada_kv = np.asarray(inp["w_ada_kv"], np.float32)
    ada_parts = []
    for j in range(4):
        ws_, bs_ = [], []
        for s_ in range(5 * j, 5 * j + 5):
            if s_ < 18:
                l, col = s_ // 9, (s_ % 9) * D
                ws_.append(w_ada[l][:, col:col + D])
                bs_.append(H["vecs"][:, V_BADA + s_, :])
            else:
                col = (s_ - 18) * D
                ws_.append(w_ada_kv[:, col:col + D])
                bs_.append(H["vecs"][:, V_BKV + (s_ - 18), :])
        ada_parts.append((np.ascontiguousarray(np.stack(ws_)), np.ascontiguousarray(np.stack(bs_, axis=1))))
    for core in range(8):
        j = core % 4
        m = maps[core]
        del m["w_ada"], m["w_ada_kv"]
        m["w_ada_part"], m["bada_part"] = ada_parts[j]
        m["w_ffn_in"] = w_ffn_in
        m["w_ffn_out"] = w_ffn_out
        m["w_o"] = w_o
        m["mask"] = mask
        sel = np.zeros((NH, HPC + 1, 128), np.float32)
        for hl in range(HPC):
            sel[HPC * j + hl, hl, :] = -1.0
            sel[HPC * j + hl, HPC, hl] = 1.0
        m["sel"] = sel
        m["idx"] = make_idx(core)
    return maps


def kernel_unfused(**inp):
    return _kernel_unfused(**inp)


_kernel_unfused = kernel


def kernel(**inp):
    inp = {k: np.asarray(v) for k, v in inp.items()}
    H = host_common(inp)
    res = run_prog(build_fused, host_in_fused(inp, H))
    outp = np.zeros((2, SEQ, D), np.float32)
    for core in range(8):
        b = core // 4
        for c in range(2):
            t0 = chunk_of(core, c) * TOK
            o = np.asarray(res[core]["out"])[c]
            outp[b, t0:t0 + TOK, :] = o.transpose(2, 1, 0).reshape(TOK, D)
    return outp
```

```python
import numpy as np
from contextlib import ExitStack
import ml_dtypes
import concourse.bass as bass
import concourse.mybir as mybir
from concourse.bass_utils import run_bass_kernel_spmd

F32 = mybir.dt.float32
BF16 = mybir.dt.bfloat16
AF = mybir.ActivationFunctionType
ALU = mybir.AluOpType

D = 2048
KD = 16
FF = 5632
KF = 44
NH = 16
SEQ = 8192
TOK = 1024
HALO = 2
TW = TOK + HALO
EPS = 1e-6
GRP = 4
NSLOT = 5
SLOT_ELEMS = 4096

TILES_M = [(2, 514), (514, 1026)]
TILES_H = [(0, 2), (2, 514), (514, 1026)]

V_NORMG = 0
V_KVG = 6
V_FING = 7
V_CONVW = 8
V_CONVB = 11
V_BADA = 12
V_BKV = 30
NVEC = 32
NADA = 20


class Res:
    __slots__ = ("name", "last_w", "readers", "dsem", "dcnt")

    def __init__(self, name):
        self.name = name
        self.last_w = None
        self.readers = []
        self.dsem = None
        self.dcnt = 0


class Sched:
    def __init__(self, nc, stack):
        self.nc = nc
        self.stack = stack
        self.eng = {"pe": nc.tensor, "act": nc.scalar, "dve": nc.vector, "pool": nc.gpsimd, "sp": nc.sync}
        self.esem = {}
        self.ecnt = {}
        for e in self.eng:
            self.esem[e] = stack.enter_context(nc.semaphore("prog_" + e))
            self.ecnt[e] = 0
        self.seen = {}
        self.sem_owner = {id(self.esem[e]): e for e in self.eng}
        self.nsem = 0
        self.out_tokens = []
        self.pending = []
        self.pump_ctr = 0

    def new_sem(self, name):
        self.nsem += 1
        return self.stack.enter_context(self.nc.semaphore(f"d{self.nsem}_{name}"))

    def _wait(self, eng, tok):
        sem, val = tok
        if self.sem_owner.get(id(sem)) == eng:
            return
        key = (eng, id(sem))
        if self.seen.get(key, 0) >= val:
            return
        self.seen[key] = val
        self.eng[eng].wait_ge(sem, val)

    def _deps(self, eng, reads, writes):
        for r in reads:
            if r.last_w is not None:
                self._wait(eng, r.last_w)
        for w in writes:
            if w.last_w is not None:
                self._wait(eng, w.last_w)
            for t in w.readers:
                self._wait(eng, t)

    def _commit(self, tok, reads, writes):
        for r in reads:
            sid = id(tok[0])
            r.readers = [t for t in r.readers if id(t[0]) != sid]
            r.readers.append(tok)
        for w in writes:
            w.last_w = tok
            w.readers = []

    def op(self, eng, fn, reads=(), writes=()):
        self._deps(eng, reads, writes)
        ins = fn()
        self.ecnt[eng] += 1
        ins.then_inc(self.esem[eng], 1)
        tok = (self.esem[eng], self.ecnt[eng])
        self._commit(tok, reads, writes)
        return tok

    def dma(self, eng, out, in_, reads=(), writes=(), sem_res=None, is_output=False):
        self._deps(eng, reads, writes)
        r = sem_res if sem_res is not None else (writes[0] if writes else reads[0])
        if r.dsem is None:
            r.dsem = self.new_sem(r.name)
        ins = self.eng[eng].dma_start(out=out, in_=in_)
        r.dcnt += 16
        ins.then_inc(r.dsem, 16)
        tok = (r.dsem, r.dcnt)
        self._commit(tok, reads, writes)
        if is_output:
            self.out_tokens.append(tok)
        return tok

    def collective(self, kind, groups, src, dst, sem_res, ntok=None):
        for t in (self.out_tokens if ntok is None else self.out_tokens[:ntok]):
            self._wait("pool", t)
        if sem_res.dsem is None:
            sem_res.dsem = self.new_sem(sem_res.name)
        ins = self.nc.gpsimd.collective_compute(kind, ALU.bypass, replica_groups=groups, ins=[src.opt()], outs=[dst.opt()])
        sem_res.dcnt += 1
        ins.then_inc(sem_res.dsem)
        tok = (sem_res.dsem, sem_res.dcnt)
        sem_res.last_w = tok
        sem_res.readers = []
        return tok

    def defer_collective(self, *args, front=False):
        item = args + (len(self.out_tokens),)
        if front:
            self.pending.insert(0, item)
        else:
            self.pending.append(item)

    def pump(self, every=4):
        self.pump_ctr += 1
        if self.pending and self.pump_ctr % every == 0:
            self.collective(*self.pending.pop(0))

    def flush_collectives(self):
        while self.pending:
            self.collective(*self.pending.pop(0))

    def gather(self, out, in_, idx_ap, reads=(), writes=()):
        self._deps("pool", reads, writes)
        r = writes[0]
        if r.dsem is None:
            r.dsem = self.new_sem(r.name)
        ins = self.nc.gpsimd.indirect_dma_start(out=out, out_offset=None, in_=in_,
                                                in_offset=bass.IndirectOffsetOnAxis(ap=idx_ap, axis=0))
        r.dcnt += 16
        ins.then_inc(r.dsem, 16)
        tok = (r.dsem, r.dcnt)
        self._commit(tok, reads, writes)
        return tok

    def drain(self):
        for e in self.eng:
            for t in self.out_tokens:
                self._wait(e, t)
        self.out_tokens = []
        self.barrier()

    def fence(self, eng):
        if self.ecnt[eng] > 0:
            self.eng[eng].wait_ge(self.esem[eng], self.ecnt[eng])

    def barrier(self):
        toks = [(self.esem[e], self.ecnt[e]) for e in self.eng if self.ecnt[e] > 0]
        for e in self.eng:
            for t in toks:
                self._wait(e, t)

    def finish(self):
        for t in self.out_tokens:
            self._wait("sp", t)
        self.out_tokens = []


class WStream:
    def __init__(self, S, slots, blocks, depth):
        self.S = S
        self.slots = slots
        self.blocks = blocks
        self.depth = depth
        self.issued = 0
        self.taken = 0

    def _view(self, tile, shape):
        n = int(np.prod(shape[1:]))
        view = tile[:, 0:n]
        if len(shape) == 3:
            view = view.rearrange("p (a b) -> p a b", a=shape[1])
        elif len(shape) == 4:
            view = view.rearrange("p (a b c) -> p a b c", a=shape[1], b=shape[2])
        return view

    def _issue(self):
        i = self.issued
        ap, dshape, vshape = self.blocks[i]
        tile, res = self.slots[i % len(self.slots)]
        self.S.dma("pool", out=self._view(tile, dshape), in_=ap, writes=[res])
        self.issued += 1

    def next(self):
        while self.issued < min(len(self.blocks), self.taken + self.depth + 1):
            self._issue()
        self.S.pump()
        i = self.taken
        self.taken += 1
        tile, res = self.slots[i % len(self.slots)]
        ap, dshape, vshape = self.blocks[i]
        return self._view(tile, vshape), res


class Ctx:
    pass


_ALLOC_N = [0]


def alloc(nc, stack, name, shape, dt):
    _ALLOC_N[0] += 1
    return stack.enter_context(nc.sbuf_tensor(f"s{_ALLOC_N[0]}_" + name, list(shape), dt))


def grouped_order(n_in, n_groups, n_out):
    order = []
    per = n_in // n_groups
    for f in range(per):
        order.append(("in", 0, f))
    for g in range(n_groups):
        if g + 1 < n_groups:
            for f in range(per):
                order.append(("in", g + 1, f))
        for o in range(n_out):
            order.append(("out", g, o))
    return order


def ffn_blocks(w_in, w_out):
    win = w_in.rearrange("(k p) (f c) -> p k f c", p=128, c=256)
    wout = w_out.rearrange("(f p) d -> p f d", p=128)
    blocks = []
    for kind, g, i in grouped_order(KF, KF // GRP, 8):
        if kind == "in":
            f = g * GRP + i
            blocks.append((win[:, :, f, :], (128, KD, 256), (128, KD, 2, 128)))
        else:
            blocks.append((wout[:, g * GRP:(g + 1) * GRP, i * 256:(i + 1) * 256], (128, GRP, 256), (128, GRP, 256)))
    return blocks


def conv_blocks(w_ci, w_co):
    wci = w_ci.rearrange("(k p) (f c) -> p k f c", p=128, c=384)
    wco = w_co.rearrange("(i p) d -> p i d", p=128)
    blocks = []
    for kind, g, i in grouped_order(KD, KD // GRP, 8):
        if kind == "in":
            f = g * GRP + i
            blocks.append((wci[:, :, f, 0:256], (128, KD, 256), (128, KD, 2, 128)))
            blocks.append((wci[:, :, f, 256:384], (128, KD, 128), (128, KD, 1, 128)))
        else:
            blocks.append((wco[:, g * GRP:(g + 1) * GRP, i * 256:(i + 1) * 256], (128, GRP, 256), (128, GRP, 256)))
    return blocks


def proj_blocks(w, col0, ncols):
    wv = w.rearrange("(k p) d -> p k d", p=128)
    return [(wv[:, :, col0 + i * 256: col0 + (i + 1) * 256], (128, KD, 256), (128, KD, 256)) for i in range(ncols // 256)]


def wo_blocks(w_o):
    wv = w_o.rearrange("(i p) d -> p i d", p=128)
    blocks = []
    for g in range(KD // GRP):
        for o in range(8):
            blocks.append((wv[:, g * GRP:(g + 1) * GRP, o * 256:(o + 1) * 256], (128, GRP, 256), (128, GRP, 256)))
    return blocks


def setup_common(nc, S, stack, C):
    C.banks = []
    for i in range(8):
        t = stack.enter_context(nc.psum_tensor(f"bank{i}", [128, 512], F32))
        C.banks.append((t, Res(f"bank{i}")))
    C.ones_f = alloc(nc, stack, "ones_f", [128, 128], F32)
    C.ones_b = alloc(nc, stack, "ones_b", [128, 128], BF16)
    C.r_const = Res("const")
    S.op("pool", lambda: nc.gpsimd.memset(C.ones_f[:], 1.0), writes=[C.r_const])
    S.op("pool", lambda: nc.gpsimd.memset(C.ones_b[:], 1.0), writes=[C.r_const])


def setup_stream_bufs(nc, S, stack, C):
    C.slots = []
    for i in range(NSLOT):
        t = alloc(nc, stack, f"wslot{i}", [128, SLOT_ELEMS], BF16)
        C.slots.append((t, Res(f"wslot{i}")))


def setup_act_bufs(nc, S, stack, C):
    C.xT = alloc(nc, stack, "xT", [128, KD, TW], F32)
    C.hT = alloc(nc, stack, "hT", [128, KD, TW], BF16)
    C.gT = [alloc(nc, stack, f"gT{i}", [128, GRP, TW], BF16) for i in range(2)]
    C.r_x = [[Res(f"x{k}_{t}") for t in range(3)] for k in range(KD)]
    C.r_h = [[Res(f"h{k}_{t}") for t in range(3)] for k in range(KD)]
    C.r_g = [[Res(f"g{i}_{t}") for t in range(3)] for i in range(2)]
    C.sq = [alloc(nc, stack, f"sq{i}", [128, 512], F32) for i in range(2)]
    C.r_sq = [Res(f"sq{i}") for i in range(2)]
    C.acc = alloc(nc, stack, "acc", [128, 512], F32)
    C.r_acc = Res("acc")
    C.sd = C.sq[0]
    C.r_sd = C.r_sq[0]
    C.rstd = alloc(nc, stack, "rstd", [128, TW], F32)
    C.r_rstd = [Res(f"rstd{t}") for t in range(3)]
    C.tmp = [alloc(nc, stack, f"tmp{i}", [128, 512], F32) for i in range(3)]
    C.r_tmp = [Res(f"tmp{i}") for i in range(3)]
    C.tmp_i = 0
    if not hasattr(C, "ADA"):
        setup_small(nc, S, stack, C)


def setup_small(nc, S, stack, C):
    C.ADA = alloc(nc, stack, "ADA", [128, NADA, KD], F32)
    C.r_ada = Res("ADA")
    C.AMOD = alloc(nc, stack, "AMOD", [128, 8, KD], F32)
    C.GATE = alloc(nc, stack, "GATE", [128, 6, KD], F32)
    C.vecs = alloc(nc, stack, "vecs", [128, NVEC, KD], F32)
    C.r_vecs = Res("vecs")
    C.r_mod = Res("mod")


def tile_idx(t0):
    return {0: 0, 2: 1, 514: 2}[t0]


def next_tmp(C):
    i = C.tmp_i % len(C.tmp)
    C.tmp_i += 1
    return C.tmp[i], C.r_tmp[i]


def emit_derived(nc, S, C):
    rd = [C.r_ada, C.r_vecs]
    for l in range(2):
        for sub in range(3):
            n = l * 3 + sub
            sc = C.ADA[:, l * 9 + sub * 3 + 1, :]
            gt = C.ADA[:, l * 9 + sub * 3 + 2, :]
            S.op("dve", lambda n=n, sc=sc: nc.vector.scalar_tensor_tensor(
                out=C.AMOD[:, n, :], in0=sc, scalar=1.0, in1=C.vecs[:, V_NORMG + n, :],
                op0=ALU.add, op1=ALU.mult), reads=rd, writes=[C.r_mod])
            S.op("dve", lambda n=n, gt=gt, sub=sub: nc.vector.tensor_scalar(
                out=C.GATE[:, n, :], in0=gt, scalar1=(1.0 if sub == 1 else 0.5), scalar2=None,
                op0=ALU.mult), reads=rd, writes=[C.r_mod])
    S.op("dve", lambda: nc.vector.scalar_tensor_tensor(
        out=C.AMOD[:, 6, :], in0=C.ADA[:, 19, :], scalar=1.0, in1=C.vecs[:, V_KVG, :],
        op0=ALU.add, op1=ALU.mult), reads=rd, writes=[C.r_mod])


def emit_rstd(nc, S, C, tiles):
    bank, r_bank = C.banks[7]
    for (t0, t1) in tiles:
        n = t1 - t0
        ti = tile_idx(t0)
        for k in range(KD):
            if k == 0:
                S.op("act", lambda: nc.scalar.activation(out=C.acc[:, :n], in_=C.xT[:, 0, t0:t1], func=AF.Square),
                     reads=[C.r_x[0][ti]], writes=[C.r_acc])
            else:
                sq, r_sq = C.sq[k % 2], C.r_sq[k % 2]
                S.op("act", lambda k=k, sq=sq: nc.scalar.activation(out=sq[:, :n], in_=C.xT[:, k, t0:t1], func=AF.Square),
                     reads=[C.r_x[k][ti]], writes=[r_sq])
                S.op("dve", lambda sq=sq: nc.vector.tensor_tensor(out=C.acc[:, :n], in0=C.acc[:, :n], in1=sq[:, :n], op=ALU.add),
                     reads=[r_sq, C.r_acc], writes=[C.r_acc])
        S.op("pe", lambda: nc.tensor.matmul(bank[:, :n], lhsT=C.ones_f[:], rhs=C.acc[:, :n], start=True, stop=True),
             reads=[C.r_acc, C.r_const], writes=[r_bank])
        S.op("act", lambda: nc.scalar.activation(out=C.sd[:, :n], in_=bank[:, :n], func=AF.Sqrt, scale=1.0 / D, bias=EPS),
             reads=[r_bank], writes=[C.r_sd])
        S.op("dve", lambda: nc.vector.reciprocal(out=C.rstd[:, t0:t1], in_=C.sd[:, :n]),
             reads=[C.r_sd], writes=[C.r_rstd[ti]])


def emit_modulate(nc, S, C, tiles, a_ap, sh_ap):
    for (t0, t1) in tiles:
        n = t1 - t0
        ti = tile_idx(t0)
        for k in range(KD):
            tmp, r_tmp = next_tmp(C)
            S.op("dve", lambda k=k, tmp=tmp: nc.vector.scalar_tensor_tensor(
                out=tmp[:, :n], in0=C.xT[:, k, t0:t1], scalar=a_ap[:, k:k + 1], in1=C.rstd[:, t0:t1],
                op0=ALU.mult, op1=ALU.mult), reads=[C.r_x[k][ti], C.r_rstd[ti], C.r_mod], writes=[r_tmp])
            S.op("act", lambda k=k, tmp=tmp: nc.scalar.activation(
                out=C.hT[:, k, t0:t1], in_=tmp[:, :n], func=AF.Identity, bias=sh_ap[:, k:k + 1], scale=1.0),
                reads=[r_tmp, C.r_mod], writes=[C.r_h[k][ti]])


def emit_out_block(nc, S, C, slot, r_slot, o, src, r_src, tiles, gate_ap, bank_ctr, nbanks=2):
    for dd in range(2):
        d = 2 * o + dd
        for (t0, t1) in tiles:
            n = t1 - t0
            ti = tile_idx(t0)
            bank, r_bank = C.banks[4 + (bank_ctr[0] % nbanks)]
            bank_ctr[0] += 1

            def mm(bank=bank, dd=dd, t0=t0, t1=t1, n=n):
                for fi in range(GRP):
                    ins = nc.tensor.matmul(bank[:, :n], lhsT=slot[:, fi, dd * 128:(dd + 1) * 128],
                                           rhs=src[:, fi, t0:t1], start=(fi == 0), stop=(fi == GRP - 1))
                return ins
            S.op("pe", mm, reads=[r_slot, r_src[ti]], writes=[r_bank])
            S.op("dve", lambda bank=bank, d=d, t0=t0, t1=t1, n=n: nc.vector.scalar_tensor_tensor(
                out=C.xT[:, d, t0:t1], in0=bank[:, :n], scalar=gate_ap[:, d:d + 1], in1=C.xT[:, d, t0:t1],
                op0=ALU.mult, op1=ALU.add), reads=[r_bank, C.r_x[d][ti], C.r_mod], writes=[C.r_x[d][ti]])


def emit_ffn(nc, S, C, ws, tiles, gate_ap):
    pair = [0]
    octr = [0]
    silu_i = [0]
    for kind, g, i in grouped_order(KF, KF // GRP, 8):
        slot, r_slot = ws.next()
        if kind == "in":
            gbuf = C.gT[g % 2]
            r_gb = C.r_g[g % 2]
            for (t0, t1) in tiles:
                n = t1 - t0
                ti = tile_idx(t0)
                pa, r_pa = C.banks[(pair[0] % 2) * 2]
                pb, r_pb = C.banks[(pair[0] % 2) * 2 + 1]
                pair[0] += 1

                def mm(pa=pa, pb=pb, t0=t0, t1=t1, n=n):
                    for k in range(KD):
                        nc.tensor.matmul(pa[:, :n], lhsT=slot[:, k, 0, :], rhs=C.hT[:, k, t0:t1],
                                         start=(k == 0), stop=(k == KD - 1))
                    for k in range(KD):
                        ins = nc.tensor.matmul(pb[:, :n], lhsT=slot[:, k, 1, :], rhs=C.hT[:, k, t0:t1],
                                               start=(k == 0), stop=(k == KD - 1))
                    return ins
                S.op("pe", mm, reads=[r_slot] + [C.r_h[k][ti] for k in range(KD)], writes=[r_pa, r_pb])
                tmp, r_tmp = next_tmp(C)
                S.op("act", lambda pa=pa, tmp=tmp, n=n: nc.scalar.activation(out=tmp[:, :n], in_=pa[:, :n], func=AF.Silu),
                     reads=[r_pa], writes=[r_tmp])
                S.op("dve", lambda pb=pb, tmp=tmp, n=n, t0=t0, t1=t1, gbuf=gbuf, i=i: nc.vector.tensor_tensor(
                    out=gbuf[:, i, t0:t1], in0=pb[:, :n], in1=tmp[:, :n], op=ALU.mult),
                    reads=[r_pb, r_tmp], writes=[r_gb[ti]])
        else:
            emit_out_block(nc, S, C, slot, r_slot, i, C.gT[g % 2], C.r_g[g % 2], tiles, gate_ap, octr, nbanks=4)


def emit_conv(nc, S, C, ws, c, gate_ap, flag):
    octr = [0]
    bank7, r_b7 = C.banks[7]
    bank6, r_b6 = C.banks[6]
    cw = lambda k: C.vecs[:, V_CONVW + k, :]
    cb = C.vecs[:, V_CONVB, :]
    for kind, g, i in grouped_order(KD, KD // GRP, 8):
        slot, r_slot = ws.next()
        if kind == "out":
            emit_out_block(nc, S, C, slot, r_slot, i, C.gT[g % 2], C.r_g[g % 2], TILES_M, gate_ap, octr)
            continue
        f = g * GRP + i
        slot2, r_slot2 = ws.next()
        U, r_U = C.U[f % 2], C.r_U[f % 2]
        BG, r_BG = C.BG[f % 2], C.r_BG[f % 2]
        rh = lambda ti: [C.r_h[k][ti] for k in range(KD)]

        def mmh():
            for k in range(KD):
                nc.tensor.matmul(bank7[:, 0:2], lhsT=slot[:, k, 1, :], rhs=C.hT[:, k, 0:2], start=(k == 0), stop=(k == KD - 1))
            for k in range(KD):
                ins = nc.tensor.matmul(bank7[:, 2:4], lhsT=slot2[:, k, 0, :], rhs=C.hT[:, k, 0:2], start=(k == 0), stop=(k == KD - 1))
            return ins
        S.op("pe", mmh, reads=[r_slot, r_slot2] + rh(0), writes=[r_b7])
        S.op("act", lambda: nc.scalar.activation(out=C.cgh[:, 0:2], in_=bank7[:, 0:2], func=AF.Identity),
             reads=[r_b7], writes=[C.r_cgh])
        S.op("dve", lambda U=U: nc.vector.scalar_tensor_tensor(
            out=U[:, 0:2], in0=bank7[:, 2:4], scalar=flag[:, c:c + 1], in1=C.cgh[:, 0:2],
            op0=ALU.mult, op1=ALU.mult), reads=[r_b7, C.r_cgh, C.r_vecs], writes=[r_U])
        for si, (t0, t1) in enumerate(TILES_M):
            ti = tile_idx(t0)
            pcg, r_pcg = C.banks[si * 2]
            pxv, r_pxv = C.banks[si * 2 + 1]

            def mm(pcg=pcg, pxv=pxv, t0=t0, t1=t1):
                for k in range(KD):
                    nc.tensor.matmul(bank6[:, :], lhsT=slot[:, k, 0, :], rhs=C.hT[:, k, t0:t1], start=(k == 0), stop=(k == KD - 1))
                for k in range(KD):
                    nc.tensor.matmul(pcg[:, :], lhsT=slot[:, k, 1, :], rhs=C.hT[:, k, t0:t1], start=(k == 0), stop=(k == KD - 1))
                for k in range(KD):
                    ins = nc.tensor.matmul(pxv[:, :], lhsT=slot2[:, k, 0, :], rhs=C.hT[:, k, t0:t1], start=(k == 0), stop=(k == KD - 1))
                return ins
            S.op("pe", mm, reads=[r_slot, r_slot2] + rh(ti), writes=[r_b6, r_pcg, r_pxv])
            S.op("act", lambda BG=BG, t0=t0, t1=t1: nc.scalar.activation(out=BG[:, t0 - 2:t1 - 2], in_=bank6[:, :], func=AF.Identity),
                 reads=[r_b6], writes=[r_BG])
            tmp, r_tmp = next_tmp(C)
            S.op("act", lambda tmp=tmp, pcg=pcg: nc.scalar.activation(out=tmp[:, :], in_=pcg[:, :], func=AF.Identity),
                 reads=[r_pcg], writes=[r_tmp])
            S.op("dve", lambda U=U, tmp=tmp, pxv=pxv, t0=t0, t1=t1: nc.vector.tensor_tensor(
                out=U[:, t0:t1], in0=pxv[:, :], in1=tmp[:, :], op=ALU.mult), reads=[r_pxv, r_tmp], writes=[r_U])
        S.op("dve", lambda U=U, f=f: nc.vector.tensor_scalar(
            out=C.c1[:, :], in0=U[:, 2:TW], scalar1=cw(2)[:, f:f + 1], scalar2=cb[:, f:f + 1], op0=ALU.mult, op1=ALU.add),
            reads=[r_U, C.r_vecs], writes=[C.r_c1])
        S.op("dve", lambda U=U, f=f: nc.vector.scalar_tensor_tensor(
            out=C.c1[:, :], in0=U[:, 1:TW - 1], scalar=cw(1)[:, f:f + 1], in1=C.c1[:, :], op0=ALU.mult, op1=ALU.add),
            reads=[r_U, C.r_c1], writes=[C.r_c1])
        S.op("dve", lambda U=U, f=f: nc.vector.scalar_tensor_tensor(
            out=C.c1[:, :], in0=U[:, 0:TW - 2], scalar=cw(0)[:, f:f + 1], in1=C.c1[:, :], op0=ALU.mult, op1=ALU.add),
            reads=[r_U, C.r_c1], writes=[C.r_c1])
        if getattr(C, "dbgc", None) is not None and f == 0:
            S.dma("sp", out=C.dbgc[0, :, :], in_=U[:, :], reads=[r_U], sem_res=Res("dU"), is_output=True)
            S.dma("sp", out=C.dbgc[1, :, 0:TOK], in_=BG[:, :], reads=[r_BG], sem_res=Res("dBG"), is_output=True)
            S.dma("sp", out=C.dbgc[2, :, 0:TOK], in_=C.c1[:, :], reads=[C.r_c1], sem_res=Res("dc1"), is_output=True)
        gb = C.gT[g % 2]
        S.op("dve", lambda gb=gb, BG=BG, i=i: nc.vector.tensor_tensor(
            out=gb[:, i, 2:TW], in0=C.c1[:, :], in1=BG[:, :], op=ALU.mult),
            reads=[C.r_c1, r_BG], writes=[C.r_g[g % 2][1], C.r_g[g % 2][2]])


def emit_featproj(nc, S, C, ws, nblk, dst_fn, stage_name):
    ctr = C.proj_ctr
    for hp in range(nblk):
        slot, r_slot = ws.next()
        for hh in range(2):
            h = 2 * hp + hh
            for (t0, t1) in TILES_M:
                ti = tile_idx(t0)
                bank, r_bank = C.banks[ctr[0] % 4]
                st, r_st = C.qst[ctr[0] % 3], C.r_qst[ctr[0] % 3]
                ctr[0] += 1

                def mm(bank=bank, hh=hh, t0=t0, t1=t1):
                    for k in range(KD):
                        ins = nc.tensor.matmul(bank[:, :], lhsT=slot[:, k, hh * 128:(hh + 1) * 128], rhs=C.hT[:, k, t0:t1],
                                               start=(k == 0), stop=(k == KD - 1))
                    return ins
                S.op("pe", mm, reads=[r_slot] + [C.r_h[k][ti] for k in range(KD)], writes=[r_bank])
                S.op("act", lambda bank=bank, st=st: nc.scalar.activation(out=st[:, :], in_=bank[:, :], func=AF.Identity),
                     reads=[r_bank], writes=[r_st])
                S.dma("sp", out=dst_fn(h, t0 - 2, t1 - 2), in_=st[:, :], reads=[r_st], sem_res=r_st, is_output=True)


def emit_proj(nc, S, C, ws, c, O):
    emit_rstd(nc, S, C, TILES_M)
    emit_modulate(nc, S, C, TILES_M, C.AMOD[:, 4, :], C.ADA[:, 12, :])
    emit_featproj(nc, S, C, ws, 8, lambda h, a, b: O.qdst(c, h, a, b), "q")
    if hasattr(O, "after"):
        O.after(S, "q", c)
    emit_modulate(nc, S, C, TILES_M, C.AMOD[:, 6, :], C.ADA[:, 18, :])
    emit_featproj(nc, S, C, ws, 8, lambda h, a, b: O.kdst(c, h, a, b), "k")
    if hasattr(O, "after"):
        O.after(S, "k", c)
    ctr = C.proj_ctr
    for vb in range(8):
        slot, r_slot = ws.next()
        for ts in range(8):
            ti = 1 if ts < 4 else 2
            c0 = 2 + ts * 128
            bank, r_bank = C.banks[ctr[0] % 4]
            st, r_st = C.vst[ctr[0] % 3], C.r_vst[ctr[0] % 3]
            ctr[0] += 1

            def mm(bank=bank, c0=c0):
                for k in range(KD):
                    ins = nc.tensor.matmul(bank[:, 0:256], lhsT=C.hT[:, k, c0:c0 + 128], rhs=slot[:, k, :],
                                           start=(k == 0), stop=(k == KD - 1))
                return ins
            S.op("pe", mm, reads=[r_slot] + [C.r_h[k][ti] for k in range(KD)], writes=[r_bank])
            S.op("act", lambda bank=bank, st=st: nc.scalar.activation(out=st[:, :], in_=bank[:, 0:256], func=AF.Identity),
                 reads=[r_bank], writes=[r_st])
            O.vstore(S, c, ts, vb, st, r_st)
    if hasattr(O, "after"):
        O.after(S, "v", c)
    for (t0, t1) in TILES_M:
        ti = tile_idx(t0)
        bank, r_bank = C.banks[ctr[0] % 4]
        es, r_es = C.est[ctr[0] % 2], C.r_est[ctr[0] % 2]
        ls, r_ls = C.lst[ctr[0] % 2], C.r_lst[ctr[0] % 2]
        ctr[0] += 1

        def mm(bank=bank, t0=t0, t1=t1):
            for k in range(KD):
                ins = nc.tensor.matmul(bank[0:16, :], lhsT=C.wfb[:, k, :], rhs=C.hT[:, k, t0:t1],
                                       start=(k == 0), stop=(k == KD - 1))
            return ins
        S.op("pe", mm, reads=[C.r_wf] + [C.r_h[k][ti] for k in range(KD)], writes=[r_bank])
        S.op("act", lambda bank=bank, es=es: nc.scalar.activation(out=es[:, :], in_=bank[0:16, :], func=AF.Exp,
                                                                  scale=-1.0, bias=C.negb[:, 0:1]),
             reads=[r_bank, C.r_negb], writes=[r_es])
        S.op("act", lambda es=es, ls=ls: nc.scalar.activation(out=ls[:, :], in_=es[:, :], func=AF.Ln, bias=1.0),
             reads=[r_es], writes=[r_ls])
        S.dma("sp", out=O.ldst(c, t0 - 2, t1 - 2), in_=ls[:, :], reads=[r_ls], sem_res=r_ls, is_output=True)


def emit_ada_sharded(nc, S, C, I, AGi, AGo):
    scratch = C.hT[:].rearrange("p k t -> p (k t)").bitcast(F32)
    wb = [scratch[:, i * 2048:(i + 1) * 2048] for i in range(3)]
    r_wb = [Res(f"adaw{i}") for i in range(3)]
    accA = scratch[:, 6144:8192]
    r_accA = Res("accA")
    bank6, r_b6 = C.banks[6]
    S.dma("sp", out=C.vecs[:], in_=I.vecs[:, :, :], writes=[C.r_vecs])
    S.dma("sp", out=C.cond[:], in_=I.condT[:, :], writes=[C.r_cond])
    r_bpart = Res("bpart")
    S.dma("sp", out=C.bpart[:], in_=I.bada_part[:, :, :], writes=[r_bpart])
    S.op("act", lambda: nc.scalar.activation(out=C.cond[:], in_=C.cond[:], func=AF.Silu), reads=[C.r_cond], writes=[C.r_cond])
    r_part = Res("adapart")
    n = 0
    for s in range(5):
        for k in range(KD):
            w, r_w = wb[n % 3], r_wb[n % 3]
            n += 1
            S.dma("sp" if n % 2 else "act", out=w, in_=I.w_ada_part[s, k * 128:(k + 1) * 128, :], writes=[r_w])
            if k == 0:
                S.op("dve", lambda w=w: nc.vector.tensor_scalar(out=accA, in0=w, scalar1=C.cond[:, 0:1], scalar2=None, op0=ALU.mult),
                     reads=[r_w, C.r_cond], writes=[r_accA])
            else:
                S.op("dve", lambda w=w, k=k: nc.vector.scalar_tensor_tensor(
                    out=accA, in0=w, scalar=C.cond[:, k:k + 1], in1=accA, op0=ALU.mult, op1=ALU.add),
                    reads=[r_w, C.r_cond, r_accA], writes=[r_accA])

        def mm():
            for dc in range(KD):
                ins = nc.tensor.matmul(bank6[:, dc:dc + 1], lhsT=accA[:, dc * 128:(dc + 1) * 128], rhs=C.ones_f[:, 0:1],
                                       start=True, stop=True)
            return ins
        S.op("pe", mm, reads=[r_accA, C.r_const], writes=[r_b6])
        S.op("dve", lambda s=s: nc.vector.tensor_tensor(out=C.apart[:, s, :], in0=bank6[:, 0:KD], in1=C.bpart[:, s, :], op=ALU.add),
             reads=[r_b6, r_bpart], writes=[r_part])
    S.dma("sp", out=AGi.rearrange("p (s k) -> p s k", s=5), in_=C.apart[:], reads=[r_part], sem_res=r_part, is_output=True)
    r_ag = Res("ccada")
    S.collective("AllGather", GROUPS, AGi, AGo, r_ag)
    for r in range(4):
        S.dma("sp", out=C.ADA[:, r * 5:(r + 1) * 5, :], in_=AGo[r * 128:(r + 1) * 128, :].rearrange("p (s k) -> p s k", s=5),
              reads=[r_ag], writes=[C.r_ada])
    S.barrier()


def emit_ada(nc, S, C, I):
    scratch = C.hT[:].rearrange("p k t -> p (k t)").bitcast(F32)
    wb = [scratch[:, i * 2048:(i + 1) * 2048] for i in range(3)]
    r_wb = [Res(f"adaw{i}") for i in range(3)]
    accA = scratch[:, 6144:8192]
    r_accA = Res("accA")
    bank6, r_b6 = C.banks[6]
    S.dma("sp", out=C.vecs[:], in_=I.vecs[:, :, :], writes=[C.r_vecs])
    S.dma("sp", out=C.cond[:], in_=I.condT[:, :], writes=[C.r_cond])
    S.op("act", lambda: nc.scalar.activation(out=C.cond[:], in_=C.cond[:], func=AF.Silu), reads=[C.r_cond], writes=[C.r_cond])
    n = 0
    for s in range(NADA):
        if s < 18:
            l, col = s // 9, (s % 9) * 2048
            src = lambda k: I.w_ada[l, k * 128:(k + 1) * 128, col:col + 2048]
            bvec = C.vecs[:, V_BADA + s, :]
        else:
            col = (s - 18) * 2048
            src = lambda k: I.w_ada_kv[k * 128:(k + 1) * 128, col:col + 2048]
            bvec = C.vecs[:, V_BKV + (s - 18), :]
        for k in range(KD):
            w, r_w = wb[n % 3], r_wb[n % 3]
            n += 1
            S.dma("sp", out=w, in_=src(k), writes=[r_w])
            if k == 0:
                S.op("dve", lambda w=w: nc.vector.tensor_scalar(out=accA, in0=w, scalar1=C.cond[:, 0:1], scalar2=None, op0=ALU.mult),
                     reads=[r_w, C.r_cond], writes=[r_accA])
            else:
                S.op("dve", lambda w=w, k=k: nc.vector.scalar_tensor_tensor(
                    out=accA, in0=w, scalar=C.cond[:, k:k + 1], in1=accA, op0=ALU.mult, op1=ALU.add),
                    reads=[r_w, C.r_cond, r_accA], writes=[r_accA])

        def mm():
            for dc in range(KD):
                ins = nc.tensor.matmul(bank6[:, dc:dc + 1], lhsT=accA[:, dc * 128:(dc + 1) * 128], rhs=C.ones_f[:, 0:1],
                                       start=True, stop=True)
            return ins
        S.op("pe", mm, reads=[r_accA, C.r_const], writes=[r_b6])
        S.op("dve", lambda s=s, bvec=bvec: nc.vector.tensor_tensor(out=C.ADA[:, s, :], in0=bank6[:, 0:KD], in1=bvec, op=ALU.add),
             reads=[r_b6, C.r_vecs], writes=[C.r_ada])
    S.barrier()


class IO:
    pass


def build_A(nc, debug=False):
    I = IO()
    I.xin = nc.dram_tensor("xin", [2, 128, KD, TW], F32, kind="ExternalInput").ap()
    I.flag = nc.dram_tensor("flag", [128, 2], F32, kind="ExternalInput").ap()
    I.condT = nc.dram_tensor("condT", [128, KD], F32, kind="ExternalInput").ap()
    I.vecs = nc.dram_tensor("vecs", [128, NVEC, KD], F32, kind="ExternalInput").ap()
    I.w_ada = nc.dram_tensor("w_ada", [2, D, 9 * D], F32, kind="ExternalInput").ap()
    I.w_ada_kv = nc.dram_tensor("w_ada_kv", [D, 2 * D], F32, kind="ExternalInput").ap()
    I.w_ffn_in = nc.dram_tensor("w_ffn_in", [3, D, 2 * FF], F32, kind="ExternalInput").ap()
    I.w_ffn_out = nc.dram_tensor("w_ffn_out", [3, FF, D], F32, kind="ExternalInput").ap()
    I.w_ci = nc.dram_tensor("w_ci", [D, 3 * D], F32, kind="ExternalInput").ap()
    I.w_co = nc.dram_tensor("w_co", [D, D], F32, kind="ExternalInput").ap()
    I.w_q = nc.dram_tensor("w_q", [D, D], F32, kind="ExternalInput").ap()
    I.w_kv = nc.dram_tensor("w_kv", [D, 2 * D], F32, kind="ExternalInput").ap()
    I.w_f = nc.dram_tensor("w_f", [128, KD, NH], F32, kind="ExternalInput").ap()
    I.b_f = nc.dram_tensor("b_f", [NH, 1], F32, kind="ExternalInput").ap()
    O = IO()
    O.xmid = nc.dram_tensor("xmid", [2, 128, KD, TOK], F32, kind="ExternalOutput").ap()
    O.q = nc.dram_tensor("q_o", [2, NH, 128, TOK], BF16, kind="ExternalOutput").ap()
    O.k = nc.dram_tensor("k_o", [2, NH, 128, TOK], BF16, kind="ExternalOutput").ap()
    O.v = nc.dram_tensor("v_o", [2, TOK, D], BF16, kind="ExternalOutput").ap()
    O.l = nc.dram_tensor("l_o", [2, NH, TOK], F32, kind="ExternalOutput").ap()
    O.ada = nc.dram_tensor("ada_o", [128, NADA, KD], F32, kind="ExternalOutput").ap()
    O.qdst = lambda c, h, a, b: O.q[c, h, :, a:b]
    O.kdst = lambda c, h, a, b: O.k[c, h, :, a:b]
    O.ldst = lambda c, a, b: O.l[c, :, a:b]
    O.vstore = lambda S, c, ts, vb, st, r_st: S.dma(
        "sp", out=O.v[c, ts * 128:(ts + 1) * 128, vb * 256:(vb + 1) * 256], in_=st[:, :],
        reads=[r_st], sem_res=r_st, is_output=True)
    if debug:
        O.dbg = nc.dram_tensor("dbg", [3, 128, KD, TW], F32, kind="ExternalOutput").ap()
        O.dbgc = nc.dram_tensor("dbgc", [3, 128, TW], F32, kind="ExternalOutput").ap()
        O.dbgh = nc.dram_tensor("dbgh", [128, KD, TW], BF16, kind="ExternalOutput").ap()

    def dbg_store(S, C, i):
        if debug:
            S.dma("sp", out=O.dbg[i, :, :, :], in_=C.xT[:], reads=[C.r_x[k][t] for k in range(KD) for t in range(3)],
                  sem_res=Res(f"dbg{i}"), is_output=True)

    with ExitStack() as stack:
        S = Sched(nc, stack)
        C = Ctx()
        setup_common(nc, S, stack, C)
        setup_stream_bufs(nc, S, stack, C)
        setup_act_bufs(nc, S, stack, C)
        C.cond = alloc(nc, stack, "cond", [128, KD], F32)
        C.r_cond = Res("cond")
        C.flag = alloc(nc, stack, "flag", [128, 2], F32)
        C.U = [alloc(nc, stack, f"U{i}", [128, TW], F32) for i in range(2)]
        C.r_U = [Res(f"U{i}") for i in range(2)]
        C.BG = [alloc(nc, stack, f"BG{i}", [128, TOK], F32) for i in range(2)]
        C.r_BG = [Res(f"BG{i}") for i in range(2)]
        C.c1 = alloc(nc, stack, "c1", [128, TOK], F32)
        C.r_c1 = Res("c1")
        C.cgh = alloc(nc, stack, "cgh", [128, 2], F32)
        C.r_cgh = Res("cgh")
        C.qst = [alloc(nc, stack, f"qst{i}", [128, 512], BF16) for i in range(3)]
        C.r_qst = [Res(f"qst{i}") for i in range(3)]
        C.vst = [alloc(nc, stack, f"vst{i}", [128, 256], BF16) for i in range(3)]
        C.r_vst = [Res(f"vst{i}") for i in range(3)]
        C.est = [alloc(nc, stack, "est0", [NH, 512], F32)] * 2
        C.r_est = [Res("est0")] * 2
        C.lst = [alloc(nc, stack, f"lst{i}", [NH, 512], F32) for i in range(2)]
        C.r_lst = [Res(f"lst{i}") for i in range(2)]
        C.wff = alloc(nc, stack, "wff", [128, KD, NH], F32)
        C.wfb = alloc(nc, stack, "wfb", [128, KD, NH], BF16)
        C.negb = alloc(nc, stack, "negb", [NH, 1], F32)
        C.r_wf = Res("wf")
        C.r_negb = Res("negb")
        C.proj_ctr = [0]
        C.r_xst = [Res(f"xst{i}") for i in range(4)]

        S.dma("sp", out=C.flag[:], in_=I.flag[:, :], writes=[C.r_vecs])
        S.dma("sp", out=C.wff[:], in_=I.w_f[:, :, :], writes=[C.r_wf])
        S.dma("sp", out=C.negb[:], in_=I.b_f[:, :], writes=[C.r_negb])
        S.op("dve", lambda: nc.vector.tensor_copy(out=C.wfb[:], in_=C.wff[:]), reads=[C.r_wf], writes=[C.r_wf])
        S.op("dve", lambda: nc.vector.tensor_scalar(out=C.negb[:], in0=C.negb[:], scalar1=-1.0, scalar2=None, op0=ALU.mult),
             reads=[C.r_negb], writes=[C.r_negb])
        emit_ada(nc, S, C, I)
        emit_derived(nc, S, C)
        S.dma("sp", out=O.ada[:, :, :], in_=C.ADA[:], reads=[C.r_ada], sem_res=Res("adao"), is_output=True)

        blocks = []
        for c in range(2):
            blocks += ffn_blocks(I.w_ffn_in[0], I.w_ffn_out[0])
            blocks += conv_blocks(I.w_ci, I.w_co)
            blocks += ffn_blocks(I.w_ffn_in[1], I.w_ffn_out[1])
            blocks += ffn_blocks(I.w_ffn_in[2], I.w_ffn_out[2])
            blocks += proj_blocks(I.w_q, 0, D)
            blocks += proj_blocks(I.w_kv, 0, D)
            blocks += proj_blocks(I.w_kv, D, D)
        ws = WStream(S, C.slots, blocks, NSLOT - 2)

        for c in range(1 if debug else 2):
            for k0 in range(0, KD, 4):
                S.dma("sp", out=C.xT[:, k0:k0 + 4, :], in_=I.xin[c, :, k0:k0 + 4, :],
                      writes=[C.r_x[k][t] for k in range(k0, k0 + 4) for t in range(3)])
            emit_rstd(nc, S, C, TILES_H)
            emit_modulate(nc, S, C, TILES_H, C.AMOD[:, 0, :], C.ADA[:, 0, :])
            emit_ffn(nc, S, C, ws, TILES_H, C.GATE[:, 0, :])
            if c == 0:
                dbg_store(S, C, 0)
            emit_rstd(nc, S, C, TILES_H)
            emit_modulate(nc, S, C, TILES_H, C.AMOD[:, 1, :], C.ADA[:, 3, :])
            if debug and c == 0:
                C.dbgc = O.dbgc
                S.dma("sp", out=O.dbgh[:, :, :], in_=C.hT[:], reads=[C.r_h[k][t] for k in range(KD) for t in range(3)],
                      sem_res=Res("dbgh"), is_output=True)
            emit_conv(nc, S, C, ws, c, C.GATE[:, 1, :], C.flag)
            C.dbgc = None
            if c == 0:
                dbg_store(S, C, 1)
            emit_rstd(nc, S, C, TILES_M)
            emit_modulate(nc, S, C, TILES_M, C.AMOD[:, 2, :], C.ADA[:, 6, :])
            emit_ffn(nc, S, C, ws, TILES_M, C.GATE[:, 2, :])
            if c == 0:
                dbg_store(S, C, 2)
            emit_rstd(nc, S, C, TILES_M)
            emit_modulate(nc, S, C, TILES_M, C.AMOD[:, 3, :], C.ADA[:, 9, :])
            emit_ffn(nc, S, C, ws, TILES_M, C.GATE[:, 3, :])
            emit_proj(nc, S, C, ws, c, O)
            for k0 in range(0, KD, 4):
                S.dma("sp", out=O.xmid[c, :, k0:k0 + 4, :], in_=C.xT[:, k0:k0 + 4, 2:TW],
                      reads=[C.r_x[k][t] for k in range(k0, k0 + 4) for t in (1, 2)], sem_res=C.r_xst[k0 // 4], is_output=True)
        S.finish()
    return nc


def _pk(v):
    return np.ascontiguousarray(np.asarray(v, np.float32).reshape(KD, 128).T)


def chunk_of(core, c):
    j = core % 4
    return j if c == 0 else 7 - j


def host_common(inp):
    H = {}
    vec = np.zeros((128, NVEC, KD), np.float32)
    for l in range(2):
        for sub in range(3):
            vec[:, V_NORMG + l * 3 + sub, :] = _pk(inp["norm_g"][l, sub])
    vec[:, V_KVG, :] = _pk(inp["kv_norm_g"])
    vec[:, V_FING, :] = _pk(inp["final_g"])
    for k in range(3):
        vec[:, V_CONVW + k, :] = _pk(inp["conv_w"][0, k])
    vec[:, V_CONVB, :] = _pk(inp["conv_b"][0])
    for l in range(2):
        for s in range(9):
            vec[:, V_BADA + l * 9 + s, :] = _pk(inp["b_ada"][l, s * D:(s + 1) * D])
    for s in range(2):
        vec[:, V_BKV + s, :] = _pk(inp["b_ada_kv"][s * D:(s + 1) * D])
    H["vecs"] = vec

    def perm_in(w):
        return np.ascontiguousarray(w.reshape(D, 2, KF, 128).transpose(0, 2, 1, 3).reshape(D, 2 * FF))
    wfi = np.asarray(inp["w_ffn_in"], np.float32)
    H["w_ffn_in4"] = [perm_in(wfi[0, 0]), perm_in(wfi[0, 1]), perm_in(wfi[1, 0]), perm_in(wfi[1, 1])]
    wci = np.asarray(inp["w_conv_in"], np.float32)[0]
    H["w_ci"] = np.ascontiguousarray(wci.reshape(D, 3, KD, 128).transpose(0, 2, 1, 3).reshape(D, 3 * D))
    wkvf = np.asarray(inp["w_kvf"], np.float32)
    H["w_kv"] = np.ascontiguousarray(wkvf[:, :2 * D])
    H["w_f"] = np.ascontiguousarray(wkvf[:, 2 * D:].reshape(KD, 128, NH).transpose(1, 0, 2))
    H["b_f"] = np.ascontiguousarray(np.asarray(inp["b_fgate"], np.float32).reshape(NH, 1))
    return H


def host_in_A(inp, H):
    x = np.asarray(inp["x"], np.float32)
    cvec = np.asarray(inp["c"], np.float32)
    wfo = np.asarray(inp["w_ffn_out"], np.float32)
    shared = {
        "vecs": H["vecs"],
        "w_ada": np.asarray(inp["w_ada"], np.float32),
        "w_ada_kv": np.asarray(inp["w_ada_kv"], np.float32),
        "w_ffn_in": np.stack(H["w_ffn_in4"][:3]),
        "w_ffn_out": np.ascontiguousarray(np.stack([wfo[0, 0], wfo[0, 1], wfo[1, 0]])),
        "w_ci": H["w_ci"],
        "w_co": np.asarray(inp["w_conv_out"], np.float32)[0],
        "w_q": np.asarray(inp["w_q"], np.float32)[0],
        "w_kv": H["w_kv"],
        "w_f": H["w_f"],
        "b_f": H["b_f"],
    }
    maps = []
    for core in range(8):
        b = core // 4
        xin = np.zeros((2, 128, KD, TW), np.float32)
        flag = np.zeros((128, 2), np.float32)
        for c in range(2):
            ci = chunk_of(core, c)
            lo = ci * TOK - HALO
            seg = np.zeros((TW, D), np.float32)
            if lo < 0:
                seg[HALO:] = x[b, 0:TOK]
            else:
                seg[:] = x[b, lo:lo + TW]
                flag[:, c] = 1.0
            xin[c] = seg.T.reshape(KD, 128, TW).transpose(1, 0, 2)
        m = dict(shared)
        m["xin"] = xin
        m["flag"] = flag
        m["condT"] = _pk(cvec[b])
        maps.append(m)
    return maps


def run_prog(build_fn, in_maps):
    nc = bass.Bass("TRN2", target_bir_lowering=False)
    build_fn(nc)
    res = run_bass_kernel_spmd(nc, in_maps, core_ids=list(range(8)))
    return res.results


HPC = 4
NQT = SEQ // 512
NKC = SEQ // 128
ATT_SCALE = 1.0 / float(np.sqrt(128.0))


def build_B1(nc):
    qT = nc.dram_tensor("qT", [HPC, 128, SEQ], BF16, kind="ExternalInput").ap()
    kT = nc.dram_tensor("kT", [HPC, 128, SEQ], BF16, kind="ExternalInput").ap()
    vv = nc.dram_tensor("vv", [HPC, SEQ, 128], BF16, kind="ExternalInput").ap()
    ll = nc.dram_tensor("ll", [HPC, SEQ], F32, kind="ExternalInput").ap()
    mask_d = nc.dram_tensor("mask", [128, 4, 512], F32, kind="ExternalInput").ap()
    sel_d = nc.dram_tensor("sel", [HPC, HPC + 1, 128], F32, kind="ExternalInput").ap()
    oT = nc.dram_tensor("oT", [HPC, 128, SEQ], BF16, kind="ExternalOutput").ap()
    with ExitStack() as stack:
        S = Sched(nc, stack)
        C = Ctx()
        setup_common(nc, S, stack, C)
        KT = [alloc(nc, stack, f"KT{i}", [128, SEQ], BF16) for i in range(2)]
        QT = [alloc(nc, stack, f"QT{i}", [128, SEQ], BF16) for i in range(2)]
        VV = [alloc(nc, stack, f"VV{i}", [128, NKC, 128], BF16) for i in range(2)]
        r_K = [Res(f"KT{i}") for i in range(2)]
        r_Q = [Res(f"QT{i}") for i in range(2)]
        r_V = [Res(f"VV{i}") for i in range(2)]
        Lrow = alloc(nc, stack, "Lrow", [HPC, SEQ], F32)
        r_L = Res("Lrow")
        onesr = alloc(nc, stack, "onesr", [HPC, SEQ], F32)
        sel = alloc(nc, stack, "sel", [HPC, HPC + 1, 128], F32)
        r_sel = Res("sel")
        LcT = alloc(nc, stack, "LcT", [128, HPC, NKC], F32)
        r_LcT = Res("LcT")
        MASK = alloc(nc, stack, "MASK", [128, 4, 512], F32)
        r_MASK = Res("MASK")
        FQ = [alloc(nc, stack, f"FQ{i}", [128, 5, 512], F32) for i in range(2)]
        r_FQ = [Res(f"FQ{i}") for i in range(2)]
        TMP = [alloc(nc, stack, f"atmp{i}", [128, 512], F32) for i in range(3)]
        r_TMP = [Res(f"atmp{i}") for i in range(3)]
        PP = [alloc(nc, stack, f"P{i}", [128, 512], BF16) for i in range(3)]
        r_PP = [Res(f"P{i}") for i in range(3)]
        RINV = alloc(nc, stack, "rinv", [128, 512], F32)
        r_RINV = Res("rinv")
        OST = [alloc(nc, stack, f"ost{i}", [128, 512], BF16) for i in range(2)]
        r_OST = [Res(f"ost{i}") for i in range(2)]

        S.dma("sp", out=Lrow[:], in_=ll[:, :], writes=[r_L])
        S.dma("sp", out=sel[:], in_=sel_d[:, :, :], writes=[r_sel])
        S.dma("sp", out=MASK[:], in_=mask_d[:, :, :], writes=[r_MASK])
        S.op("dve", lambda: nc.vector.memset(onesr[:], 1.0), writes=[r_sel])
        S.op("dve", lambda: nc.vector.tensor_tensor_scan(
            out=Lrow[:, :], data0=onesr[:, :], data1=Lrow[:, :], initial=0.0, op0=ALU.mult, op1=ALU.add),
            reads=[r_L, r_sel], writes=[r_L])
        bank7, r_b7 = C.banks[7]
        for kc0 in range(0, NKC, 32):
            def mm(kc0=kc0):
                for kc in range(kc0, kc0 + 32):
                    ins = nc.tensor.matmul(bank7[:, (kc - kc0) * 4:(kc - kc0) * 4 + 4], lhsT=Lrow[:, kc * 128:(kc + 1) * 128],
                                           rhs=sel[:, HPC, 0:HPC], start=True, stop=True)
                return ins
            S.op("pe", mm, reads=[r_L, r_sel], writes=[r_b7])
            S.op("dve", lambda kc0=kc0: nc.vector.tensor_copy(
                out=LcT[:, :, kc0:kc0 + 32], in_=bank7[:, 0:128].rearrange("p (c h) -> p h c", h=HPC)),
                reads=[r_b7], writes=[r_LcT])

        ev = 0
        for h in range(HPC):
            hb = h % 2
            for a in range(0, SEQ, 2048):
                S.dma("sp", out=KT[hb][:, a:a + 2048], in_=kT[h, :, a:a + 2048], writes=[r_K[hb]])
                S.dma("sp", out=QT[hb][:, a:a + 2048], in_=qT[h, :, a:a + 2048], writes=[r_Q[hb]])
                S.dma("sp", out=VV[hb][:, a // 128:(a + 2048) // 128, :],
                      in_=vv[h, a:a + 2048, :].rearrange("(c p) d -> p c d", p=128), writes=[r_V[hb]])
            for qt in range(NQT):
                fq, r_fq = FQ[ev % 2], r_FQ[ev % 2]
                ob, r_ob = C.banks[3 + ev % 2]
                rb, r_rb = C.banks[5 + ev % 2]
                ost, r_ost = OST[ev % 2], r_OST[ev % 2]
                ev += 1
                q0 = qt * 512
                S.op("pe", lambda: nc.tensor.matmul(bank7[:, :], lhsT=sel[:, h, :], rhs=Lrow[:, q0:q0 + 512], start=True, stop=True),
                     reads=[r_L, r_sel], writes=[r_b7])
                S.op("dve", lambda fq=fq: nc.vector.tensor_copy(out=fq[:, 4, :], in_=bank7[:, :]), reads=[r_b7], writes=[r_fq])
                for i in range(4):
                    S.op("dve", lambda fq=fq, i=i: nc.vector.tensor_tensor(out=fq[:, i, :], in0=bank7[:, :], in1=MASK[:, i, :], op=ALU.add),
                         reads=[r_b7, r_MASK], writes=[r_fq])
                nkc = 4 * (qt + 1)

                def emit_S(kc):
                    sb, r_sb = C.banks[kc % 3]
                    S.op("pe", lambda: nc.tensor.matmul(sb[:, :], lhsT=KT[hb][:, kc * 128:(kc + 1) * 128], rhs=QT[hb][:, q0:q0 + 512],
                                                        start=True, stop=True), reads=[r_K[hb], r_Q[hb]], writes=[r_sb])
                emit_S(0)
                emit_S(1)
                for kc in range(nkc):
                    sb, r_sb = C.banks[kc % 3]
                    tmp, r_tmp = TMP[kc % 3], r_TMP[kc % 3]
                    pp, r_pp = PP[kc % 3], r_PP[kc % 3]
                    fi = (kc - 4 * qt) if kc >= 4 * qt else 4
                    S.op("dve", lambda: nc.vector.scalar_tensor_tensor(
                        out=tmp[:, :], in0=sb[:, :], scalar=ATT_SCALE, in1=fq[:, fi, :], op0=ALU.mult, op1=ALU.add),
                        reads=[r_sb, r_fq], writes=[r_tmp])
                    S.op("act", lambda: nc.scalar.activation(out=pp[:, :], in_=tmp[:, :], func=AF.Exp, bias=LcT[:, h, kc:kc + 1], scale=1.0),
                         reads=[r_tmp, r_LcT], writes=[r_pp])
                    if kc + 2 < nkc:
                        emit_S(kc + 2)

                    def mm():
                        nc.tensor.matmul(ob[:, :], lhsT=VV[hb][:, kc, :], rhs=pp[:, :], start=(kc == 0), stop=(kc == nkc - 1))
                        return nc.tensor.matmul(rb[:, :], lhsT=C.ones_b[:, :], rhs=pp[:, :], start=(kc == 0), stop=(kc == nkc - 1))
                    S.op("pe", mm, reads=[r_V[hb], r_pp, C.r_const], writes=[r_ob, r_rb])
                S.op("dve", lambda: nc.vector.reciprocal(out=RINV[:, :], in_=rb[:, :]), reads=[r_rb], writes=[r_RINV])
                S.op("dve", lambda: nc.vector.tensor_tensor(out=ost[:, :], in0=ob[:, :], in1=RINV[:, :], op=ALU.mult),
                     reads=[r_ob, r_RINV], writes=[r_ost])
                S.dma("sp", out=oT[h, :, q0:q0 + 512], in_=ost[:, :], reads=[r_ost], sem_res=r_ost, is_output=True)
        S.finish()
    return nc


def build_B2(nc):
    I = IO()
    I.xmid = nc.dram_tensor("xmid_in", [2, 128, KD, TOK], F32, kind="ExternalInput").ap()
    I.oT = nc.dram_tensor("oT_in", [2, NH, 128, TOK], BF16, kind="ExternalInput").ap()
    I.ada = nc.dram_tensor("ada_in", [128, NADA, KD], F32, kind="ExternalInput").ap()
    I.vecs = nc.dram_tensor("vecs", [128, NVEC, KD], F32, kind="ExternalInput").ap()
    I.w_o = nc.dram_tensor("w_o", [D, D], F32, kind="ExternalInput").ap()
    I.w_ffn_in = nc.dram_tensor("w_ffn_in", [D, 2 * FF], F32, kind="ExternalInput").ap()
    I.w_ffn_out = nc.dram_tensor("w_ffn_out", [FF, D], F32, kind="ExternalInput").ap()
    out = nc.dram_tensor("out", [2, 128, KD, TOK], F32, kind="ExternalOutput").ap()
    with ExitStack() as stack:
        S = Sched(nc, stack)
        C = Ctx()
        setup_common(nc, S, stack, C)
        setup_stream_bufs(nc, S, stack, C)
        setup_act_bufs(nc, S, stack, C)
        S.dma("sp", out=C.vecs[:], in_=I.vecs[:, :, :], writes=[C.r_vecs])
        S.dma("sp", out=C.ADA[:], in_=I.ada[:, :, :], writes=[C.r_ada])
        emit_derived(nc, S, C)
        blocks = []
        for c in range(2):
            blocks += wo_blocks(I.w_o)
            blocks += ffn_blocks(I.w_ffn_in, I.w_ffn_out)
        ws = WStream(S, C.slots, blocks, NSLOT - 2)
        fg = C.vecs[:, V_FING, :]
        for c in range(2):
            for k0 in range(0, KD, 4):
                S.dma("sp", out=C.xT[:, k0:k0 + 4, 2:TW], in_=I.xmid[c, :, k0:k0 + 4, :],
                      writes=[C.r_x[k][t] for k in range(k0, k0 + 4) for t in (1, 2)])
            octr = [0]
            for g in range(KD // GRP):
                gb, r_gb = C.gT[g % 2], C.r_g[g % 2]
                S.dma("sp", out=gb[:, :, 2:TW], in_=I.oT[c, g * GRP:(g + 1) * GRP, :, :].rearrange("h p t -> p h t"),
                      writes=[r_gb[1], r_gb[2]])
                for o in range(8):
                    slot, r_slot = ws.next()
                    emit_out_block(nc, S, C, slot, r_slot, o, gb, r_gb, TILES_M, C.GATE[:, 4, :], octr)
            emit_rstd(nc, S, C, TILES_M)
            emit_modulate(nc, S, C, TILES_M, C.AMOD[:, 5, :], C.ADA[:, 15, :])
            emit_ffn(nc, S, C, ws, TILES_M, C.GATE[:, 5, :])
            emit_rstd(nc, S, C, TILES_M)
            for (t0, t1) in TILES_M:
                ti = tile_idx(t0)
                for k in range(KD):
                    tmp, r_tmp = next_tmp(C)
                    S.op("dve", lambda k=k, tmp=tmp, t0=t0, t1=t1: nc.vector.scalar_tensor_tensor(
                        out=tmp[:, :], in0=C.xT[:, k, t0:t1], scalar=fg[:, k:k + 1], in1=C.rstd[:, t0:t1],
                        op0=ALU.mult, op1=ALU.mult), reads=[C.r_x[k][ti], C.r_rstd[ti], C.r_vecs], writes=[r_tmp])
                    S.dma("sp", out=out[c, :, k, t0 - 2:t1 - 2], in_=tmp[:, :], reads=[r_tmp], sem_res=r_tmp, is_output=True)
        S.finish()
    return nc


_CAP = None


def kernel(**inp):
    inp = {k: np.asarray(v) for k, v in inp.items()}
    H = host_common(inp)
    resA = run_prog(build_A, host_in_A(inp, H))
    bf = ml_dtypes.bfloat16
    Q = np.zeros((2, NH, 128, SEQ), bf)
    Kf = np.zeros((2, NH, 128, SEQ), bf)
    V = np.zeros((2, SEQ, D), bf)
    L = np.zeros((2, NH, SEQ), np.float32)
    for core in range(8):
        b = core // 4
        for c in range(2):
            t0 = chunk_of(core, c) * TOK
            Q[b, :, :, t0:t0 + TOK] = np.asarray(resA[core]["q_o"])[c]
            Kf[b, :, :, t0:t0 + TOK] = np.asarray(resA[core]["k_o"])[c]
            V[b, t0:t0 + TOK, :] = np.asarray(resA[core]["v_o"])[c]
            L[b, :, t0:t0 + TOK] = np.asarray(resA[core]["l_o"])[c]
    mask = np.zeros((128, 4, 512), np.float32)
    pidx = np.arange(128)[:, None]
    tidx = np.arange(512)[None, :]
    for i in range(4):
        mask[:, i, :] = np.where(128 * i + pidx <= tidx, 0.0, -30000.0)
    sel = np.zeros((HPC, HPC + 1, 128), np.float32)
    for h in range(HPC):
        sel[h, h, :] = -1.0
        sel[h, HPC, h] = 1.0
    mapsB1 = []
    for core in range(8):
        b, j = core // 4, core % 4
        hs = slice(HPC * j, HPC * (j + 1))
        mapsB1.append({
            "qT": np.ascontiguousarray(Q[b, hs]), "kT": np.ascontiguousarray(Kf[b, hs]),
            "vv": np.ascontiguousarray(V[b].reshape(SEQ, NH, 128)[:, hs, :].transpose(1, 0, 2)),
            "ll": np.ascontiguousarray(L[b, hs]), "mask": mask, "sel": sel})
    resB1 = run_prog(build_B1, mapsB1)
    if _CAP is not None:
        _CAP.update(resA=resA, Q=Q, K=Kf, V=V, L=L, resB1=resB1)
    OT = np.zeros((2, NH, 128, SEQ), bf)
    for core in range(8):
        b, j = core // 4, core % 4
        OT[b, HPC * j:HPC * (j + 1)] = np.asarray(resB1[core]["oT"])
    wfo = np.asarray(inp["w_ffn_out"], np.float32)
    mapsB2 = []
    for core in range(8):
        b = core // 4
        oin = np.stack([OT[b, :, :, chunk_of(core, c) * TOK:(chunk_of(core, c) + 1) * TOK] for c in range(2)])
        mapsB2.append({
            "xmid_in": np.asarray(resA[core]["xmid"]), "oT_in": np.ascontiguousarray(oin),
            "ada_in": np.asarray(resA[core]["ada_o"]), "vecs": H["vecs"],
            "w_o": np.asarray(inp["w_o"], np.float32)[0], "w_ffn_in": H["w_ffn_in4"][3],
            "w_ffn_out": np.ascontiguousarray(wfo[1, 1])})
    resB2 = run_prog(build_B2, mapsB2)
    outp = np.zeros((2, SEQ, D), np.float32)
    for core in range(8):
        b = core // 4
        for c in range(2):
            t0 = chunk_of(core, c) * TOK
            o = np.asarray(resB2[core]["out"])[c]
            outp[b, t0:t0 + TOK, :] = o.transpose(2, 1, 0).reshape(TOK, D)
    return outp


GROWS = 6144
GROUPS = [[0, 1, 2, 3], [4, 5, 6, 7]]
PROWS = 512
NPIECE = GROWS // PROWS
IX_Q, IX_K, IX_V, IX_O, IX_N = 0, 16, 32, 160, 192


DEBUG_FUSED = False


def build_fused(nc):
    I = IO()
    I.xin = nc.dram_tensor("xin", [2, 128, KD, TW], F32, kind="ExternalInput").ap()
    I.flag = nc.dram_tensor("flag", [128, 2], F32, kind="ExternalInput").ap()
    I.condT = nc.dram_tensor("condT", [128, KD], F32, kind="ExternalInput").ap()
    I.vecs = nc.dram_tensor("vecs", [128, NVEC, KD], F32, kind="ExternalInput").ap()
    I.w_ada_part = nc.dram_tensor("w_ada_part", [5, D, D], F32, kind="ExternalInput").ap()
    I.bada_part = nc.dram_tensor("bada_part", [128, 5, KD], F32, kind="ExternalInput").ap()
    I.w_ffn_in = nc.dram_tensor("w_ffn_in", [4, D, 2 * FF], F32, kind="ExternalInput").ap()
    I.w_ffn_out = nc.dram_tensor("w_ffn_out", [4, FF, D], F32, kind="ExternalInput").ap()
    I.w_ci = nc.dram_tensor("w_ci", [D, 3 * D], F32, kind="ExternalInput").ap()
    I.w_co = nc.dram_tensor("w_co", [D, D], F32, kind="ExternalInput").ap()
    I.w_q = nc.dram_tensor("w_q", [D, D], F32, kind="ExternalInput").ap()
    I.w_kv = nc.dram_tensor("w_kv", [D, 2 * D], F32, kind="ExternalInput").ap()
    I.w_o = nc.dram_tensor("w_o", [D, D], F32, kind="ExternalInput").ap()
    I.w_f = nc.dram_tensor("w_f", [128, KD, NH], F32, kind="ExternalInput").ap()
    I.b_f = nc.dram_tensor("b_f", [NH, 1], F32, kind="ExternalInput").ap()
    I.mask = nc.dram_tensor("mask", [128, 4, 512], F32, kind="ExternalInput").ap()
    I.sel = nc.dram_tensor("sel", [NH, HPC + 1, 128], F32, kind="ExternalInput").ap()
    I.idx = nc.dram_tensor("idx", [128, IX_N], mybir.dt.int32, kind="ExternalInput").ap()
    out = nc.dram_tensor("out", [2, 128, KD, TOK], F32, kind="ExternalOutput").ap()
    G = [nc.dram_tensor(f"G{c}", [GROWS, TOK], BF16).ap() for c in range(2)]
    GO = [nc.dram_tensor(f"GO{c}", [NPIECE * 4 * PROWS, TOK], BF16).ap() for c in range(2)]
    GL = [nc.dram_tensor(f"GL{c}", [NH, TOK], F32).ap() for c in range(2)]
    GOL = [nc.dram_tensor(f"GOL{c}", [4 * NH, TOK], F32).ap() for c in range(2)]
    G2 = nc.dram_tensor("G2", [8 * 512, TOK], BF16).ap()
    GO2 = nc.dram_tensor("GO2", [8 * 4 * PROWS, TOK], BF16).ap()
    xmid_d = nc.dram_tensor("xmid_d", [2, 128, KD, TOK], F32).ap()
    AGi = nc.dram_tensor("AGi", [128, 5 * KD], F32).ap()
    AGo = nc.dram_tensor("AGo", [4 * 128, 5 * KD], F32).ap()
    if DEBUG_FUSED:
        dbgL = nc.dram_tensor("dbgL", [NH, SEQ], F32, kind="ExternalOutput").ap()
        dbgO = nc.dram_tensor("dbgO", [HPC, 128, SEQ], BF16, kind="ExternalOutput").ap()
        dbgA = nc.dram_tensor("dbgA", [128, NADA, KD], F32, kind="ExternalOutput").ap()

    O = IO()
    O.qdst = lambda c, h, a, b: G[c][h * 128:(h + 1) * 128, a:b]
    O.kdst = lambda c, h, a, b: G[c][2048 + h * 128:2048 + (h + 1) * 128, a:b]
    O.ldst = lambda c, a, b: GL[c][:, a:b]
    def vstore(S, c, ts, vb, st, r_st):
        for hh in range(2):
            h = 2 * vb + hh
            S.dma("sp", out=G[c][4096 + h * 128:4096 + (h + 1) * 128, ts * 128:(ts + 1) * 128], in_=st[:, hh * 128:(hh + 1) * 128],
                  reads=[r_st], sem_res=r_st, is_output=True)
    O.vstore = vstore

    def gather_pieces(S, c, lo, hi, r):
        for i in range(lo, hi):
            S.defer_collective("AllGather", GROUPS, G[c][i * PROWS:(i + 1) * PROWS, :], GO[c][i * 4 * PROWS:(i + 1) * 4 * PROWS, :], r)

    with ExitStack() as top:
        S = Sched(nc, top)
        C = Ctx()
        setup_common(nc, S, top, C)
        setup_small(nc, S, top, C)
        IDX = alloc(nc, top, "IDX", [128, IX_N], mybir.dt.int32)
        r_IDX = Res("IDX")
        S.dma("sp", out=IDX[:], in_=I.idx[:, :], writes=[r_IDX])
        r_cc = [Res("cc0"), Res("cc1"), Res("ccl0"), Res("ccl1"), Res("cc2")]
        O.after = lambda S, stage, c: gather_pieces(S, c, {"q": 0, "k": 4, "v": 8}[stage], {"q": 4, "k": 8, "v": 12}[stage], r_cc[c])

        with ExitStack() as stack:
            setup_stream_bufs(nc, S, stack, C)
            setup_act_bufs(nc, S, stack, C)
            C.cond = alloc(nc, stack, "cond", [128, KD], F32)
            C.r_cond = Res("cond")
            C.flag = alloc(nc, stack, "flag", [128, 2], F32)
            C.U = [alloc(nc, stack, f"U{i}", [128, TW], F32) for i in range(2)]
            C.r_U = [Res(f"U{i}") for i in range(2)]
            C.BG = [alloc(nc, stack, f"BG{i}", [128, TOK], F32) for i in range(2)]
            C.r_BG = [Res(f"BG{i}") for i in range(2)]
            C.c1 = alloc(nc, stack, "c1", [128, TOK], F32)
            C.r_c1 = Res("c1")
            C.cgh = alloc(nc, stack, "cgh", [128, 2], F32)
            C.r_cgh = Res("cgh")
            C.qst = [alloc(nc, stack, f"qst{i}", [128, 512], BF16) for i in range(3)]
            C.r_qst = [Res(f"qst{i}") for i in range(3)]
            C.vst = [alloc(nc, stack, f"vst{i}", [128, 256], BF16) for i in range(3)]
            C.r_vst = [Res(f"vst{i}") for i in range(3)]
            C.est = [alloc(nc, stack, "est0", [NH, 512], F32)] * 2
            C.r_est = [Res("est0")] * 2
            C.lst = [alloc(nc, stack, f"lst{i}", [NH, 512], F32) for i in range(2)]
            C.r_lst = [Res(f"lst{i}") for i in range(2)]
            C.wff = alloc(nc, stack, "wff", [128, KD, NH], F32)
            C.wfb = alloc(nc, stack, "wfb", [128, KD, NH], BF16)
            C.negb = alloc(nc, stack, "negb", [NH, 1], F32)
            C.r_wf = Res("wf")
            C.r_negb = Res("negb")
            C.proj_ctr = [0]
            C.r_xst = [Res(f"xst{i}") for i in range(4)]
            S.dma("sp", out=C.flag[:], in_=I.flag[:, :], writes=[C.r_vecs])
            S.dma("sp", out=C.wff[:], in_=I.w_f[:, :, :], writes=[C.r_wf])
            S.dma("sp", out=C.negb[:], in_=I.b_f[:, :], writes=[C.r_negb])
            S.op("dve", lambda: nc.vector.tensor_copy(out=C.wfb[:], in_=C.wff[:]), reads=[C.r_wf], writes=[C.r_wf])
            S.op("dve", lambda: nc.vector.tensor_scalar(out=C.negb[:], in0=C.negb[:], scalar1=-1.0, scalar2=None, op0=ALU.mult),
                 reads=[C.r_negb], writes=[C.r_negb])
            C.apart = alloc(nc, stack, "apart", [128, 5, KD], F32)
            C.bpart = alloc(nc, stack, "bpart", [128, 5, KD], F32)
            emit_ada_sharded(nc, S, C, I, AGi, AGo)
            if DEBUG_FUSED:
                S.dma("sp", out=dbgA[:, :, :], in_=C.ADA[:], reads=[C.r_ada], sem_res=Res("dbgA"), is_output=True)
            emit_derived(nc, S, C)
            blocks = []
            for c in range(2):
                blocks += ffn_blocks(I.w_ffn_in[0], I.w_ffn_out[0])
                blocks += conv_blocks(I.w_ci, I.w_co)
                blocks += ffn_blocks(I.w_ffn_in[1], I.w_ffn_out[1])
                blocks += ffn_blocks(I.w_ffn_in[2], I.w_ffn_out[2])
                blocks += proj_blocks(I.w_q, 0, D)
                blocks += proj_blocks(I.w_kv, 0, D)
                blocks += proj_blocks(I.w_kv, D, D)
            ws = WStream(S, C.slots, blocks, NSLOT - 2)
            for c in range(2):
                for k0 in range(0, KD, 4):
                    S.dma("sp", out=C.xT[:, k0:k0 + 4, :], in_=I.xin[c, :, k0:k0 + 4, :],
                          writes=[C.r_x[k][t] for k in range(k0, k0 + 4) for t in range(3)])
                for _t in TILES_H:
                    emit_rstd(nc, S, C, [_t])
                    emit_modulate(nc, S, C, [_t], C.AMOD[:, 0, :], C.ADA[:, 0, :])
                emit_ffn(nc, S, C, ws, TILES_H, C.GATE[:, 0, :])
                for _t in TILES_H:
                    emit_rstd(nc, S, C, [_t])
                    emit_modulate(nc, S, C, [_t], C.AMOD[:, 1, :], C.ADA[:, 3, :])
                emit_conv(nc, S, C, ws, c, C.GATE[:, 1, :], C.flag)
                for _t in TILES_M:
                    emit_rstd(nc, S, C, [_t])
                    emit_modulate(nc, S, C, [_t], C.AMOD[:, 2, :], C.ADA[:, 6, :])
                emit_ffn(nc, S, C, ws, TILES_M, C.GATE[:, 2, :])
                for _t in TILES_M:
                    emit_rstd(nc, S, C, [_t])
                    emit_modulate(nc, S, C, [_t], C.AMOD[:, 3, :], C.ADA[:, 9, :])
                emit_ffn(nc, S, C, ws, TILES_M, C.GATE[:, 3, :])
                emit_proj(nc, S, C, ws, c, O)
                for k0 in range(0, KD, 4):
                    S.dma("sp", out=xmid_d[c, :, k0:k0 + 4, :], in_=C.xT[:, k0:k0 + 4, 2:TW],
                          reads=[C.r_x[k][t] for k in range(k0, k0 + 4) for t in (1, 2)], sem_res=C.r_xst[k0 // 4], is_output=True)
                S.defer_collective("AllGather", GROUPS, GL[c], GOL[c], r_cc[2 + c], front=True)
            S.flush_collectives()
            S.drain()

        with ExitStack() as stack:
            KT = [alloc(nc, stack, f"KT{i}", [128, SEQ], BF16) for i in range(2)]
            QT = [alloc(nc, stack, f"QT{i}", [128, SEQ], BF16) for i in range(2)]
            VV = [alloc(nc, stack, f"VV{i}", [128, NKC, 128], BF16) for i in range(2)]
            r_K = [[Res(f"KT{i}")] * 8 for i in range(2)]
            r_Q = [[Res(f"QT{i}")] * 8 for i in range(2)]
            r_V = [[Res(f"VV{i}")] * 8 for i in range(2)]
            Lrow = alloc(nc, stack, "Lrow", [NH, SEQ], F32)
            r_L = Res("Lrow")
            onesr = alloc(nc, stack, "onesr", [NH, 1024], F32)
            sel = alloc(nc, stack, "sel", [NH, HPC + 1, 128], F32)
            r_sel = Res("sel")
            LcT = alloc(nc, stack, "LcT", [128, HPC, NKC], F32)
            r_LcT = Res("LcT")
            MASK = alloc(nc, stack, "MASK", [128, 4, 512], F32)
            r_MASK = Res("MASK")
            FQ = [alloc(nc, stack, f"FQ{i}", [128, 5, 512], F32) for i in range(2)]
            r_FQ = [Res(f"FQ{i}") for i in range(2)]
            TMP = [alloc(nc, stack, f"atmp{i}", [128, 512], F32) for i in range(4)]
            r_TMP = [Res(f"atmp{i}") for i in range(4)]
            PP = [alloc(nc, stack, f"P{i}", [128, 512], BF16) for i in range(4)]
            r_PP = [Res(f"P{i}") for i in range(4)]
            RINV = alloc(nc, stack, "rinv", [128, 512], F32)
            r_RINV = Res("rinv")
            OST = [alloc(nc, stack, f"ost{i}", [128, 512], BF16) for i in range(2)]
            r_OST = [Res(f"ost{i}") for i in range(2)]
            GOv = [GO[c].rearrange("r (a d) -> (r a) d", d=128) for c in range(2)]
            chunk = lambda r, c: (r if c == 0 else 7 - r)

            S.dma("sp", out=sel[:], in_=I.sel[:, :, :], writes=[r_sel])
            S.dma("sp", out=MASK[:], in_=I.mask[:, :, :], writes=[r_MASK])
            S.op("dve", lambda: nc.vector.memset(onesr[:], 1.0), writes=[r_sel])
            for c in range(2):
                for r in range(4):
                    tb = chunk(r, c) * TOK
                    S.dma("sp", out=Lrow[:, tb:tb + TOK], in_=GOL[c][r * NH:(r + 1) * NH, :], reads=[r_cc[2 + c]], writes=[r_L])
            for sg in range(SEQ // 1024):
                a, b = sg * 1024, (sg + 1) * 1024
                init = 0.0 if sg == 0 else Lrow[:, a - 1:a]
                S.fence("dve")
                S.op("dve", lambda a=a, b=b, init=init: nc.vector.tensor_tensor_scan(
                    out=Lrow[:, a:b], data0=onesr[:, :], data1=Lrow[:, a:b], initial=init, op0=ALU.mult, op1=ALU.add),
                    reads=[r_L, r_sel], writes=[r_L])
            if DEBUG_FUSED:
                S.dma("sp", out=dbgL[:, :], in_=Lrow[:, :], reads=[r_L], sem_res=Res("dbgL"), is_output=True)
            bank7, r_b7 = C.banks[7]
            for kc0 in range(0, NKC, 32):
                def mm(kc0=kc0):
                    for kc in range(kc0, kc0 + 32):
                        ins = nc.tensor.matmul(bank7[:, (kc - kc0) * 4:(kc - kc0) * 4 + 4], lhsT=Lrow[:, kc * 128:(kc + 1) * 128],
                                               rhs=sel[:, HPC, 0:HPC], start=True, stop=True)
                    return ins
                S.op("pe", mm, reads=[r_L, r_sel], writes=[r_b7])
                S.op("dve", lambda kc0=kc0: nc.vector.tensor_copy(
                    out=LcT[:, :, kc0:kc0 + 32], in_=bank7[:, 0:128].rearrange("p (c h) -> p h c", h=HPC)),
                    reads=[r_b7], writes=[r_LcT])

            ev = 0
            for h in range(HPC):
                hb = h % 2
                for blk in range(8):
                    c, r = (0, blk) if blk < 4 else (1, 7 - blk)
                    tb = blk * TOK
                    col = h * 4 + r
                    S.gather(KT[hb][:, tb:tb + TOK], GO[c][:, :], IDX[:, IX_K + col:IX_K + col + 1],
                             reads=[r_cc[c], r_IDX], writes=[r_K[hb][blk]])
                    S.gather(QT[hb][:, tb:tb + TOK], GO[c][:, :], IDX[:, IX_Q + col:IX_Q + col + 1],
                             reads=[r_cc[c], r_IDX], writes=[r_Q[hb][blk]])
                    S.gather(VV[hb][:, blk * 8:(blk + 1) * 8, :].rearrange("p a d -> p (a d)"), GO[c][:, :],
                             IDX[:, IX_V + col:IX_V + col + 1], reads=[r_cc[c], r_IDX], writes=[r_V[hb][blk]])
                for qt in range(NQT):
                    fq, r_fq = FQ[ev % 2], r_FQ[ev % 2]
                    ob, r_ob = C.banks[4 + ev % 2]
                    rb, r_rb = C.banks[6 + ev % 2]
                    ost, r_ost = OST[ev % 2], r_OST[ev % 2]
                    ev += 1
                    q0 = qt * 512
                    fqb, r_fqb = C.banks[3]
                    S.op("pe", lambda: nc.tensor.matmul(fqb[:, :], lhsT=sel[:, h, :], rhs=Lrow[:, q0:q0 + 512], start=True, stop=True),
                         reads=[r_L, r_sel], writes=[r_fqb])
                    S.op("dve", lambda fq=fq: nc.vector.tensor_copy(out=fq[:, 4, :], in_=fqb[:, :]), reads=[r_fqb], writes=[r_fq])
                    for i in range(4):
                        S.op("dve", lambda fq=fq, i=i: nc.vector.tensor_tensor(out=fq[:, i, :], in0=fqb[:, :], in1=MASK[:, i, :], op=ALU.add),
                             reads=[r_fqb, r_MASK], writes=[r_fq])
                    nkc = 4 * (qt + 1)

                    def emit_S(kc):
                        sb, r_sb = C.banks[kc % 4]
                        S.op("pe", lambda: nc.tensor.matmul(sb[:, :], lhsT=KT[hb][:, kc * 128:(kc + 1) * 128], rhs=QT[hb][:, q0:q0 + 512],
                                                            start=True, stop=True), reads=[r_K[hb][kc // 8], r_Q[hb][qt // 2]], writes=[r_sb])
                    emit_S(0)
                    emit_S(1)
                    emit_S(2)
                    for kc in range(nkc):
                        sb, r_sb = C.banks[kc % 4]
                        tmp, r_tmp = TMP[kc % 4], r_TMP[kc % 4]
                        pp, r_pp = PP[kc % 4], r_PP[kc % 4]
                        fi = (kc - 4 * qt) if kc >= 4 * qt else 4
                        S.op("dve", lambda: nc.vector.scalar_tensor_tensor(
                            out=tmp[:, :], in0=sb[:, :], scalar=ATT_SCALE, in1=fq[:, fi, :], op0=ALU.mult, op1=ALU.add),
                            reads=[r_sb, r_fq], writes=[r_tmp])
                        S.op("act", lambda: nc.scalar.activation(out=pp[:, :], in_=tmp[:, :], func=AF.Exp, bias=LcT[:, h, kc:kc + 1], scale=1.0),
                             reads=[r_tmp, r_LcT], writes=[r_pp])
                        if kc + 3 < nkc:
                            emit_S(kc + 3)

                        def mm():
                            nc.tensor.matmul(ob[:, :], lhsT=VV[hb][:, kc, :], rhs=pp[:, :], start=(kc == 0), stop=(kc == nkc - 1))
                            return nc.tensor.matmul(rb[:, :], lhsT=C.ones_b[:, :], rhs=pp[:, :], start=(kc == 0), stop=(kc == nkc - 1))
                        S.op("pe", mm, reads=[r_V[hb][kc // 8], r_pp, C.r_const], writes=[r_ob, r_rb])
                    S.op("dve", lambda: nc.vector.reciprocal(out=RINV[:, :], in_=rb[:, :]), reads=[r_rb], writes=[r_RINV])
                    S.op("dve", lambda: nc.vector.tensor_tensor(out=ost[:, :], in0=ob[:, :], in1=RINV[:, :], op=ALU.mult),
                         reads=[r_ob, r_RINV], writes=[r_ost])
                    ci = qt // 2
                    g2r = (h * 2 + ci // 4) * PROWS + (ci % 4) * 128
                    S.dma("sp", out=G2[g2r:g2r + 128, (qt % 2) * 512:(qt % 2 + 1) * 512], in_=ost[:, :],
                          reads=[r_ost], sem_res=r_ost, is_output=True)
                    if qt % 8 == 7:
                        i = h * 2 + qt // 8
                        S.collective("AllGather", GROUPS, G2[i * PROWS:(i + 1) * PROWS, :], GO2[i * 4 * PROWS:(i + 1) * 4 * PROWS, :], r_cc[4])
                    if DEBUG_FUSED:
                        S.dma("sp", out=dbgO[h, :, q0:q0 + 512], in_=ost[:, :], reads=[r_ost], sem_res=r_ost, is_output=True)
            S.drain()

        with ExitStack() as stack:
            setup_stream_bufs(nc, S, stack, C)
            setup_act_bufs(nc, S, stack, C)
            blocks = []
            for c in range(2):
                blocks += wo_blocks(I.w_o)
                blocks += ffn_blocks(I.w_ffn_in[3], I.w_ffn_out[3])
            ws = WStream(S, C.slots, blocks, NSLOT - 2)
            fg = C.vecs[:, V_FING, :]
            for c in range(2):
                for k0 in range(0, KD, 4):
                    S.dma("sp", out=C.xT[:, k0:k0 + 4, 2:TW], in_=xmid_d[c, :, k0:k0 + 4, :],
                          writes=[C.r_x[k][t] for k in range(k0, k0 + 4) for t in (1, 2)])
                octr = [0]
                for g in range(KD // GRP):
                    gb, r_gb = C.gT[g % 2], C.r_g[g % 2]
                    for hi in range(GRP):
                        oc = IX_O + c * NH + g * GRP + hi
                        S.gather(gb[:, hi, 2:TW], GO2[:, :], IDX[:, oc:oc + 1], reads=[r_cc[4], r_IDX], writes=[r_gb[1], r_gb[2]])
                    for o in range(8):
                        slot, r_slot = ws.next()
                        emit_out_block(nc, S, C, slot, r_slot, o, gb, r_gb, TILES_M, C.GATE[:, 4, :], octr, nbanks=4)
                for _t in TILES_M:
                    emit_rstd(nc, S, C, [_t])
                    emit_modulate(nc, S, C, [_t], C.AMOD[:, 5, :], C.ADA[:, 15, :])
                emit_ffn(nc, S, C, ws, TILES_M, C.GATE[:, 5, :])
                emit_rstd(nc, S, C, TILES_M)
                for (t0, t1) in TILES_M:
                    ti = tile_idx(t0)
                    for k in range(KD):
                        tmp, r_tmp = next_tmp(C)
                        S.op("dve", lambda k=k, tmp=tmp, t0=t0, t1=t1: nc.vector.scalar_tensor_tensor(
                            out=tmp[:, :], in0=C.xT[:, k, t0:t1], scalar=fg[:, k:k + 1], in1=C.rstd[:, t0:t1],
                            op0=ALU.mult, op1=ALU.mult), reads=[C.r_x[k][ti], C.r_rstd[ti], C.r_vecs], writes=[r_tmp])
                        S.dma("sp", out=out[c, :, k, t0 - 2:t1 - 2], in_=tmp[:, :], reads=[r_tmp], sem_res=r_tmp, is_output=True)
            S.finish()
    return nc


def make_idx(core):
    j = core % 4
    p = np.arange(128, dtype=np.int64)
    idx = np.zeros((128, IX_N), np.int64)
    for hl in range(HPC):
        hg = HPC * j + hl
        for r in range(4):
            col = hl * 4 + r
            base = r * PROWS + (hg % 4) * 128
            idx[:, IX_Q + col] = (hg // 4) * 4 * PROWS + base + p
            idx[:, IX_K + col] = (4 + hg // 4) * 4 * PROWS + base + p
            idx[:, IX_V + col] = (8 + hg // 4) * 4 * PROWS + base + p
    for c in range(2):
        ci = chunk_of(core, c)
        for hg in range(NH):
            idx[:, IX_O + c * NH + hg] = ((hg % HPC) * 2 + ci // 4) * 4 * PROWS + (hg // HPC) * PROWS + (ci % 4) * 128 + p
    return idx.astype(np.int32)


def host_in_fused(inp, H):
    maps = host_in_A(inp, H)
    wfo = np.asarray(inp["w_ffn_out"], np.float32)
    w_ffn_in = np.stack(H["w_ffn_in4"])
    w_ffn_out = np.ascontiguousarray(np.stack([wfo[0, 0], wfo[0, 1], wfo[1, 0], wfo[1, 1]]))
    w_o = np.asarray(inp["w_o"], np.float32)[0]
    mask = np.zeros((128, 4, 512), np.float32)
    pidx = np.arange(128)[:, None]
    tidx = np.arange(512)[None, :]
    for i in range(4):
        mask[:, i, :] = np.where(128 * i + pidx <= tidx, 0.0, -30000.0)
    p = np.arange(128, dtype=np.int64)
    w_ada = np.asarray(inp["w_ada"], np.float32)
    w_ada_kv = np.asarray(inp["w_ada_kv"], np.float32)
    ada_parts = []
    for j in range(4):
        ws_, bs_ = [], []
        for s_ in range(5 * j, 5 * j + 5):
            if s_ < 18:
                l, col = s_ // 9, (s_ % 9) * D
                ws_.append(w_ada[l][:, col:col + D])
                bs_.append(H["vecs"][:, V_BADA + s_, :])
            else:
                col = (s_ - 18) * D
                ws_.append(w_ada_kv[:, col:col + D])
                bs_.append(H["vecs"][:, V_BKV + (s_ - 18), :])
        ada_parts.append((np.ascontiguousarray(np.stack(ws_)), np.ascontiguousarray(np.stack(bs_, axis=1))))
    for core in range(8):
        j = core % 4
        m = maps[core]
        del m["w_ada"], m["w_ada_kv"]
        m["w_ada_part"], m["bada_part"] = ada_parts[j]
        m["w_ffn_in"] = w_ffn_in
        m["w_ffn_out"] = w_ffn_out
        m["w_o"] = w_o
        m["mask"] = mask
        sel = np.zeros((NH, HPC + 1, 128), np.float32)
        for hl in range(HPC):
            sel[HPC * j + hl, hl, :] = -1.0
            sel[HPC * j + hl, HPC, hl] = 1.0
        m["sel"] = sel
        m["idx"] = make_idx(core)
    return maps


def kernel_unfused(**inp):
    return _kernel_unfused(**inp)


_kernel_unfused = kernel


def kernel(**inp):
    inp = {k: np.asarray(v) for k, v in inp.items()}
    H = host_common(inp)
    res = run_prog(build_fused, host_in_fused(inp, H))
    outp = np.zeros((2, SEQ, D), np.float32)
    for core in range(8):
        b = core // 4
        for c in range(2):
            t0 = chunk_of(core, c) * TOK
            o = np.asarray(res[core]["out"])[c]
            outp[b, t0:t0 + TOK, :] = o.transpose(2, 1, 0).reshape(TOK, D)
    return outp
```

```python
import numpy as np
from contextlib import ExitStack
import ml_dtypes
import concourse.bass as bass
import concourse.mybir as mybir
from concourse.bass_utils import run_bass_kernel_spmd

F32 = mybir.dt.float32
BF16 = mybir.dt.bfloat16
AF = mybir.ActivationFunctionType
ALU = mybir.AluOpType

D = 2048
KD = 16
FF = 5632
KF = 44
NH = 16
SEQ = 8192
TOK = 1024
HALO = 2
TW = TOK + HALO
EPS = 1e-6
GRP = 4
NSLOT = 5
SLOT_ELEMS = 4096

TILES_M = [(2, 514), (514, 1026)]
TILES_H = [(0, 2), (2, 514), (514, 1026)]

V_NORMG = 0
V_KVG = 6
V_FING = 7
V_CONVW = 8
V_CONVB = 11
V_BADA = 12
V_BKV = 30
NVEC = 32
NADA = 20


class Res:
    __slots__ = ("name", "last_w", "readers", "dsem", "dcnt")

    def __init__(self, name):
        self.name = name
        self.last_w = None
        self.readers = []
        self.dsem = None
        self.dcnt = 0


class Sched:
    def __init__(self, nc, stack):
        self.nc = nc
        self.stack = stack
        self.eng = {"pe": nc.tensor, "act": nc.scalar, "dve": nc.vector, "pool": nc.gpsimd, "sp": nc.sync}
        self.esem = {}
        self.ecnt = {}
        for e in self.eng:
            self.esem[e] = stack.enter_context(nc.semaphore("prog_" + e))
            self.ecnt[e] = 0
        self.seen = {}
        self.sem_owner = {id(self.esem[e]): e for e in self.eng}
        self.nsem = 0
        self.out_tokens = []
        self.pending = []
        self.pump_ctr = 0

    def new_sem(self, name):
        self.nsem += 1
        return self.stack.enter_context(self.nc.semaphore(f"d{self.nsem}_{name}"))

    def _wait(self, eng, tok):
        sem, val = tok
        if self.sem_owner.get(id(sem)) == eng:
            return
        key = (eng, id(sem))
        if self.seen.get(key, 0) >= val:
            return
        self.seen[key] = val
        self.eng[eng].wait_ge(sem, val)

    def _deps(self, eng, reads, writes):
        for r in reads:
            if r.last_w is not None:
                self._wait(eng, r.last_w)
        for w in writes:
            if w.last_w is not None:
                self._wait(eng, w.last_w)
            for t in w.readers:
                self._wait(eng, t)

    def _commit(self, tok, reads, writes):
        for r in reads:
            sid = id(tok[0])
            r.readers = [t for t in r.readers if id(t[0]) != sid]
            r.readers.append(tok)
        for w in writes:
            w.last_w = tok
            w.readers = []

    def op(self, eng, fn, reads=(), writes=()):
        self._deps(eng, reads, writes)
        ins = fn()
        self.ecnt[eng] += 1
        ins.then_inc(self.esem[eng], 1)
        tok = (self.esem[eng], self.ecnt[eng])
        self._commit(tok, reads, writes)
        return tok

    def dma(self, eng, out, in_, reads=(), writes=(), sem_res=None, is_output=False):
        self._deps(eng, reads, writes)
        r = sem_res if sem_res is not None else (writes[0] if writes else reads[0])
        if r.dsem is None:
            r.dsem = self.new_sem(r.name)
        ins = self.eng[eng].dma_start(out=out, in_=in_)
        r.dcnt += 16
        ins.then_inc(r.dsem, 16)
        tok = (r.dsem, r.dcnt)
        self._commit(tok, reads, writes)
        if is_output:
            self.out_tokens.append(tok)
        return tok

    def collective(self, kind, groups, src, dst, sem_res, ntok=None):
        for t in (self.out_tokens if ntok is None else self.out_tokens[:ntok]):
            self._wait("pool", t)
        if sem_res.dsem is None:
            sem_res.dsem = self.new_sem(sem_res.name)
        ins = self.nc.gpsimd.collective_compute(kind, ALU.bypass, replica_groups=groups, ins=[src.opt()], outs=[dst.opt()])
        sem_res.dcnt += 1
        ins.then_inc(sem_res.dsem)
        tok = (sem_res.dsem, sem_res.dcnt)
        sem_res.last_w = tok
        sem_res.readers = []
        return tok

    def defer_collective(self, *args, front=False):
        item = args + (len(self.out_tokens),)
        if front:
            self.pending.insert(0, item)
        else:
            self.pending.append(item)

    def pump(self, every=None):
        every = every or getattr(self, "pump_every", 4)
        self.pump_ctr += 1
        if self.pending and self.pump_ctr % every == 0:
            self.collective(*self.pending.pop(0))

    def flush_collectives(self):
        while self.pending:
            self.collective(*self.pending.pop(0))

    def gather(self, out, in_, idx_ap, reads=(), writes=()):
        self._deps("pool", reads, writes)
        r = writes[0]
        if r.dsem is None:
            r.dsem = self.new_sem(r.name)
        ins = self.nc.gpsimd.indirect_dma_start(out=out, out_offset=None, in_=in_,
                                                in_offset=bass.IndirectOffsetOnAxis(ap=idx_ap, axis=0))
        r.dcnt += 16
        ins.then_inc(r.dsem, 16)
        tok = (r.dsem, r.dcnt)
        self._commit(tok, reads, writes)
        return tok

    def drain(self):
        for e in self.eng:
            for t in self.out_tokens:
                self._wait(e, t)
        self.out_tokens = []
        self.barrier()

    def fence(self, eng):
        if self.ecnt[eng] > 0:
            self.eng[eng].wait_ge(self.esem[eng], self.ecnt[eng])

    def barrier(self):
        toks = [(self.esem[e], self.ecnt[e]) for e in self.eng if self.ecnt[e] > 0]
        for e in self.eng:
            for t in toks:
                self._wait(e, t)

    def finish(self):
        for t in self.out_tokens:
            self._wait("sp", t)
        self.out_tokens = []


class WStream:
    def __init__(self, S, slots, blocks, depth):
        self.S = S
        self.slots = slots
        self.blocks = blocks
        self.depth = depth
        self.issued = 0
        self.taken = 0

    def _view(self, tile, shape):
        n = int(np.prod(shape[1:]))
        view = tile[:, 0:n]
        if len(shape) == 3:
            view = view.rearrange("p (a b) -> p a b", a=shape[1])
        elif len(shape) == 4:
            view = view.rearrange("p (a b c) -> p a b c", a=shape[1], b=shape[2])
        return view

    def _issue(self):
        i = self.issued
        ap, dshape, vshape = self.blocks[i]
        tile, res = self.slots[i % len(self.slots)]
        self.S.dma("pool", out=self._view(tile, dshape), in_=ap, writes=[res])
        self.issued += 1

    def next(self):
        while self.issued < min(len(self.blocks), self.taken + self.depth + 1):
            self._issue()
        self.S.pump()
        i = self.taken
        self.taken += 1
        tile, res = self.slots[i % len(self.slots)]
        ap, dshape, vshape = self.blocks[i]
        return self._view(tile, vshape), res


class Ctx:
    pass


_ALLOC_N = [0]


def alloc(nc, stack, name, shape, dt):
    _ALLOC_N[0] += 1
    return stack.enter_context(nc.sbuf_tensor(f"s{_ALLOC_N[0]}_" + name, list(shape), dt))


def grouped_order(n_in, n_groups, n_out):
    order = []
    per = n_in // n_groups
    for f in range(per):
        order.append(("in", 0, f))
    for g in range(n_groups):
        if g + 1 < n_groups:
            for f in range(per):
                order.append(("in", g + 1, f))
        for o in range(n_out):
            order.append(("out", g, o))
    return order


def ffn_blocks(w_in, w_out):
    win = w_in.rearrange("(k p) (f c) -> p k f c", p=128, c=256)
    wout = w_out.rearrange("(f p) d -> p f d", p=128)
    blocks = []
    for kind, g, i in grouped_order(KF, KF // GRP, 8):
        if kind == "in":
            f = g * GRP + i
            blocks.append((win[:, :, f, :], (128, KD, 256), (128, KD, 2, 128)))
        else:
            blocks.append((wout[:, g * GRP:(g + 1) * GRP, i * 256:(i + 1) * 256], (128, GRP, 256), (128, GRP, 256)))
    return blocks


def conv_blocks(w_ci, w_co):
    wci = w_ci.rearrange("(k p) (f c) -> p k f c", p=128, c=384)
    wco = w_co.rearrange("(i p) d -> p i d", p=128)
    blocks = []
    for kind, g, i in grouped_order(KD, KD // GRP, 8):
        if kind == "in":
            f = g * GRP + i
            blocks.append((wci[:, :, f, 0:256], (128, KD, 256), (128, KD, 2, 128)))
            blocks.append((wci[:, :, f, 256:384], (128, KD, 128), (128, KD, 1, 128)))
        else:
            blocks.append((wco[:, g * GRP:(g + 1) * GRP, i * 256:(i + 1) * 256], (128, GRP, 256), (128, GRP, 256)))
    return blocks


def proj_blocks(w, col0, ncols):
    wv = w.rearrange("(k p) d -> p k d", p=128)
    return [(wv[:, :, col0 + i * 256: col0 + (i + 1) * 256], (128, KD, 256), (128, KD, 256)) for i in range(ncols // 256)]


def wo_blocks(w_o):
    wv = w_o.rearrange("(i p) d -> p i d", p=128)
    blocks = []
    for g in range(KD // GRP):
        for o in range(8):
            blocks.append((wv[:, g * GRP:(g + 1) * GRP, o * 256:(o + 1) * 256], (128, GRP, 256), (128, GRP, 256)))
    return blocks


def setup_common(nc, S, stack, C):
    C.banks = []
    for i in range(8):
        t = stack.enter_context(nc.psum_tensor(f"bank{i}", [128, 512], F32))
        C.banks.append((t, Res(f"bank{i}")))
    C.ones_f = alloc(nc, stack, "ones_f", [128, 128], F32)
    C.ones_b = alloc(nc, stack, "ones_b", [128, 128], BF16)
    C.r_const = Res("const")
    S.op("pool", lambda: nc.gpsimd.memset(C.ones_f[:], 1.0), writes=[C.r_const])
    S.op("pool", lambda: nc.gpsimd.memset(C.ones_b[:], 1.0), writes=[C.r_const])


def setup_stream_bufs(nc, S, stack, C):
    C.slots = []
    for i in range(NSLOT):
        t = alloc(nc, stack, f"wslot{i}", [128, SLOT_ELEMS], BF16)
        C.slots.append((t, Res(f"wslot{i}")))


def setup_act_bufs(nc, S, stack, C):
    C.xT = alloc(nc, stack, "xT", [128, KD, TW], F32)
    C.hT = alloc(nc, stack, "hT", [128, KD, TW], BF16)
    C.gT = [alloc(nc, stack, f"gT{i}", [128, GRP, TW], BF16) for i in range(2)]
    C.r_x = [[Res(f"x{k}_{t}") for t in range(3)] for k in range(KD)]
    C.r_h = [[Res(f"h{k}_{t}") for t in range(3)] for k in range(KD)]
    C.r_g = [[Res(f"g{i}_{t}") for t in range(3)] for i in range(2)]
    C.sq = [alloc(nc, stack, f"sq{i}", [128, 512], F32) for i in range(2)]
    C.r_sq = [Res(f"sq{i}") for i in range(2)]
    C.acc = alloc(nc, stack, "acc", [128, 512], F32)
    C.r_acc = Res("acc")
    C.sd = C.sq[0]
    C.r_sd = C.r_sq[0]
    C.rstd = alloc(nc, stack, "rstd", [128, TW], F32)
    C.r_rstd = [Res(f"rstd{t}") for t in range(3)]
    C.tmp = [alloc(nc, stack, f"tmp{i}", [128, 512], F32) for i in range(3)]
    C.r_tmp = [Res(f"tmp{i}") for i in range(3)]
    C.tmp_i = 0
    if not hasattr(C, "ADA"):
        setup_small(nc, S, stack, C)


def setup_small(nc, S, stack, C):
    C.ADA = alloc(nc, stack, "ADA", [128, NADA, KD], F32)
    C.r_ada = Res("ADA")
    C.AMOD = alloc(nc, stack, "AMOD", [128, 8, KD], F32)
    C.GATE = alloc(nc, stack, "GATE", [128, 6, KD], F32)
    C.vecs = alloc(nc, stack, "vecs", [128, NVEC, KD], F32)
    C.r_vecs = Res("vecs")
    C.r_mod = Res("mod")


def tile_idx(t0):
    return {0: 0, 2: 1, 514: 2}[t0]


def next_tmp(C):
    i = C.tmp_i % len(C.tmp)
    C.tmp_i += 1
    return C.tmp[i], C.r_tmp[i]


def emit_derived(nc, S, C):
    rd = [C.r_ada, C.r_vecs]
    for l in range(2):
        for sub in range(3):
            n = l * 3 + sub
            sc = C.ADA[:, l * 9 + sub * 3 + 1, :]
            gt = C.ADA[:, l * 9 + sub * 3 + 2, :]
            S.op("dve", lambda n=n, sc=sc: nc.vector.scalar_tensor_tensor(
                out=C.AMOD[:, n, :], in0=sc, scalar=1.0, in1=C.vecs[:, V_NORMG + n, :],
                op0=ALU.add, op1=ALU.mult), reads=rd, writes=[C.r_mod])
            S.op("dve", lambda n=n, gt=gt, sub=sub: nc.vector.tensor_scalar(
                out=C.GATE[:, n, :], in0=gt, scalar1=(1.0 if sub == 1 else 0.5), scalar2=None,
                op0=ALU.mult), reads=rd, writes=[C.r_mod])
    S.op("dve", lambda: nc.vector.scalar_tensor_tensor(
        out=C.AMOD[:, 6, :], in0=C.ADA[:, 19, :], scalar=1.0, in1=C.vecs[:, V_KVG, :],
        op0=ALU.add, op1=ALU.mult), reads=rd, writes=[C.r_mod])


def emit_rstd(nc, S, C, tiles):
    bank, r_bank = C.banks[7]
    for (t0, t1) in tiles:
        n = t1 - t0
        ti = tile_idx(t0)
        for k in range(KD):
            if k == 0:
                S.op("act", lambda: nc.scalar.activation(out=C.acc[:, :n], in_=C.xT[:, 0, t0:t1], func=AF.Square),
                     reads=[C.r_x[0][ti]], writes=[C.r_acc])
            else:
                sq, r_sq = C.sq[k % 2], C.r_sq[k % 2]
                S.op("act", lambda k=k, sq=sq: nc.scalar.activation(out=sq[:, :n], in_=C.xT[:, k, t0:t1], func=AF.Square),
                     reads=[C.r_x[k][ti]], writes=[r_sq])
                S.op("dve", lambda sq=sq: nc.vector.tensor_tensor(out=C.acc[:, :n], in0=C.acc[:, :n], in1=sq[:, :n], op=ALU.add),
                     reads=[r_sq, C.r_acc], writes=[C.r_acc])
        S.op("pe", lambda: nc.tensor.matmul(bank[:, :n], lhsT=C.ones_f[:], rhs=C.acc[:, :n], start=True, stop=True),
             reads=[C.r_acc, C.r_const], writes=[r_bank])
        S.op("act", lambda: nc.scalar.activation(out=C.sd[:, :n], in_=bank[:, :n], func=AF.Sqrt, scale=1.0 / D, bias=EPS),
             reads=[r_bank], writes=[C.r_sd])
        S.op("dve", lambda: nc.vector.reciprocal(out=C.rstd[:, t0:t1], in_=C.sd[:, :n]),
             reads=[C.r_sd], writes=[C.r_rstd[ti]])


def emit_modulate(nc, S, C, tiles, a_ap, sh_ap):
    for (t0, t1) in tiles:
        n = t1 - t0
        ti = tile_idx(t0)
        for k in range(KD):
            tmp, r_tmp = next_tmp(C)
            S.op("dve", lambda k=k, tmp=tmp: nc.vector.scalar_tensor_tensor(
                out=tmp[:, :n], in0=C.xT[:, k, t0:t1], scalar=a_ap[:, k:k + 1], in1=C.rstd[:, t0:t1],
                op0=ALU.mult, op1=ALU.mult), reads=[C.r_x[k][ti], C.r_rstd[ti], C.r_mod], writes=[r_tmp])
            S.op("act", lambda k=k, tmp=tmp: nc.scalar.activation(
                out=C.hT[:, k, t0:t1], in_=tmp[:, :n], func=AF.Identity, bias=sh_ap[:, k:k + 1], scale=1.0),
                reads=[r_tmp, C.r_mod], writes=[C.r_h[k][ti]])


def emit_out_block(nc, S, C, slot, r_slot, o, src, r_src, tiles, gate_ap, bank_ctr, nbanks=2):
    for dd in range(2):
        d = 2 * o + dd
        for (t0, t1) in tiles:
            n = t1 - t0
            ti = tile_idx(t0)
            bank, r_bank = C.banks[4 + (bank_ctr[0] % nbanks)]
            bank_ctr[0] += 1

            def mm(bank=bank, dd=dd, t0=t0, t1=t1, n=n):
                for fi in range(GRP):
                    ins = nc.tensor.matmul(bank[:, :n], lhsT=slot[:, fi, dd * 128:(dd + 1) * 128],
                                           rhs=src[:, fi, t0:t1], start=(fi == 0), stop=(fi == GRP - 1))
                return ins
            S.op("pe", mm, reads=[r_slot, r_src[ti]], writes=[r_bank])
            S.op("dve", lambda bank=bank, d=d, t0=t0, t1=t1, n=n: nc.vector.scalar_tensor_tensor(
                out=C.xT[:, d, t0:t1], in0=bank[:, :n], scalar=gate_ap[:, d:d + 1], in1=C.xT[:, d, t0:t1],
                op0=ALU.mult, op1=ALU.add), reads=[r_bank, C.r_x[d][ti], C.r_mod], writes=[C.r_x[d][ti]])


def emit_ffn(nc, S, C, ws, tiles, gate_ap):
    pair = [0]
    octr = [0]
    silu_i = [0]
    for kind, g, i in grouped_order(KF, KF // GRP, 8):
        slot, r_slot = ws.next()
        if kind == "in":
            gbuf = C.gT[g % 2]
            r_gb = C.r_g[g % 2]
            for (t0, t1) in tiles:
                n = t1 - t0
                ti = tile_idx(t0)
                pa, r_pa = C.banks[(pair[0] % 2) * 2]
                pb, r_pb = C.banks[(pair[0] % 2) * 2 + 1]
                pair[0] += 1

                def mm(pa=pa, pb=pb, t0=t0, t1=t1, n=n):
                    for k in range(KD):
                        nc.tensor.matmul(pa[:, :n], lhsT=slot[:, k, 0, :], rhs=C.hT[:, k, t0:t1],
                                         start=(k == 0), stop=(k == KD - 1))
                    for k in range(KD):
                        ins = nc.tensor.matmul(pb[:, :n], lhsT=slot[:, k, 1, :], rhs=C.hT[:, k, t0:t1],
                                               start=(k == 0), stop=(k == KD - 1))
                    return ins
                S.op("pe", mm, reads=[r_slot] + [C.r_h[k][ti] for k in range(KD)], writes=[r_pa, r_pb])
                tmp, r_tmp = next_tmp(C)
                S.op("act", lambda pa=pa, tmp=tmp, n=n: nc.scalar.activation(out=tmp[:, :n], in_=pa[:, :n], func=AF.Silu),
                     reads=[r_pa], writes=[r_tmp])
                S.op("dve", lambda pb=pb, tmp=tmp, n=n, t0=t0, t1=t1, gbuf=gbuf, i=i: nc.vector.tensor_tensor(
                    out=gbuf[:, i, t0:t1], in0=pb[:, :n], in1=tmp[:, :n], op=ALU.mult),
                    reads=[r_pb, r_tmp], writes=[r_gb[ti]])
        else:
            emit_out_block(nc, S, C, slot, r_slot, i, C.gT[g % 2], C.r_g[g % 2], tiles, gate_ap, octr, nbanks=4)


def emit_conv(nc, S, C, ws, c, gate_ap, flag):
    octr = [0]
    bank7, r_b7 = C.banks[7]
    bank6, r_b6 = C.banks[6]
    cw = lambda k: C.vecs[:, V_CONVW + k, :]
    cb = C.vecs[:, V_CONVB, :]
    for kind, g, i in grouped_order(KD, KD // GRP, 8):
        slot, r_slot = ws.next()
        if kind == "out":
            emit_out_block(nc, S, C, slot, r_slot, i, C.gT[g % 2], C.r_g[g % 2], TILES_M, gate_ap, octr)
            continue
        f = g * GRP + i
        slot2, r_slot2 = ws.next()
        U, r_U = C.U[f % 2], C.r_U[f % 2]
        BG, r_BG = C.BG[f % 2], C.r_BG[f % 2]
        rh = lambda ti: [C.r_h[k][ti] for k in range(KD)]

        def mmh():
            for k in range(KD):
                nc.tensor.matmul(bank7[:, 0:2], lhsT=slot[:, k, 1, :], rhs=C.hT[:, k, 0:2], start=(k == 0), stop=(k == KD - 1))
            for k in range(KD):
                ins = nc.tensor.matmul(bank7[:, 2:4], lhsT=slot2[:, k, 0, :], rhs=C.hT[:, k, 0:2], start=(k == 0), stop=(k == KD - 1))
            return ins
        S.op("pe", mmh, reads=[r_slot, r_slot2] + rh(0), writes=[r_b7])
        S.op("act", lambda: nc.scalar.activation(out=C.cgh[:, 0:2], in_=bank7[:, 0:2], func=AF.Identity),
             reads=[r_b7], writes=[C.r_cgh])
        S.op("dve", lambda U=U: nc.vector.scalar_tensor_tensor(
            out=U[:, 0:2], in0=bank7[:, 2:4], scalar=flag[:, c:c + 1], in1=C.cgh[:, 0:2],
            op0=ALU.mult, op1=ALU.mult), reads=[r_b7, C.r_cgh, C.r_vecs], writes=[r_U])
        for si, (t0, t1) in enumerate(TILES_M):
            ti = tile_idx(t0)
            pcg, r_pcg = C.banks[si * 2]
            pxv, r_pxv = C.banks[si * 2 + 1]

            def mm(pcg=pcg, pxv=pxv, t0=t0, t1=t1):
                for k in range(KD):
                    nc.tensor.matmul(bank6[:, :], lhsT=slot[:, k, 0, :], rhs=C.hT[:, k, t0:t1], start=(k == 0), stop=(k == KD - 1))
                for k in range(KD):
                    nc.tensor.matmul(pcg[:, :], lhsT=slot[:, k, 1, :], rhs=C.hT[:, k, t0:t1], start=(k == 0), stop=(k == KD - 1))
                for k in range(KD):
                    ins = nc.tensor.matmul(pxv[:, :], lhsT=slot2[:, k, 0, :], rhs=C.hT[:, k, t0:t1], start=(k == 0), stop=(k == KD - 1))
                return ins
            S.op("pe", mm, reads=[r_slot, r_slot2] + rh(ti), writes=[r_b6, r_pcg, r_pxv])
            S.op("act", lambda BG=BG, t0=t0, t1=t1: nc.scalar.activation(out=BG[:, t0 - 2:t1 - 2], in_=bank6[:, :], func=AF.Identity),
                 reads=[r_b6], writes=[r_BG])
            tmp, r_tmp = next_tmp(C)
            S.op("act", lambda tmp=tmp, pcg=pcg: nc.scalar.activation(out=tmp[:, :], in_=pcg[:, :], func=AF.Identity),
                 reads=[r_pcg], writes=[r_tmp])
            S.op("dve", lambda U=U, tmp=tmp, pxv=pxv, t0=t0, t1=t1: nc.vector.tensor_tensor(
                out=U[:, t0:t1], in0=pxv[:, :], in1=tmp[:, :], op=ALU.mult), reads=[r_pxv, r_tmp], writes=[r_U])
        S.op("dve", lambda U=U, f=f: nc.vector.tensor_scalar(
            out=C.c1[:, :], in0=U[:, 2:TW], scalar1=cw(2)[:, f:f + 1], scalar2=cb[:, f:f + 1], op0=ALU.mult, op1=ALU.add),
            reads=[r_U, C.r_vecs], writes=[C.r_c1])
        S.op("dve", lambda U=U, f=f: nc.vector.scalar_tensor_tensor(
            out=C.c1[:, :], in0=U[:, 1:TW - 1], scalar=cw(1)[:, f:f + 1], in1=C.c1[:, :], op0=ALU.mult, op1=ALU.add),
            reads=[r_U, C.r_c1], writes=[C.r_c1])
        S.op("dve", lambda U=U, f=f: nc.vector.scalar_tensor_tensor(
            out=C.c1[:, :], in0=U[:, 0:TW - 2], scalar=cw(0)[:, f:f + 1], in1=C.c1[:, :], op0=ALU.mult, op1=ALU.add),
            reads=[r_U, C.r_c1], writes=[C.r_c1])
        if getattr(C, "dbgc", None) is not None and f == 0:
            S.dma("sp", out=C.dbgc[0, :, :], in_=U[:, :], reads=[r_U], sem_res=Res("dU"), is_output=True)
            S.dma("sp", out=C.dbgc[1, :, 0:TOK], in_=BG[:, :], reads=[r_BG], sem_res=Res("dBG"), is_output=True)
            S.dma("sp", out=C.dbgc[2, :, 0:TOK], in_=C.c1[:, :], reads=[C.r_c1], sem_res=Res("dc1"), is_output=True)
        gb = C.gT[g % 2]
        S.op("dve", lambda gb=gb, BG=BG, i=i: nc.vector.tensor_tensor(
            out=gb[:, i, 2:TW], in0=C.c1[:, :], in1=BG[:, :], op=ALU.mult),
            reads=[C.r_c1, r_BG], writes=[C.r_g[g % 2][1], C.r_g[g % 2][2]])


def emit_featproj(nc, S, C, ws, nblk, dst_fn, stage_name):
    ctr = C.proj_ctr
    for hp in range(nblk):
        slot, r_slot = ws.next()
        for hh in range(2):
            h = 2 * hp + hh
            for (t0, t1) in TILES_M:
                ti = tile_idx(t0)
                bank, r_bank = C.banks[ctr[0] % 4]
                st, r_st = C.qst[ctr[0] % 3], C.r_qst[ctr[0] % 3]
                ctr[0] += 1

                def mm(bank=bank, hh=hh, t0=t0, t1=t1):
                    for k in range(KD):
                        ins = nc.tensor.matmul(bank[:, :], lhsT=slot[:, k, hh * 128:(hh + 1) * 128], rhs=C.hT[:, k, t0:t1],
                                               start=(k == 0), stop=(k == KD - 1))
                    return ins
                S.op("pe", mm, reads=[r_slot] + [C.r_h[k][ti] for k in range(KD)], writes=[r_bank])
                S.op("act", lambda bank=bank, st=st: nc.scalar.activation(out=st[:, :], in_=bank[:, :], func=AF.Identity),
                     reads=[r_bank], writes=[r_st])
                S.dma("sp", out=dst_fn(h, t0 - 2, t1 - 2), in_=st[:, :], reads=[r_st], sem_res=r_st, is_output=True)


def emit_proj(nc, S, C, ws, c, O):
    emit_rstd(nc, S, C, TILES_M)
    emit_modulate(nc, S, C, TILES_M, C.AMOD[:, 4, :], C.ADA[:, 12, :])
    emit_featproj(nc, S, C, ws, 8, lambda h, a, b: O.qdst(c, h, a, b), "q")
    if hasattr(O, "after"):
        O.after(S, "q", c)
    emit_modulate(nc, S, C, TILES_M, C.AMOD[:, 6, :], C.ADA[:, 18, :])
    emit_featproj(nc, S, C, ws, 8, lambda h, a, b: O.kdst(c, h, a, b), "k")
    if hasattr(O, "after"):
        O.after(S, "k", c)
    ctr = C.proj_ctr
    for vb in range(8):
        slot, r_slot = ws.next()
        for ts in range(8):
            ti = 1 if ts < 4 else 2
            c0 = 2 + ts * 128
            bank, r_bank = C.banks[ctr[0] % 4]
            st, r_st = C.vst[ctr[0] % 3], C.r_vst[ctr[0] % 3]
            ctr[0] += 1

            def mm(bank=bank, c0=c0):
                for k in range(KD):
                    ins = nc.tensor.matmul(bank[:, 0:256], lhsT=C.hT[:, k, c0:c0 + 128], rhs=slot[:, k, :],
                                           start=(k == 0), stop=(k == KD - 1))
                return ins
            S.op("pe", mm, reads=[r_slot] + [C.r_h[k][ti] for k in range(KD)], writes=[r_bank])
            S.op("act", lambda bank=bank, st=st: nc.scalar.activation(out=st[:, :], in_=bank[:, 0:256], func=AF.Identity),
                 reads=[r_bank], writes=[r_st])
            O.vstore(S, c, ts, vb, st, r_st)
    if hasattr(O, "after"):
        O.after(S, "v", c)
    for (t0, t1) in TILES_M:
        ti = tile_idx(t0)
        bank, r_bank = C.banks[ctr[0] % 4]
        es, r_es = C.est[ctr[0] % 2], C.r_est[ctr[0] % 2]
        ls, r_ls = C.lst[ctr[0] % 2], C.r_lst[ctr[0] % 2]
        ctr[0] += 1

        def mm(bank=bank, t0=t0, t1=t1):
            for k in range(KD):
                ins = nc.tensor.matmul(bank[0:16, :], lhsT=C.wfb[:, k, :], rhs=C.hT[:, k, t0:t1],
                                       start=(k == 0), stop=(k == KD - 1))
            return ins
        S.op("pe", mm, reads=[C.r_wf] + [C.r_h[k][ti] for k in range(KD)], writes=[r_bank])
        S.op("act", lambda bank=bank, es=es: nc.scalar.activation(out=es[:, :], in_=bank[0:16, :], func=AF.Exp,
                                                                  scale=-1.0, bias=C.negb[:, 0:1]),
             reads=[r_bank, C.r_negb], writes=[r_es])
        S.op("act", lambda es=es, ls=ls: nc.scalar.activation(out=ls[:, :], in_=es[:, :], func=AF.Ln, bias=1.0),
             reads=[r_es], writes=[r_ls])
        S.dma("sp", out=O.ldst(c, t0 - 2, t1 - 2), in_=ls[:, :], reads=[r_ls], sem_res=r_ls, is_output=True)


def emit_ada_sharded(nc, S, C, I, AGi, AGo):
    scratch = C.hT[:].rearrange("p k t -> p (k t)").bitcast(F32)
    wb = [scratch[:, i * 2048:(i + 1) * 2048] for i in range(3)]
    r_wb = [Res(f"adaw{i}") for i in range(3)]
    accA = scratch[:, 6144:8192]
    r_accA = Res("accA")
    bank6, r_b6 = C.banks[6]
    S.dma("sp", out=C.vecs[:], in_=I.vecs[:, :, :], writes=[C.r_vecs])
    S.dma("sp", out=C.cond[:], in_=I.condT[:, :], writes=[C.r_cond])
    r_bpart = Res("bpart")
    S.dma("sp", out=C.bpart[:], in_=I.bada_part[:, :, :], writes=[r_bpart])
    S.op("act", lambda: nc.scalar.activation(out=C.cond[:], in_=C.cond[:], func=AF.Silu), reads=[C.r_cond], writes=[C.r_cond])
    r_part = Res("adapart")
    n = 0
    for s in range(5):
        for k in range(KD):
            w, r_w = wb[n % 3], r_wb[n % 3]
            n += 1
            S.dma("sp" if n % 2 else "act", out=w, in_=I.w_ada_part[s, k * 128:(k + 1) * 128, :], writes=[r_w])
            if k == 0:
                S.op("dve", lambda w=w: nc.vector.tensor_scalar(out=accA, in0=w, scalar1=C.cond[:, 0:1], scalar2=None, op0=ALU.mult),
                     reads=[r_w, C.r_cond], writes=[r_accA])
            else:
                S.op("dve", lambda w=w, k=k: nc.vector.scalar_tensor_tensor(
                    out=accA, in0=w, scalar=C.cond[:, k:k + 1], in1=accA, op0=ALU.mult, op1=ALU.add),
                    reads=[r_w, C.r_cond, r_accA], writes=[r_accA])

        def mm():
            for dc in range(KD):
                ins = nc.tensor.matmul(bank6[:, dc:dc + 1], lhsT=accA[:, dc * 128:(dc + 1) * 128], rhs=C.ones_f[:, 0:1],
                                       start=True, stop=True)
            return ins
        S.op("pe", mm, reads=[r_accA, C.r_const], writes=[r_b6])
        S.op("dve", lambda s=s: nc.vector.tensor_tensor(out=C.apart[:, s, :], in0=bank6[:, 0:KD], in1=C.bpart[:, s, :], op=ALU.add),
             reads=[r_b6, r_bpart], writes=[r_part])
    S.dma("sp", out=AGi.rearrange("p (s k) -> p s k", s=5), in_=C.apart[:], reads=[r_part], sem_res=r_part, is_output=True)
    r_ag = Res("ccada")
    S.collective("AllGather", GROUPS, AGi, AGo, r_ag)
    for r in range(4):
        S.dma("sp", out=C.ADA[:, r * 5:(r + 1) * 5, :], in_=AGo[r * 128:(r + 1) * 128, :].rearrange("p (s k) -> p s k", s=5),
              reads=[r_ag], writes=[C.r_ada])
    S.barrier()


def emit_ada(nc, S, C, I):
    scratch = C.hT[:].rearrange("p k t -> p (k t)").bitcast(F32)
    wb = [scratch[:, i * 2048:(i + 1) * 2048] for i in range(3)]
    r_wb = [Res(f"adaw{i}") for i in range(3)]
    accA = scratch[:, 6144:8192]
    r_accA = Res("accA")
    bank6, r_b6 = C.banks[6]
    S.dma("sp", out=C.vecs[:], in_=I.vecs[:, :, :], writes=[C.r_vecs])
    S.dma("sp", out=C.cond[:], in_=I.condT[:, :], writes=[C.r_cond])
    S.op("act", lambda: nc.scalar.activation(out=C.cond[:], in_=C.cond[:], func=AF.Silu), reads=[C.r_cond], writes=[C.r_cond])
    n = 0
    for s in range(NADA):
        if s < 18:
            l, col = s // 9, (s % 9) * 2048
            src = lambda k: I.w_ada[l, k * 128:(k + 1) * 128, col:col + 2048]
            bvec = C.vecs[:, V_BADA + s, :]
        else:
            col = (s - 18) * 2048
            src = lambda k: I.w_ada_kv[k * 128:(k + 1) * 128, col:col + 2048]
            bvec = C.vecs[:, V_BKV + (s - 18), :]
        for k in range(KD):
            w, r_w = wb[n % 3], r_wb[n % 3]
            n += 1
            S.dma("sp", out=w, in_=src(k), writes=[r_w])
            if k == 0:
                S.op("dve", lambda w=w: nc.vector.tensor_scalar(out=accA, in0=w, scalar1=C.cond[:, 0:1], scalar2=None, op0=ALU.mult),
                     reads=[r_w, C.r_cond], writes=[r_accA])
            else:
                S.op("dve", lambda w=w, k=k: nc.vector.scalar_tensor_tensor(
                    out=accA, in0=w, scalar=C.cond[:, k:k + 1], in1=accA, op0=ALU.mult, op1=ALU.add),
                    reads=[r_w, C.r_cond, r_accA], writes=[r_accA])

        def mm():
            for dc in range(KD):
                ins = nc.tensor.matmul(bank6[:, dc:dc + 1], lhsT=accA[:, dc * 128:(dc + 1) * 128], rhs=C.ones_f[:, 0:1],
                                       start=True, stop=True)
            return ins
        S.op("pe", mm, reads=[r_accA, C.r_const], writes=[r_b6])
        S.op("dve", lambda s=s, bvec=bvec: nc.vector.tensor_tensor(out=C.ADA[:, s, :], in0=bank6[:, 0:KD], in1=bvec, op=ALU.add),
             reads=[r_b6, C.r_vecs], writes=[C.r_ada])
    S.barrier()


class IO:
    pass


def build_A(nc, debug=False):
    I = IO()
    I.xin = nc.dram_tensor("xin", [2, 128, KD, TW], F32, kind="ExternalInput").ap()
    I.flag = nc.dram_tensor("flag", [128, 2], F32, kind="ExternalInput").ap()
    I.condT = nc.dram_tensor("condT", [128, KD], F32, kind="ExternalInput").ap()
    I.vecs = nc.dram_tensor("vecs", [128, NVEC, KD], F32, kind="ExternalInput").ap()
    I.w_ada = nc.dram_tensor("w_ada", [2, D, 9 * D], F32, kind="ExternalInput").ap()
    I.w_ada_kv = nc.dram_tensor("w_ada_kv", [D, 2 * D], F32, kind="ExternalInput").ap()
    I.w_ffn_in = nc.dram_tensor("w_ffn_in", [3, D, 2 * FF], F32, kind="ExternalInput").ap()
    I.w_ffn_out = nc.dram_tensor("w_ffn_out", [3, FF, D], F32, kind="ExternalInput").ap()
    I.w_ci = nc.dram_tensor("w_ci", [D, 3 * D], F32, kind="ExternalInput").ap()
    I.w_co = nc.dram_tensor("w_co", [D, D], F32, kind="ExternalInput").ap()
    I.w_q = nc.dram_tensor("w_q", [D, D], F32, kind="ExternalInput").ap()
    I.w_kv = nc.dram_tensor("w_kv", [D, 2 * D], F32, kind="ExternalInput").ap()
    I.w_f = nc.dram_tensor("w_f", [128, KD, NH], F32, kind="ExternalInput").ap()
    I.b_f = nc.dram_tensor("b_f", [NH, 1], F32, kind="ExternalInput").ap()
    O = IO()
    O.xmid = nc.dram_tensor("xmid", [2, 128, KD, TOK], F32, kind="ExternalOutput").ap()
    O.q = nc.dram_tensor("q_o", [2, NH, 128, TOK], BF16, kind="ExternalOutput").ap()
    O.k = nc.dram_tensor("k_o", [2, NH, 128, TOK], BF16, kind="ExternalOutput").ap()
    O.v = nc.dram_tensor("v_o", [2, TOK, D], BF16, kind="ExternalOutput").ap()
    O.l = nc.dram_tensor("l_o", [2, NH, TOK], F32, kind="ExternalOutput").ap()
    O.ada = nc.dram_tensor("ada_o", [128, NADA, KD], F32, kind="ExternalOutput").ap()
    O.qdst = lambda c, h, a, b: O.q[c, h, :, a:b]
    O.kdst = lambda c, h, a, b: O.k[c, h, :, a:b]
    O.ldst = lambda c, a, b: O.l[c, :, a:b]
    O.vstore = lambda S, c, ts, vb, st, r_st: S.dma(
        "sp", out=O.v[c, ts * 128:(ts + 1) * 128, vb * 256:(vb + 1) * 256], in_=st[:, :],
        reads=[r_st], sem_res=r_st, is_output=True)
    if debug:
        O.dbg = nc.dram_tensor("dbg", [3, 128, KD, TW], F32, kind="ExternalOutput").ap()
        O.dbgc = nc.dram_tensor("dbgc", [3, 128, TW], F32, kind="ExternalOutput").ap()
        O.dbgh = nc.dram_tensor("dbgh", [128, KD, TW], BF16, kind="ExternalOutput").ap()

    def dbg_store(S, C, i):
        if debug:
            S.dma("sp", out=O.dbg[i, :, :, :], in_=C.xT[:], reads=[C.r_x[k][t] for k in range(KD) for t in range(3)],
                  sem_res=Res(f"dbg{i}"), is_output=True)

    with ExitStack() as stack:
        S = Sched(nc, stack)
        C = Ctx()
        setup_common(nc, S, stack, C)
        setup_stream_bufs(nc, S, stack, C)
        setup_act_bufs(nc, S, stack, C)
        C.cond = alloc(nc, stack, "cond", [128, KD], F32)
        C.r_cond = Res("cond")
        C.flag = alloc(nc, stack, "flag", [128, 2], F32)
        C.U = [alloc(nc, stack, f"U{i}", [128, TW], F32) for i in range(2)]
        C.r_U = [Res(f"U{i}") for i in range(2)]
        C.BG = [alloc(nc, stack, f"BG{i}", [128, TOK], F32) for i in range(2)]
        C.r_BG = [Res(f"BG{i}") for i in range(2)]
        C.c1 = alloc(nc, stack, "c1", [128, TOK], F32)
        C.r_c1 = Res("c1")
        C.cgh = alloc(nc, stack, "cgh", [128, 2], F32)
        C.r_cgh = Res("cgh")
        C.qst = [alloc(nc, stack, f"qst{i}", [128, 512], BF16) for i in range(3)]
        C.r_qst = [Res(f"qst{i}") for i in range(3)]
        C.vst = [alloc(nc, stack, f"vst{i}", [128, 256], BF16) for i in range(3)]
        C.r_vst = [Res(f"vst{i}") for i in range(3)]
        C.est = [alloc(nc, stack, "est0", [NH, 512], F32)] * 2
        C.r_est = [Res("est0")] * 2
        C.lst = [alloc(nc, stack, f"lst{i}", [NH, 512], F32) for i in range(2)]
        C.r_lst = [Res(f"lst{i}") for i in range(2)]
        C.wff = alloc(nc, stack, "wff", [128, KD, NH], F32)
        C.wfb = alloc(nc, stack, "wfb", [128, KD, NH], BF16)
        C.negb = alloc(nc, stack, "negb", [NH, 1], F32)
        C.r_wf = Res("wf")
        C.r_negb = Res("negb")
        C.proj_ctr = [0]
        C.r_xst = [Res(f"xst{i}") for i in range(4)]

        S.dma("sp", out=C.flag[:], in_=I.flag[:, :], writes=[C.r_vecs])
        S.dma("sp", out=C.wff[:], in_=I.w_f[:, :, :], writes=[C.r_wf])
        S.dma("sp", out=C.negb[:], in_=I.b_f[:, :], writes=[C.r_negb])
        S.op("dve", lambda: nc.vector.tensor_copy(out=C.wfb[:], in_=C.wff[:]), reads=[C.r_wf], writes=[C.r_wf])
        S.op("dve", lambda: nc.vector.tensor_scalar(out=C.negb[:], in0=C.negb[:], scalar1=-1.0, scalar2=None, op0=ALU.mult),
             reads=[C.r_negb], writes=[C.r_negb])
        emit_ada(nc, S, C, I)
        emit_derived(nc, S, C)
        S.dma("sp", out=O.ada[:, :, :], in_=C.ADA[:], reads=[C.r_ada], sem_res=Res("adao"), is_output=True)

        blocks = []
        for c in range(2):
            blocks += ffn_blocks(I.w_ffn_in[0], I.w_ffn_out[0])
            blocks += conv_blocks(I.w_ci, I.w_co)
            blocks += ffn_blocks(I.w_ffn_in[1], I.w_ffn_out[1])
            blocks += ffn_blocks(I.w_ffn_in[2], I.w_ffn_out[2])
            blocks += proj_blocks(I.w_q, 0, D)
            blocks += proj_blocks(I.w_kv, 0, D)
            blocks += proj_blocks(I.w_kv, D, D)
        ws = WStream(S, C.slots, blocks, NSLOT - 2)

        for c in range(1 if debug else 2):
            for k0 in range(0, KD, 4):
                S.dma("sp", out=C.xT[:, k0:k0 + 4, :], in_=I.xin[c, :, k0:k0 + 4, :],
                      writes=[C.r_x[k][t] for k in range(k0, k0 + 4) for t in range(3)])
            emit_rstd(nc, S, C, TILES_H)
            emit_modulate(nc, S, C, TILES_H, C.AMOD[:, 0, :], C.ADA[:, 0, :])
            emit_ffn(nc, S, C, ws, TILES_H, C.GATE[:, 0, :])
            if c == 0:
                dbg_store(S, C, 0)
            emit_rstd(nc, S, C, TILES_H)
            emit_modulate(nc, S, C, TILES_H, C.AMOD[:, 1, :], C.ADA[:, 3, :])
            if debug and c == 0:
                C.dbgc = O.dbgc
                S.dma("sp", out=O.dbgh[:, :, :], in_=C.hT[:], reads=[C.r_h[k][t] for k in range(KD) for t in range(3)],
                      sem_res=Res("dbgh"), is_output=True)
            emit_conv(nc, S, C, ws, c, C.GATE[:, 1, :], C.flag)
            C.dbgc = None
            if c == 0:
                dbg_store(S, C, 1)
            emit_rstd(nc, S, C, TILES_M)
            emit_modulate(nc, S, C, TILES_M, C.AMOD[:, 2, :], C.ADA[:, 6, :])
            emit_ffn(nc, S, C, ws, TILES_M, C.GATE[:, 2, :])
            if c == 0:
                dbg_store(S, C, 2)
            emit_rstd(nc, S, C, TILES_M)
            emit_modulate(nc, S, C, TILES_M, C.AMOD[:, 3, :], C.ADA[:, 9, :])
            emit_ffn(nc, S, C, ws, TILES_M, C.GATE[:, 3, :])
            emit_proj(nc, S, C, ws, c, O)
            for k0 in range(0, KD, 4):
                S.dma("sp", out=O.xmid[c, :, k0:k0 + 4, :], in_=C.xT[:, k0:k0 + 4, 2:TW],
                      reads=[C.r_x[k][t] for k in range(k0, k0 + 4) for t in (1, 2)], sem_res=C.r_xst[k0 // 4], is_output=True)
        S.finish()
    return nc


def _pk(v):
    return np.ascontiguousarray(np.asarray(v, np.float32).reshape(KD, 128).T)


def chunk_of(core, c):
    j = core % 4
    return j if c == 0 else 7 - j


def host_common(inp):
    H = {}
    vec = np.zeros((128, NVEC, KD), np.float32)
    for l in range(2):
        for sub in range(3):
            vec[:, V_NORMG + l * 3 + sub, :] = _pk(inp["norm_g"][l, sub])
    vec[:, V_KVG, :] = _pk(inp["kv_norm_g"])
    vec[:, V_FING, :] = _pk(inp["final_g"])
    for k in range(3):
        vec[:, V_CONVW + k, :] = _pk(inp["conv_w"][0, k])
    vec[:, V_CONVB, :] = _pk(inp["conv_b"][0])
    for l in range(2):
        for s in range(9):
            vec[:, V_BADA + l * 9 + s, :] = _pk(inp["b_ada"][l, s * D:(s + 1) * D])
    for s in range(2):
        vec[:, V_BKV + s, :] = _pk(inp["b_ada_kv"][s * D:(s + 1) * D])
    H["vecs"] = vec

    def perm_in(w):
        return np.ascontiguousarray(w.reshape(D, 2, KF, 128).transpose(0, 2, 1, 3).reshape(D, 2 * FF))
    wfi = np.asarray(inp["w_ffn_in"], np.float32)
    H["w_ffn_in4"] = [perm_in(wfi[0, 0]), perm_in(wfi[0, 1]), perm_in(wfi[1, 0]), perm_in(wfi[1, 1])]
    wci = np.asarray(inp["w_conv_in"], np.float32)[0]
    H["w_ci"] = np.ascontiguousarray(wci.reshape(D, 3, KD, 128).transpose(0, 2, 1, 3).reshape(D, 3 * D))
    wkvf = np.asarray(inp["w_kvf"], np.float32)
    H["w_kv"] = np.ascontiguousarray(wkvf[:, :2 * D])
    H["w_f"] = np.ascontiguousarray(wkvf[:, 2 * D:].reshape(KD, 128, NH).transpose(1, 0, 2))
    H["b_f"] = np.ascontiguousarray(np.asarray(inp["b_fgate"], np.float32).reshape(NH, 1))
    return H


def host_in_A(inp, H):
    x = np.asarray(inp["x"], np.float32)
    cvec = np.asarray(inp["c"], np.float32)
    wfo = np.asarray(inp["w_ffn_out"], np.float32)
    shared = {
        "vecs": H["vecs"],
        "w_ada": np.asarray(inp["w_ada"], np.float32),
        "w_ada_kv": np.asarray(inp["w_ada_kv"], np.float32),
        "w_ffn_in": np.stack(H["w_ffn_in4"][:3]),
        "w_ffn_out": np.ascontiguousarray(np.stack([wfo[0, 0], wfo[0, 1], wfo[1, 0]])),
        "w_ci": H["w_ci"],
        "w_co": np.asarray(inp["w_conv_out"], np.float32)[0],
        "w_q": np.asarray(inp["w_q"], np.float32)[0],
        "w_kv": H["w_kv"],
        "w_f": H["w_f"],
        "b_f": H["b_f"],
    }
    maps = []
    for core in range(8):
        b = core // 4
        xin = np.zeros((2, 128, KD, TW), np.float32)
        flag = np.zeros((128, 2), np.float32)
        for c in range(2):
            ci = chunk_of(core, c)
            lo = ci * TOK - HALO
            seg = np.zeros((TW, D), np.float32)
            if lo < 0:
                seg[HALO:] = x[b, 0:TOK]
            else:
                seg[:] = x[b, lo:lo + TW]
                flag[:, c] = 1.0
            xin[c] = seg.T.reshape(KD, 128, TW).transpose(1, 0, 2)
        m = dict(shared)
        m["xin"] = xin
        m["flag"] = flag
        m["condT"] = _pk(cvec[b])
        maps.append(m)
    return maps


def run_prog(build_fn, in_maps):
    nc = bass.Bass("TRN2", target_bir_lowering=False)
    build_fn(nc)
    res = run_bass_kernel_spmd(nc, in_maps, core_ids=list(range(8)))
    return res.results


HPC = 4
NQT = SEQ // 512
NKC = SEQ // 128
ATT_SCALE = 1.0 / float(np.sqrt(128.0))


def build_B1(nc):
    qT = nc.dram_tensor("qT", [HPC, 128, SEQ], BF16, kind="ExternalInput").ap()
    kT = nc.dram_tensor("kT", [HPC, 128, SEQ], BF16, kind="ExternalInput").ap()
    vv = nc.dram_tensor("vv", [HPC, SEQ, 128], BF16, kind="ExternalInput").ap()
    ll = nc.dram_tensor("ll", [HPC, SEQ], F32, kind="ExternalInput").ap()
    mask_d = nc.dram_tensor("mask", [128, 4, 512], F32, kind="ExternalInput").ap()
    sel_d = nc.dram_tensor("sel", [HPC, HPC + 1, 128], F32, kind="ExternalInput").ap()
    oT = nc.dram_tensor("oT", [HPC, 128, SEQ], BF16, kind="ExternalOutput").ap()
    with ExitStack() as stack:
        S = Sched(nc, stack)
        C = Ctx()
        setup_common(nc, S, stack, C)
        KT = [alloc(nc, stack, f"KT{i}", [128, SEQ], BF16) for i in range(2)]
        QT = [alloc(nc, stack, f"QT{i}", [128, SEQ], BF16) for i in range(2)]
        VV = [alloc(nc, stack, f"VV{i}", [128, NKC, 128], BF16) for i in range(2)]
        r_K = [Res(f"KT{i}") for i in range(2)]
        r_Q = [Res(f"QT{i}") for i in range(2)]
        r_V = [Res(f"VV{i}") for i in range(2)]
        Lrow = alloc(nc, stack, "Lrow", [HPC, SEQ], F32)
        r_L = Res("Lrow")
        onesr = alloc(nc, stack, "onesr", [HPC, SEQ], F32)
        sel = alloc(nc, stack, "sel", [HPC, HPC + 1, 128], F32)
        r_sel = Res("sel")
        LcT = alloc(nc, stack, "LcT", [128, HPC, NKC], F32)
        r_LcT = Res("LcT")
        MASK = alloc(nc, stack, "MASK", [128, 4, 512], F32)
        r_MASK = Res("MASK")
        FQ = [alloc(nc, stack, f"FQ{i}", [128, 5, 512], F32) for i in range(2)]
        r_FQ = [Res(f"FQ{i}") for i in range(2)]
        TMP = [alloc(nc, stack, f"atmp{i}", [128, 512], F32) for i in range(3)]
        r_TMP = [Res(f"atmp{i}") for i in range(3)]
        PP = [alloc(nc, stack, f"P{i}", [128, 512], BF16) for i in range(3)]
        r_PP = [Res(f"P{i}") for i in range(3)]
        RINV = alloc(nc, stack, "rinv", [128, 512], F32)
        r_RINV = Res("rinv")
        OST = [alloc(nc, stack, f"ost{i}", [128, 512], BF16) for i in range(2)]
        r_OST = [Res(f"ost{i}") for i in range(2)]

        S.dma("sp", out=Lrow[:], in_=ll[:, :], writes=[r_L])
        S.dma("sp", out=sel[:], in_=sel_d[:, :, :], writes=[r_sel])
        S.dma("sp", out=MASK[:], in_=mask_d[:, :, :], writes=[r_MASK])
        S.op("dve", lambda: nc.vector.memset(onesr[:], 1.0), writes=[r_sel])
        S.op("dve", lambda: nc.vector.tensor_tensor_scan(
            out=Lrow[:, :], data0=onesr[:, :], data1=Lrow[:, :], initial=0.0, op0=ALU.mult, op1=ALU.add),
            reads=[r_L, r_sel], writes=[r_L])
        bank7, r_b7 = C.banks[7]
        for kc0 in range(0, NKC, 32):
            def mm(kc0=kc0):
                for kc in range(kc0, kc0 + 32):
                    ins = nc.tensor.matmul(bank7[:, (kc - kc0) * 4:(kc - kc0) * 4 + 4], lhsT=Lrow[:, kc * 128:(kc + 1) * 128],
                                           rhs=sel[:, HPC, 0:HPC], start=True, stop=True)
                return ins
            S.op("pe", mm, reads=[r_L, r_sel], writes=[r_b7])
            S.op("dve", lambda kc0=kc0: nc.vector.tensor_copy(
                out=LcT[:, :, kc0:kc0 + 32], in_=bank7[:, 0:128].rearrange("p (c h) -> p h c", h=HPC)),
                reads=[r_b7], writes=[r_LcT])

        ev = 0
        for h in range(HPC):
            hb = h % 2
            for a in range(0, SEQ, 2048):
                S.dma("sp", out=KT[hb][:, a:a + 2048], in_=kT[h, :, a:a + 2048], writes=[r_K[hb]])
                S.dma("sp", out=QT[hb][:, a:a + 2048], in_=qT[h, :, a:a + 2048], writes=[r_Q[hb]])
                S.dma("sp", out=VV[hb][:, a // 128:(a + 2048) // 128, :],
                      in_=vv[h, a:a + 2048, :].rearrange("(c p) d -> p c d", p=128), writes=[r_V[hb]])
            for qt in range(NQT):
                fq, r_fq = FQ[ev % 2], r_FQ[ev % 2]
                ob, r_ob = C.banks[3 + ev % 2]
                rb, r_rb = C.banks[5 + ev % 2]
                ost, r_ost = OST[ev % 2], r_OST[ev % 2]
                ev += 1
                q0 = qt * 512
                S.op("pe", lambda: nc.tensor.matmul(bank7[:, :], lhsT=sel[:, h, :], rhs=Lrow[:, q0:q0 + 512], start=True, stop=True),
                     reads=[r_L, r_sel], writes=[r_b7])
                S.op("dve", lambda fq=fq: nc.vector.tensor_copy(out=fq[:, 4, :], in_=bank7[:, :]), reads=[r_b7], writes=[r_fq])
                for i in range(4):
                    S.op("dve", lambda fq=fq, i=i: nc.vector.tensor_tensor(out=fq[:, i, :], in0=bank7[:, :], in1=MASK[:, i, :], op=ALU.add),
                         reads=[r_b7, r_MASK], writes=[r_fq])
                nkc = 4 * (qt + 1)

                def emit_S(kc):
                    sb, r_sb = C.banks[kc % 3]
                    S.op("pe", lambda: nc.tensor.matmul(sb[:, :], lhsT=KT[hb][:, kc * 128:(kc + 1) * 128], rhs=QT[hb][:, q0:q0 + 512],
                                                        start=True, stop=True), reads=[r_K[hb], r_Q[hb]], writes=[r_sb])
                emit_S(0)
                emit_S(1)
                for kc in range(nkc):
                    sb, r_sb = C.banks[kc % 3]
                    tmp, r_tmp = TMP[kc % 3], r_TMP[kc % 3]
                    pp, r_pp = PP[kc % 3], r_PP[kc % 3]
                    fi = (kc - 4 * qt) if kc >= 4 * qt else 4
                    S.op("dve", lambda: nc.vector.scalar_tensor_tensor(
                        out=tmp[:, :], in0=sb[:, :], scalar=ATT_SCALE, in1=fq[:, fi, :], op0=ALU.mult, op1=ALU.add),
                        reads=[r_sb, r_fq], writes=[r_tmp])
                    S.op("act", lambda: nc.scalar.activation(out=pp[:, :], in_=tmp[:, :], func=AF.Exp, bias=LcT[:, h, kc:kc + 1], scale=1.0),
                         reads=[r_tmp, r_LcT], writes=[r_pp])
                    if kc + 2 < nkc:
                        emit_S(kc + 2)

                    def mm():
                        nc.tensor.matmul(ob[:, :], lhsT=VV[hb][:, kc, :], rhs=pp[:, :], start=(kc == 0), stop=(kc == nkc - 1))
                        return nc.tensor.matmul(rb[:, :], lhsT=C.ones_b[:, :], rhs=pp[:, :], start=(kc == 0), stop=(kc == nkc - 1))
                    S.op("pe", mm, reads=[r_V[hb], r_pp, C.r_const], writes=[r_ob, r_rb])
                S.op("dve", lambda: nc.vector.reciprocal(out=RINV[:, :], in_=rb[:, :]), reads=[r_rb], writes=[r_RINV])
                S.op("dve", lambda: nc.vector.tensor_tensor(out=ost[:, :], in0=ob[:, :], in1=RINV[:, :], op=ALU.mult),
                     reads=[r_ob, r_RINV], writes=[r_ost])
                S.dma("sp", out=oT[h, :, q0:q0 + 512], in_=ost[:, :], reads=[r_ost], sem_res=r_ost, is_output=True)
        S.finish()
    return nc


def build_B2(nc):
    I = IO()
    I.xmid = nc.dram_tensor("xmid_in", [2, 128, KD, TOK], F32, kind="ExternalInput").ap()
    I.oT = nc.dram_tensor("oT_in", [2, NH, 128, TOK], BF16, kind="ExternalInput").ap()
    I.ada = nc.dram_tensor("ada_in", [128, NADA, KD], F32, kind="ExternalInput").ap()
    I.vecs = nc.dram_tensor("vecs", [128, NVEC, KD], F32, kind="ExternalInput").ap()
    I.w_o = nc.dram_tensor("w_o", [D, D], F32, kind="ExternalInput").ap()
    I.w_ffn_in = nc.dram_tensor("w_ffn_in", [D, 2 * FF], F32, kind="ExternalInput").ap()
    I.w_ffn_out = nc.dram_tensor("w_ffn_out", [FF, D], F32, kind="ExternalInput").ap()
    out = nc.dram_tensor("out", [2, 128, KD, TOK], F32, kind="ExternalOutput").ap()
    with ExitStack() as stack:
        S = Sched(nc, stack)
        C = Ctx()
        setup_common(nc, S, stack, C)
        setup_stream_bufs(nc, S, stack, C)
        setup_act_bufs(nc, S, stack, C)
        S.dma("sp", out=C.vecs[:], in_=I.vecs[:, :, :], writes=[C.r_vecs])
        S.dma("sp", out=C.ADA[:], in_=I.ada[:, :, :], writes=[C.r_ada])
        emit_derived(nc, S, C)
        blocks = []
        for c in range(2):
            blocks += wo_blocks(I.w_o)
            blocks += ffn_blocks(I.w_ffn_in, I.w_ffn_out)
        ws = WStream(S, C.slots, blocks, NSLOT - 2)
        fg = C.vecs[:, V_FING, :]
        for c in range(2):
            for k0 in range(0, KD, 4):
                S.dma("sp", out=C.xT[:, k0:k0 + 4, 2:TW], in_=I.xmid[c, :, k0:k0 + 4, :],
                      writes=[C.r_x[k][t] for k in range(k0, k0 + 4) for t in (1, 2)])
            octr = [0]
            for g in range(KD // GRP):
                gb, r_gb = C.gT[g % 2], C.r_g[g % 2]
                S.dma("sp", out=gb[:, :, 2:TW], in_=I.oT[c, g * GRP:(g + 1) * GRP, :, :].rearrange("h p t -> p h t"),
                      writes=[r_gb[1], r_gb[2]])
                for o in range(8):
                    slot, r_slot = ws.next()
                    emit_out_block(nc, S, C, slot, r_slot, o, gb, r_gb, TILES_M, C.GATE[:, 4, :], octr)
            emit_rstd(nc, S, C, TILES_M)
            emit_modulate(nc, S, C, TILES_M, C.AMOD[:, 5, :], C.ADA[:, 15, :])
            emit_ffn(nc, S, C, ws, TILES_M, C.GATE[:, 5, :])
            emit_rstd(nc, S, C, TILES_M)
            for (t0, t1) in TILES_M:
                ti = tile_idx(t0)
                for k in range(KD):
                    tmp, r_tmp = next_tmp(C)
                    S.op("dve", lambda k=k, tmp=tmp, t0=t0, t1=t1: nc.vector.scalar_tensor_tensor(
                        out=tmp[:, :], in0=C.xT[:, k, t0:t1], scalar=fg[:, k:k + 1], in1=C.rstd[:, t0:t1],
                        op0=ALU.mult, op1=ALU.mult), reads=[C.r_x[k][ti], C.r_rstd[ti], C.r_vecs], writes=[r_tmp])
                    S.dma("sp", out=out[c, :, k, t0 - 2:t1 - 2], in_=tmp[:, :], reads=[r_tmp], sem_res=r_tmp, is_output=True)
        S.finish()
    return nc


_CAP = None


def kernel(**inp):
    inp = {k: np.asarray(v) for k, v in inp.items()}
    H = host_common(inp)
    resA = run_prog(build_A, host_in_A(inp, H))
    bf = ml_dtypes.bfloat16
    Q = np.zeros((2, NH, 128, SEQ), bf)
    Kf = np.zeros((2, NH, 128, SEQ), bf)
    V = np.zeros((2, SEQ, D), bf)
    L = np.zeros((2, NH, SEQ), np.float32)
    for core in range(8):
        b = core // 4
        for c in range(2):
            t0 = chunk_of(core, c) * TOK
            Q[b, :, :, t0:t0 + TOK] = np.asarray(resA[core]["q_o"])[c]
            Kf[b, :, :, t0:t0 + TOK] = np.asarray(resA[core]["k_o"])[c]
            V[b, t0:t0 + TOK, :] = np.asarray(resA[core]["v_o"])[c]
            L[b, :, t0:t0 + TOK] = np.asarray(resA[core]["l_o"])[c]
    mask = np.zeros((128, 4, 512), np.float32)
    pidx = np.arange(128)[:, None]
    tidx = np.arange(512)[None, :]
    for i in range(4):
        mask[:, i, :] = np.where(128 * i + pidx <= tidx, 0.0, -30000.0)
    sel = np.zeros((HPC, HPC + 1, 128), np.float32)
    for h in range(HPC):
        sel[h, h, :] = -1.0
        sel[h, HPC, h] = 1.0
    mapsB1 = []
    for core in range(8):
        b, j = core // 4, core % 4
        hs = slice(HPC * j, HPC * (j + 1))
        mapsB1.append({
            "qT": np.ascontiguousarray(Q[b, hs]), "kT": np.ascontiguousarray(Kf[b, hs]),
            "vv": np.ascontiguousarray(V[b].reshape(SEQ, NH, 128)[:, hs, :].transpose(1, 0, 2)),
            "ll": np.ascontiguousarray(L[b, hs]), "mask": mask, "sel": sel})
    resB1 = run_prog(build_B1, mapsB1)
    if _CAP is not None:
        _CAP.update(resA=resA, Q=Q, K=Kf, V=V, L=L, resB1=resB1)
    OT = np.zeros((2, NH, 128, SEQ), bf)
    for core in range(8):
        b, j = core // 4, core % 4
        OT[b, HPC * j:HPC * (j + 1)] = np.asarray(resB1[core]["oT"])
    wfo = np.asarray(inp["w_ffn_out"], np.float32)
    mapsB2 = []
    for core in range(8):
        b = core // 4
        oin = np.stack([OT[b, :, :, chunk_of(core, c) * TOK:(chunk_of(core, c) + 1) * TOK] for c in range(2)])
        mapsB2.append({
            "xmid_in": np.asarray(resA[core]["xmid"]), "oT_in": np.ascontiguousarray(oin),
            "ada_in": np.asarray(resA[core]["ada_o"]), "vecs": H["vecs"],
            "w_o": np.asarray(inp["w_o"], np.float32)[0], "w_ffn_in": H["w_ffn_in4"][3],
            "w_ffn_out": np.ascontiguousarray(wfo[1, 1])})
    resB2 = run_prog(build_B2, mapsB2)
    outp = np.zeros((2, SEQ, D), np.float32)
    for core in range(8):
        b = core // 4
        for c in range(2):
            t0 = chunk_of(core, c) * TOK
            o = np.asarray(resB2[core]["out"])[c]
            outp[b, t0:t0 + TOK, :] = o.transpose(2, 1, 0).reshape(TOK, D)
    return outp


GROWS = 6144
GROUPS = [[0, 1, 2, 3], [4, 5, 6, 7]]
PROWS = 512
NPIECE = GROWS // PROWS
IX_Q, IX_K, IX_V, IX_O, IX_N = 0, 16, 32, 160, 192


DEBUG_FUSED = False


def build_fused(nc):
    I = IO()
    I.xin = nc.dram_tensor("xin", [2, 128, KD, TW], F32, kind="ExternalInput").ap()
    I.flag = nc.dram_tensor("flag", [128, 2], F32, kind="ExternalInput").ap()
    I.condT = nc.dram_tensor("condT", [128, KD], F32, kind="ExternalInput").ap()
    I.vecs = nc.dram_tensor("vecs", [128, NVEC, KD], F32, kind="ExternalInput").ap()
    I.w_ada_part = nc.dram_tensor("w_ada_part", [5, D, D], F32, kind="ExternalInput").ap()
    I.bada_part = nc.dram_tensor("bada_part", [128, 5, KD], F32, kind="ExternalInput").ap()
    I.w_ffn_in = nc.dram_tensor("w_ffn_in", [4, D, 2 * FF], F32, kind="ExternalInput").ap()
    I.w_ffn_out = nc.dram_tensor("w_ffn_out", [4, FF, D], F32, kind="ExternalInput").ap()
    I.w_ci = nc.dram_tensor("w_ci", [D, 3 * D], F32, kind="ExternalInput").ap()
    I.w_co = nc.dram_tensor("w_co", [D, D], F32, kind="ExternalInput").ap()
    I.w_q = nc.dram_tensor("w_q", [D, D], F32, kind="ExternalInput").ap()
    I.w_kv = nc.dram_tensor("w_kv", [D, 2 * D], F32, kind="ExternalInput").ap()
    I.w_o = nc.dram_tensor("w_o", [D, D], F32, kind="ExternalInput").ap()
    I.w_f = nc.dram_tensor("w_f", [128, KD, NH], F32, kind="ExternalInput").ap()
    I.b_f = nc.dram_tensor("b_f", [NH, 1], F32, kind="ExternalInput").ap()
    I.mask = nc.dram_tensor("mask", [128, 4, 512], F32, kind="ExternalInput").ap()
    I.sel = nc.dram_tensor("sel", [NH, HPC + 1, 128], F32, kind="ExternalInput").ap()
    I.idx = nc.dram_tensor("idx", [128, IX_N], mybir.dt.int32, kind="ExternalInput").ap()
    out = nc.dram_tensor("out", [2, 128, KD, TOK], F32, kind="ExternalOutput").ap()
    G = [nc.dram_tensor(f"G{c}", [GROWS, TOK], BF16).ap() for c in range(2)]
    GO = [nc.dram_tensor(f"GO{c}", [NPIECE * 4 * PROWS, TOK], BF16).ap() for c in range(2)]
    GL = [nc.dram_tensor(f"GL{c}", [NH, TOK], F32).ap() for c in range(2)]
    GOL = [nc.dram_tensor(f"GOL{c}", [4 * NH, TOK], F32).ap() for c in range(2)]
    G2 = nc.dram_tensor("G2", [8 * 512, TOK], BF16).ap()
    GO2 = nc.dram_tensor("GO2", [8 * 4 * PROWS, TOK], BF16).ap()
    xmid_d = nc.dram_tensor("xmid_d", [2, 128, KD, TOK], F32).ap()
    AGi = nc.dram_tensor("AGi", [128, 5 * KD], F32).ap()
    AGo = nc.dram_tensor("AGo", [4 * 128, 5 * KD], F32).ap()
    if DEBUG_FUSED:
        dbgL = nc.dram_tensor("dbgL", [NH, SEQ], F32, kind="ExternalOutput").ap()
        dbgO = nc.dram_tensor("dbgO", [HPC, 128, SEQ], BF16, kind="ExternalOutput").ap()
        dbgA = nc.dram_tensor("dbgA", [128, NADA, KD], F32, kind="ExternalOutput").ap()

    O = IO()
    O.qdst = lambda c, h, a, b: G[c][h * 128:(h + 1) * 128, a:b]
    O.kdst = lambda c, h, a, b: G[c][2048 + h * 128:2048 + (h + 1) * 128, a:b]
    O.ldst = lambda c, a, b: GL[c][:, a:b]
    def vstore(S, c, ts, vb, st, r_st):
        for hh in range(2):
            h = 2 * vb + hh
            S.dma("sp", out=G[c][4096 + h * 128:4096 + (h + 1) * 128, ts * 128:(ts + 1) * 128], in_=st[:, hh * 128:(hh + 1) * 128],
                  reads=[r_st], sem_res=r_st, is_output=True)
    O.vstore = vstore

    def gather_pieces(S, c, lo, hi, r):
        for i in range(lo, hi):
            S.defer_collective("AllGather", GROUPS, G[c][i * PROWS:(i + 1) * PROWS, :], GO[c][i * 4 * PROWS:(i + 1) * 4 * PROWS, :], r)

    with ExitStack() as top:
        S = Sched(nc, top)
        C = Ctx()
        setup_common(nc, S, top, C)
        setup_small(nc, S, top, C)
        IDX = alloc(nc, top, "IDX", [128, IX_N], mybir.dt.int32)
        r_IDX = Res("IDX")
        S.dma("sp", out=IDX[:], in_=I.idx[:, :], writes=[r_IDX])
        r_cc = [Res("cc0"), Res("cc1"), Res("ccl0"), Res("ccl1"), Res("cc2")]
        O.after = lambda S, stage, c: gather_pieces(S, c, {"q": 0, "k": 4, "v": 8}[stage], {"q": 4, "k": 8, "v": 12}[stage], r_cc[c])

        with ExitStack() as stack:
            setup_stream_bufs(nc, S, stack, C)
            setup_act_bufs(nc, S, stack, C)
            C.cond = alloc(nc, stack, "cond", [128, KD], F32)
            C.r_cond = Res("cond")
            C.flag = alloc(nc, stack, "flag", [128, 2], F32)
            C.U = [alloc(nc, stack, f"U{i}", [128, TW], F32) for i in range(2)]
            C.r_U = [Res(f"U{i}") for i in range(2)]
            C.BG = [alloc(nc, stack, f"BG{i}", [128, TOK], F32) for i in range(2)]
            C.r_BG = [Res(f"BG{i}") for i in range(2)]
            C.c1 = alloc(nc, stack, "c1", [128, TOK], F32)
            C.r_c1 = Res("c1")
            C.cgh = alloc(nc, stack, "cgh", [128, 2], F32)
            C.r_cgh = Res("cgh")
            C.qst = [alloc(nc, stack, f"qst{i}", [128, 512], BF16) for i in range(3)]
            C.r_qst = [Res(f"qst{i}") for i in range(3)]
            C.vst = [alloc(nc, stack, f"vst{i}", [128, 256], BF16) for i in range(3)]
            C.r_vst = [Res(f"vst{i}") for i in range(3)]
            C.est = [alloc(nc, stack, "est0", [NH, 512], F32)] * 2
            C.r_est = [Res("est0")] * 2
            C.lst = [alloc(nc, stack, f"lst{i}", [NH, 512], F32) for i in range(2)]
            C.r_lst = [Res(f"lst{i}") for i in range(2)]
            C.wff = alloc(nc, stack, "wff", [128, KD, NH], F32)
            C.wfb = alloc(nc, stack, "wfb", [128, KD, NH], BF16)
            C.negb = alloc(nc, stack, "negb", [NH, 1], F32)
            C.r_wf = Res("wf")
            C.r_negb = Res("negb")
            C.proj_ctr = [0]
            C.r_xst = [Res(f"xst{i}") for i in range(4)]
            S.dma("sp", out=C.flag[:], in_=I.flag[:, :], writes=[C.r_vecs])
            S.dma("sp", out=C.wff[:], in_=I.w_f[:, :, :], writes=[C.r_wf])
            S.dma("sp", out=C.negb[:], in_=I.b_f[:, :], writes=[C.r_negb])
            S.op("dve", lambda: nc.vector.tensor_copy(out=C.wfb[:], in_=C.wff[:]), reads=[C.r_wf], writes=[C.r_wf])
            S.op("dve", lambda: nc.vector.tensor_scalar(out=C.negb[:], in0=C.negb[:], scalar1=-1.0, scalar2=None, op0=ALU.mult),
                 reads=[C.r_negb], writes=[C.r_negb])
            C.apart = alloc(nc, stack, "apart", [128, 5, KD], F32)
            C.bpart = alloc(nc, stack, "bpart", [128, 5, KD], F32)
            emit_ada_sharded(nc, S, C, I, AGi, AGo)
            if DEBUG_FUSED:
                S.dma("sp", out=dbgA[:, :, :], in_=C.ADA[:], reads=[C.r_ada], sem_res=Res("dbgA"), is_output=True)
            emit_derived(nc, S, C)
            blocks = []
            for c in range(2):
                blocks += ffn_blocks(I.w_ffn_in[0], I.w_ffn_out[0])
                blocks += conv_blocks(I.w_ci, I.w_co)
                blocks += ffn_blocks(I.w_ffn_in[1], I.w_ffn_out[1])
                blocks += ffn_blocks(I.w_ffn_in[2], I.w_ffn_out[2])
                blocks += proj_blocks(I.w_q, 0, D)
                blocks += proj_blocks(I.w_kv, 0, D)
                blocks += proj_blocks(I.w_kv, D, D)
            ws = WStream(S, C.slots, blocks, NSLOT - 2)
            for c in range(2):
                for k0 in range(0, KD, 4):
                    S.dma("sp", out=C.xT[:, k0:k0 + 4, :], in_=I.xin[c, :, k0:k0 + 4, :],
                          writes=[C.r_x[k][t] for k in range(k0, k0 + 4) for t in range(3)])
                emit_rstd(nc, S, C, TILES_H)
                emit_modulate(nc, S, C, TILES_H, C.AMOD[:, 0, :], C.ADA[:, 0, :])
                emit_ffn(nc, S, C, ws, TILES_H, C.GATE[:, 0, :])
                emit_rstd(nc, S, C, TILES_H)
                emit_modulate(nc, S, C, TILES_H, C.AMOD[:, 1, :], C.ADA[:, 3, :])
                emit_conv(nc, S, C, ws, c, C.GATE[:, 1, :], C.flag)
                emit_rstd(nc, S, C, TILES_M)
                emit_modulate(nc, S, C, TILES_M, C.AMOD[:, 2, :], C.ADA[:, 6, :])
                emit_ffn(nc, S, C, ws, TILES_M, C.GATE[:, 2, :])
                emit_rstd(nc, S, C, TILES_M)
                emit_modulate(nc, S, C, TILES_M, C.AMOD[:, 3, :], C.ADA[:, 9, :])
                emit_ffn(nc, S, C, ws, TILES_M, C.GATE[:, 3, :])
                S.pump_every = 2 if c == 1 else 4
                emit_proj(nc, S, C, ws, c, O)
                S.pump_every = 4
                for k0 in range(0, KD, 4):
                    S.dma("sp", out=xmid_d[c, :, k0:k0 + 4, :], in_=C.xT[:, k0:k0 + 4, 2:TW],
                          reads=[C.r_x[k][t] for k in range(k0, k0 + 4) for t in (1, 2)], sem_res=C.r_xst[k0 // 4], is_output=True)
                S.defer_collective("AllGather", GROUPS, GL[c], GOL[c], r_cc[2 + c], front=True)
            S.flush_collectives()
            S.drain()

        with ExitStack() as stack:
            KT = [alloc(nc, stack, f"KT{i}", [128, SEQ], BF16) for i in range(2)]
            QT = [alloc(nc, stack, f"QT{i}", [128, SEQ], BF16) for i in range(2)]
            VV = [alloc(nc, stack, f"VV{i}", [128, NKC, 128], BF16) for i in range(2)]
            r_K = [[Res(f"KT{i}")] * 8 for i in range(2)]
            r_Q = [[Res(f"QT{i}")] * 8 for i in range(2)]
            r_V = [[Res(f"VV{i}")] * 8 for i in range(2)]
            Lrow = alloc(nc, stack, "Lrow", [NH, SEQ], F32)
            r_L = Res("Lrow")
            onesr = alloc(nc, stack, "onesr", [NH, 1024], F32)
            sel = alloc(nc, stack, "sel", [NH, HPC + 1, 128], F32)
            r_sel = Res("sel")
            LcT = alloc(nc, stack, "LcT", [128, HPC, NKC], F32)
            r_LcT = Res("LcT")
            MASK = alloc(nc, stack, "MASK", [128, 4, 512], F32)
            r_MASK = Res("MASK")
            FQ = [alloc(nc, stack, f"FQ{i}", [128, 5, 512], F32) for i in range(2)]
            r_FQ = [Res(f"FQ{i}") for i in range(2)]
            TMP = [alloc(nc, stack, f"atmp{i}", [128, 512], F32) for i in range(4)]
            r_TMP = [Res(f"atmp{i}") for i in range(4)]
            PP = [alloc(nc, stack, f"P{i}", [128, 512], BF16) for i in range(4)]
            r_PP = [Res(f"P{i}") for i in range(4)]
            RINV = alloc(nc, stack, "rinv", [128, 512], F32)
            r_RINV = Res("rinv")
            OST = [alloc(nc, stack, f"ost{i}", [128, 512], BF16) for i in range(2)]
            r_OST = [Res(f"ost{i}") for i in range(2)]
            GOv = [GO[c].rearrange("r (a d) -> (r a) d", d=128) for c in range(2)]
            chunk = lambda r, c: (r if c == 0 else 7 - r)

            S.dma("sp", out=sel[:], in_=I.sel[:, :, :], writes=[r_sel])
            S.dma("sp", out=MASK[:], in_=I.mask[:, :, :], writes=[r_MASK])
            S.op("dve", lambda: nc.vector.memset(onesr[:], 1.0), writes=[r_sel])
            for c in range(2):
                for r in range(4):
                    tb = chunk(r, c) * TOK
                    S.dma("sp", out=Lrow[:, tb:tb + TOK], in_=GOL[c][r * NH:(r + 1) * NH, :], reads=[r_cc[2 + c]], writes=[r_L])
            for sg in range(SEQ // 1024):
                a, b = sg * 1024, (sg + 1) * 1024
                init = 0.0 if sg == 0 else Lrow[:, a - 1:a]
                S.fence("dve")
                S.op("dve", lambda a=a, b=b, init=init: nc.vector.tensor_tensor_scan(
                    out=Lrow[:, a:b], data0=onesr[:, :], data1=Lrow[:, a:b], initial=init, op0=ALU.mult, op1=ALU.add),
                    reads=[r_L, r_sel], writes=[r_L])
            if DEBUG_FUSED:
                S.dma("sp", out=dbgL[:, :], in_=Lrow[:, :], reads=[r_L], sem_res=Res("dbgL"), is_output=True)
            bank7, r_b7 = C.banks[7]
            for kc0 in range(0, NKC, 32):
                def mm(kc0=kc0):
                    for kc in range(kc0, kc0 + 32):
                        ins = nc.tensor.matmul(bank7[:, (kc - kc0) * 4:(kc - kc0) * 4 + 4], lhsT=Lrow[:, kc * 128:(kc + 1) * 128],
                                               rhs=sel[:, HPC, 0:HPC], start=True, stop=True)
                    return ins
                S.op("pe", mm, reads=[r_L, r_sel], writes=[r_b7])
                S.op("dve", lambda kc0=kc0: nc.vector.tensor_copy(
                    out=LcT[:, :, kc0:kc0 + 32], in_=bank7[:, 0:128].rearrange("p (c h) -> p h c", h=HPC)),
                    reads=[r_b7], writes=[r_LcT])

            ev = 0
            for h in range(HPC):
                hb = h % 2
                for blk in range(8):
                    c, r = (0, blk) if blk < 4 else (1, 7 - blk)
                    tb = blk * TOK
                    col = h * 4 + r
                    S.gather(KT[hb][:, tb:tb + TOK], GO[c][:, :], IDX[:, IX_K + col:IX_K + col + 1],
                             reads=[r_cc[c], r_IDX], writes=[r_K[hb][blk]])
                    S.gather(QT[hb][:, tb:tb + TOK], GO[c][:, :], IDX[:, IX_Q + col:IX_Q + col + 1],
                             reads=[r_cc[c], r_IDX], writes=[r_Q[hb][blk]])
                    S.gather(VV[hb][:, blk * 8:(blk + 1) * 8, :].rearrange("p a d -> p (a d)"), GO[c][:, :],
                             IDX[:, IX_V + col:IX_V + col + 1], reads=[r_cc[c], r_IDX], writes=[r_V[hb][blk]])
                for qt in range(NQT):
                    fq, r_fq = FQ[ev % 2], r_FQ[ev % 2]
                    ob, r_ob = C.banks[4 + ev % 2]
                    rb, r_rb = C.banks[6 + ev % 2]
                    ost, r_ost = OST[ev % 2], r_OST[ev % 2]
                    ev += 1
                    q0 = qt * 512
                    fqb, r_fqb = C.banks[3]
                    S.op("pe", lambda: nc.tensor.matmul(fqb[:, :], lhsT=sel[:, h, :], rhs=Lrow[:, q0:q0 + 512], start=True, stop=True),
                         reads=[r_L, r_sel], writes=[r_fqb])
                    S.op("dve", lambda fq=fq: nc.vector.tensor_copy(out=fq[:, 4, :], in_=fqb[:, :]), reads=[r_fqb], writes=[r_fq])
                    for i in range(4):
                        S.op("dve", lambda fq=fq, i=i: nc.vector.tensor_tensor(out=fq[:, i, :], in0=fqb[:, :], in1=MASK[:, i, :], op=ALU.add),
                             reads=[r_fqb, r_MASK], writes=[r_fq])
                    nkc = 4 * (qt + 1)

                    def emit_S(kc):
                        sb, r_sb = C.banks[kc % 4]
                        S.op("pe", lambda: nc.tensor.matmul(sb[:, :], lhsT=KT[hb][:, kc * 128:(kc + 1) * 128], rhs=QT[hb][:, q0:q0 + 512],
                                                            start=True, stop=True), reads=[r_K[hb][kc // 8], r_Q[hb][qt // 2]], writes=[r_sb])
                    emit_S(0)
                    emit_S(1)
                    emit_S(2)
                    for kc in range(nkc):
                        sb, r_sb = C.banks[kc % 4]
                        tmp, r_tmp = TMP[kc % 4], r_TMP[kc % 4]
                        pp, r_pp = PP[kc % 4], r_PP[kc % 4]
                        fi = (kc - 4 * qt) if kc >= 4 * qt else 4
                        S.op("dve", lambda: nc.vector.scalar_tensor_tensor(
                            out=tmp[:, :], in0=sb[:, :], scalar=ATT_SCALE, in1=fq[:, fi, :], op0=ALU.mult, op1=ALU.add),
                            reads=[r_sb, r_fq], writes=[r_tmp])
                        S.op("act", lambda: nc.scalar.activation(out=pp[:, :], in_=tmp[:, :], func=AF.Exp, bias=LcT[:, h, kc:kc + 1], scale=1.0),
                             reads=[r_tmp, r_LcT], writes=[r_pp])
                        if kc + 3 < nkc:
                            emit_S(kc + 3)

                        def mm():
                            nc.tensor.matmul(ob[:, :], lhsT=VV[hb][:, kc, :], rhs=pp[:, :], start=(kc == 0), stop=(kc == nkc - 1))
                            return nc.tensor.matmul(rb[:, :], lhsT=C.ones_b[:, :], rhs=pp[:, :], start=(kc == 0), stop=(kc == nkc - 1))
                        S.op("pe", mm, reads=[r_V[hb][kc // 8], r_pp, C.r_const], writes=[r_ob, r_rb])
                    S.op("dve", lambda: nc.vector.reciprocal(out=RINV[:, :], in_=rb[:, :]), reads=[r_rb], writes=[r_RINV])
                    S.op("dve", lambda: nc.vector.tensor_tensor(out=ost[:, :], in0=ob[:, :], in1=RINV[:, :], op=ALU.mult),
                         reads=[r_ob, r_RINV], writes=[r_ost])
                    ci = qt // 2
                    g2r = (h * 2 + ci // 4) * PROWS + (ci % 4) * 128
                    S.dma("sp", out=G2[g2r:g2r + 128, (qt % 2) * 512:(qt % 2 + 1) * 512], in_=ost[:, :],
                          reads=[r_ost], sem_res=r_ost, is_output=True)
                    if qt % 8 == 7:
                        i = h * 2 + qt // 8
                        S.collective("AllGather", GROUPS, G2[i * PROWS:(i + 1) * PROWS, :], GO2[i * 4 * PROWS:(i + 1) * 4 * PROWS, :], r_cc[4])
                    if DEBUG_FUSED:
                        S.dma("sp", out=dbgO[h, :, q0:q0 + 512], in_=ost[:, :], reads=[r_ost], sem_res=r_ost, is_output=True)
            S.drain()

        with ExitStack() as stack:
            setup_stream_bufs(nc, S, stack, C)
            setup_act_bufs(nc, S, stack, C)
            blocks = []
            for c in range(2):
                blocks += wo_blocks(I.w_o)
                blocks += ffn_blocks(I.w_ffn_in[3], I.w_ffn_out[3])
            ws = WStream(S, C.slots, blocks, NSLOT - 2)
            fg = C.vecs[:, V_FING, :]
            for c in range(2):
                for k0 in range(0, KD, 4):
                    S.dma("sp", out=C.xT[:, k0:k0 + 4, 2:TW], in_=xmid_d[c, :, k0:k0 + 4, :],
                          writes=[C.r_x[k][t] for k in range(k0, k0 + 4) for t in (1, 2)])
                octr = [0]
                for g in range(KD // GRP):
                    gb, r_gb = C.gT[g % 2], C.r_g[g % 2]
                    for hi in range(GRP):
                        oc = IX_O + c * NH + g * GRP + hi
                        S.gather(gb[:, hi, 2:TW], GO2[:, :], IDX[:, oc:oc + 1], reads=[r_cc[4], r_IDX], writes=[r_gb[1], r_gb[2]])
                    for o in range(8):
                        slot, r_slot = ws.next()
                        emit_out_block(nc, S, C, slot, r_slot, o, gb, r_gb, TILES_M, C.GATE[:, 4, :], octr, nbanks=4)
                emit_rstd(nc, S, C, TILES_M)
                emit_modulate(nc, S, C, TILES_M, C.AMOD[:, 5, :], C.ADA[:, 15, :])
                emit_ffn(nc, S, C, ws, TILES_M, C.GATE[:, 5, :])
                emit_rstd(nc, S, C, TILES_M)
                for (t0, t1) in TILES_M:
                    ti = tile_idx(t0)
                    for k in range(KD):
                        tmp, r_tmp = next_tmp(C)
                        S.op("dve", lambda k=k, tmp=tmp, t0=t0, t1=t1: nc.vector.scalar_tensor_tensor(
                            out=tmp[:, :], in0=C.xT[:, k, t0:t1], scalar=fg[:, k:k + 1], in1=C.rstd[:, t0:t1],
                            op0=ALU.mult, op1=ALU.mult), reads=[C.r_x[k][ti], C.r_rstd[ti], C.r_vecs], writes=[r_tmp])
                        S.dma("sp", out=out[c, :, k, t0 - 2:t1 - 2], in_=tmp[:, :], reads=[r_tmp], sem_res=r_tmp, is_output=True)
            S.finish()
    return nc


def make_idx(core):
    j = core % 4
    p = np.arange(128, dtype=np.int64)
    idx = np.zeros((128, IX_N), np.int64)
    for hl in range(HPC):
        hg = HPC * j + hl
        for r in range(4):
            col = hl * 4 + r
            base = r * PROWS + (hg % 4) * 128
            idx[:, IX_Q + col] = (hg // 4) * 4 * PROWS + base + p
            idx[:, IX_K + col] = (4 + hg // 4) * 4 * PROWS + base + p
            idx[:, IX_V + col] = (8 + hg // 4) * 4 * PROWS + base + p
    for c in range(2):
        ci = chunk_of(core, c)
        for hg in range(NH):
            idx[:, IX_O + c * NH + hg] = ((hg % HPC) * 2 + ci // 4) * 4 * PROWS + (hg // HPC) * PROWS + (ci % 4) * 128 + p
    return idx.astype(np.int32)


def host_in_fused(inp, H):
    maps = host_in_A(inp, H)
    wfo = np.asarray(inp["w_ffn_out"], np.float32)
    w_ffn_in = np.stack(H["w_ffn_in4"])
    w_ffn_out = np.ascontiguousarray(np.stack([wfo[0, 0], wfo[0, 1], wfo[1, 0], wfo[1, 1]]))
    w_o = np.asarray(inp["w_o"], np.float32)[0]
    mask = np.zeros((128, 4, 512), np.float32)
    pidx = np.arange(128)[:, None]
    tidx = np.arange(512)[None, :]
    for i in range(4):
        mask[:, i, :] = np.where(128 * i + pidx <= tidx, 0.0, -30000.0)
    p = np.arange(128, dtype=np.int64)
    w_ada = np.asarray(inp["w_ada"], np.float32)
    w_ada_kv = np.asarray(inp["w_ada_kv"], np.float32)
    ada_parts = []
    for j in range(4):
        ws_, bs_ = [], []
        for s_ in range(5 * j, 5 * j + 5):
            if s_ < 18:
                l, col = s_ // 9, (s_ % 9) * D
                ws_.append(w_ada[l][:, col:col + D])
                bs_.append(H["vecs"][:, V_BADA + s_, :])
            else:
                col = (s_ - 18) * D
                ws_.append(w_ada_kv[:, col:col + D])
                bs_.append(H["vecs"][:, V_BKV + (s_ - 18), :])
        ada_parts.append((np.ascontiguousarray(np.stack(ws_)), np.ascontiguousarray(np.stack(bs_, axis=1))))
    for core in range(8):
        j = core % 4
        m = maps[core]
        del m["w_ada"], m["w_ada_kv"]
        m["w_ada_part"], m["bada_part"] = ada_parts[j]
        m["w_ffn_in"] = w_ffn_in
        m["w_ffn_out"] = w_ffn_out
        m["w_o"] = w_o
        m["mask"] = mask
        sel = np.zeros((NH, HPC + 1, 128), np.float32)
        for hl in range(HPC):
            sel[HPC * j + hl, hl, :] = -1.0
            sel[HPC * j + hl, HPC, hl] = 1.0
        m["sel"] = sel
        m["idx"] = make_idx(core)
    return maps


def kernel_unfused(**inp):
    return _kernel_unfused(**inp)


_kernel_unfused = kernel


def kernel(**inp):
    inp = {k: np.asarray(v) for k, v in inp.items()}
    H = host_common(inp)
    res = run_prog(build_fused, host_in_fused(inp, H))
    outp = np.zeros((2, SEQ, D), np.float32)
    for core in range(8):
        b = core // 4
        for c in range(2):
            t0 = chunk_of(core, c) * TOK
            o = np.asarray(res[core]["out"])[c]
            outp[b, t0:t0 + TOK, :] = o.transpose(2, 1, 0).reshape(TOK, D)
    return outp
```
